# Optimizing a Trainium2 kernel written in Bass

```python
import numpy as np
import jax
import jax.numpy as jnp
from jax import lax

D_MODEL = 1024
BATCH = 2
SEQ = 8192
DEPTH = 2

HEAD_DIM = 64
MIX_WIDTH = D_MODEL
N_HEADS_DIL = (MIX_WIDTH // 2) // HEAD_DIM
DIL_CONFIGS = ((128, 1), (512, 4), (2048, 16))
N_HEADS_MLA = (MIX_WIDTH // 2) // HEAD_DIM
MLA_NOPE = HEAD_DIM
MLA_ROPE = HEAD_DIM // 2
MLA_V = HEAD_DIM
Q_LORA = D_MODEL // 4
KV_LORA = D_MODEL // 8
D_FF = 4 * D_MODEL
ROPE_THETA = 10000.0
EPS = 1e-6
Q_BLOCK = 128
NEG_INF = -1e30
IN_SPLITS = (N_HEADS_DIL * HEAD_DIM, N_HEADS_DIL * HEAD_DIM, N_HEADS_DIL * HEAD_DIM, Q_LORA, KV_LORA, MLA_ROPE)
W_IN_COLS = sum(IN_SPLITS)

kernel_name = 'hybrid_dilated_mla_adaln_encoder'


def rms_norm(x, g):
    xf = x.astype(jnp.float32)
    y = xf * lax.rsqrt(jnp.mean(xf * xf, axis=-1, keepdims=True) + EPS)
    return (y * g.astype(jnp.float32)).astype(x.dtype)


def alibi_slopes(n):
    return jnp.asarray([2.0 ** (-8.0 * (h + 1) / n) for h in range(n)], dtype=jnp.float32)


def rope(x, pos):
    half = x.shape[-1] // 2
    inv = ROPE_THETA ** (-jnp.arange(half, dtype=jnp.float32) / half)
    ang = pos.astype(jnp.float32)[..., None] * inv
    ang = ang.reshape(ang.shape[:2] + (1,) * (x.ndim - 3) + (half,))
    cos, sin = jnp.cos(ang), jnp.sin(ang)
    xf = x.astype(jnp.float32)
    x1, x2 = xf[..., :half], xf[..., half:]
    return jnp.concatenate([x1 * cos - x2 * sin, x2 * cos + x1 * sin], axis=-1).astype(x.dtype)


def dilated_branch(q, k, v, pos, slopes, window, dil):
    B, S, H, Dh = q.shape
    half = window // (2 * dil)
    blk = half
    span = dil * blk
    Sp = -(-S // span) * span
    L = Sp // dil
    nb = L // blk
    pad = Sp - S

    def to_strided(a):
        a = jnp.pad(a, ((0, 0), (0, pad), (0, 0), (0, 0)))
        return a.reshape(B, L, dil, H, Dh).transpose(0, 2, 3, 1, 4)

    def band(a, axis, fill):
        pw = [(0, 0)] * a.ndim
        pw[axis] = (blk, blk)
        a = jnp.pad(a, pw, constant_values=fill)
        a = a.reshape(a.shape[:axis] + (nb + 2, blk) + a.shape[axis + 1:])
        parts = [lax.slice_in_dim(a, s, s + nb, axis=axis) for s in range(3)]
        return jnp.concatenate(parts, axis=axis + 1)

    qs, ks, vs = to_strided(q), to_strided(k), to_strided(v)
    pos_s = jnp.pad(pos, ((0, 0), (0, pad))).reshape(B, L, dil).transpose(0, 2, 1)
    val_s = (jnp.arange(Sp) < S).reshape(L, dil).T

    qb = qs.reshape(B, dil, H, nb, blk, Dh)
    kw = band(ks, 3, 0)
    vw = band(vs, 3, 0)
    pk = band(pos_s, 2, 0)
    vk = band(val_s, 1, False)
    pq = pos_s.reshape(B, dil, nb, blk)

    s = jnp.einsum('brhnid,brhnjd->brhnij', qb, kw, preferred_element_type=jnp.float32) * (Dh ** -0.5)
    dist = jnp.abs(pq[..., :, None] - pk[..., None, :]).astype(jnp.float32)
    s = s - slopes[None, None, :, None, None, None] * dist[:, :, None]
    rel = jnp.arange(3 * blk)[None, :] - blk - jnp.arange(blk)[:, None]
    mask = (jnp.abs(rel) <= half)[None] & vk[:, :, None, :]
    s = jnp.where(mask[None, :, None], s, NEG_INF)
    m = jnp.max(s, axis=-1, keepdims=True)
    p = jnp.exp(s - m)
    l = jnp.sum(p, axis=-1, keepdims=True)
    o = jnp.einsum('brhnij,brhnjd->brhnid', p, vw.astype(jnp.float32)) / l
    lse = (m + jnp.log(l))[..., 0]
    o = o.reshape(B, dil, H, L, Dh).transpose(0, 3, 1, 2, 4).reshape(B, Sp, H, Dh)[:, :S]
    lse = lse.reshape(B, dil, H, L).transpose(0, 3, 1, 2).reshape(B, Sp, H)[:, :S]
    return o, lse


def dilated_attention(q, k, v, pos, slopes):
    outs, lses = [], []
    for window, dil in DIL_CONFIGS:
        o, lse = dilated_branch(q, k, v, pos, slopes, window, dil)
        outs.append(o)
        lses.append(lse)
    wts = jax.nn.softmax(jnp.stack(lses), axis=0)
    return jnp.sum(jnp.stack(outs) * wts[..., None], axis=0)


def latent_attention(cq, ckv, k_rope, pos, g_cq, w_q_up, g_ckv, w_kv_up, g_qn, g_qr, g_kn, g_kr):
    B, S, _ = cq.shape
    H = N_HEADS_MLA
    q = (rms_norm(cq, g_cq) @ w_q_up).reshape(B, S, H, MLA_NOPE + MLA_ROPE)
    q_nope = rms_norm(q[..., :MLA_NOPE], g_qn)
    q_rot = rope(rms_norm(q[..., MLA_NOPE:], g_qr), pos)
    kv = (rms_norm(ckv, g_ckv) @ w_kv_up).reshape(B, S, H, MLA_NOPE + MLA_V)
    k_nope = rms_norm(kv[..., :MLA_NOPE], g_kn)
    v = kv[..., MLA_NOPE:]
    k_rot = rope(rms_norm(k_rope, g_kr), pos)
    scale = (MLA_NOPE + MLA_ROPE) ** -0.5
    nq = S // Q_BLOCK
    qn_b = q_nope.reshape(B, nq, Q_BLOCK, H, MLA_NOPE).transpose(1, 0, 2, 3, 4)
    qr_b = q_rot.reshape(B, nq, Q_BLOCK, H, MLA_ROPE).transpose(1, 0, 2, 3, 4)

    def block(args):
        qn, qr = args
        s = (jnp.einsum('bqhd,bkhd->bhqk', qn, k_nope, preferred_element_type=jnp.float32)
             + jnp.einsum('bqhr,bkr->bhqk', qr, k_rot, preferred_element_type=jnp.float32)) * scale
        p = jax.nn.softmax(s, axis=-1)
        return jnp.einsum('bhqk,bkhd->bqhd', p.astype(v.dtype), v)

    o = lax.map(block, (qn_b, qr_b))
    return o.transpose(1, 0, 2, 3, 4).reshape(B, S, H * MLA_V)


def setup_inputs(seed: int = 0) -> dict:
    key = jax.random.key(seed)
    k = jax.random.split(key, 21)
    Ld = DEPTH
    f32 = jnp.float32

    def nrm(i, shape, fan_in, mult=1.0):
        return jax.random.normal(k[i], shape, f32) * (mult * fan_in ** -0.5)

    def gain(i, shape):
        return 1.0 + 0.02 * jax.random.normal(k[i], shape, f32)

    x = jax.random.normal(k[0], (BATCH, SEQ, D_MODEL), f32)
    c = jax.random.normal(k[1], (BATCH, D_MODEL), f32)
    offs = jax.random.randint(k[2], (BATCH, 1), 0, 1024, dtype=jnp.int32)
    positions = jnp.arange(SEQ, dtype=jnp.int32)[None, :] + offs
    return {
        'x': x,
        'c': c,
        'positions': positions,
        'w_mod': nrm(3, (Ld, D_MODEL, 6 * D_MODEL), D_MODEL, 0.5),
        'b_mod': 0.01 * jax.random.normal(k[4], (Ld, 6 * D_MODEL), f32),
        'g_norm_mix': gain(5, (Ld, D_MODEL)),
        'w_in': nrm(6, (Ld, D_MODEL, W_IN_COLS), D_MODEL),
        'g_q_dil': gain(7, (Ld, HEAD_DIM)),
        'g_k_dil': gain(8, (Ld, HEAD_DIM)),
        'g_cq': gain(9, (Ld, Q_LORA)),
        'w_q_up': nrm(10, (Ld, Q_LORA, N_HEADS_MLA * (MLA_NOPE + MLA_ROPE)), Q_LORA),
        'g_ckv': gain(11, (Ld, KV_LORA)),
        'w_kv_up': nrm(12, (Ld, KV_LORA, N_HEADS_MLA * (MLA_NOPE + MLA_V)), KV_LORA),
        'g_q_nope': gain(13, (Ld, MLA_NOPE)),
        'g_q_rope': gain(14, (Ld, MLA_ROPE)),
        'g_k_nope': gain(15, (Ld, MLA_NOPE)),
        'g_k_rope': gain(16, (Ld, MLA_ROPE)),
        'w_out': nrm(17, (Ld, MIX_WIDTH, D_MODEL), MIX_WIDTH),
        'g_norm_mlp': gain(18, (Ld, D_MODEL)),
        'w_mlp_in': nrm(19, (Ld, D_MODEL, D_FF), D_MODEL),
        'w_mlp_out': nrm(20, (Ld, D_FF, D_MODEL), D_FF),
    }


def reference(x, c, positions, w_mod, b_mod, g_norm_mix, w_in, g_q_dil, g_k_dil, g_cq, w_q_up,
              g_ckv, w_kv_up, g_q_nope, g_q_rope, g_k_nope, g_k_rope, w_out, g_norm_mlp,
              w_mlp_in, w_mlp_out):
    B, S, _ = x.shape
    slopes = alibi_slopes(N_HEADS_DIL)
    cuts = [int(v) for v in np.cumsum(IN_SPLITS)[:-1]]
    c_act = jax.nn.silu(c)
    for l in range(DEPTH):
        mod = c_act @ w_mod[l] + b_mod[l]
        sh1, sc1, gt1, sh2, sc2, gt2 = jnp.split(mod, 6, axis=-1)

        h = rms_norm(x, g_norm_mix[l]) * (1.0 + sc1[:, None]) + sh1[:, None]
        proj = h @ w_in[l]
        q_d, k_d, v_d, cq, ckv, k_rope = jnp.split(proj, cuts, axis=-1)
        q_d = rms_norm(q_d.reshape(B, S, N_HEADS_DIL, HEAD_DIM), g_q_dil[l])
        k_d = rms_norm(k_d.reshape(B, S, N_HEADS_DIL, HEAD_DIM), g_k_dil[l])
        v_d = v_d.reshape(B, S, N_HEADS_DIL, HEAD_DIM)
        o_dil = dilated_attention(q_d, k_d, v_d, positions, slopes).reshape(B, S, -1).astype(x.dtype)
        o_mla = latent_attention(cq, ckv, k_rope, positions, g_cq[l], w_q_up[l], g_ckv[l], w_kv_up[l],
                                 g_q_nope[l], g_q_rope[l], g_k_nope[l], g_k_rope[l]).astype(x.dtype)
        mix = jnp.concatenate([o_dil, o_mla], axis=-1) @ w_out[l]
        x = x + gt1[:, None] * mix

        h = rms_norm(x, g_norm_mlp[l]) * (1.0 + sc2[:, None]) + sh2[:, None]
        y = jnp.square(jax.nn.relu(h @ w_mlp_in[l])) @ w_mlp_out[l]
        x = x + gt2[:, None] * y
    return x
```

```python
import numpy as np
import ml_dtypes
import concourse.bass as bass
import concourse.mybir as mybir
from concourse.bass_utils import run_bass_kernel_spmd
from contextlib import ExitStack

F32 = mybir.dt.float32
BF16 = mybir.dt.bfloat16
I32 = mybir.dt.int32
AF = mybir.ActivationFunctionType
ALU = mybir.AluOpType

NSLOT = 8
SAME_WIN = 3
T = 2048
SEQ = 8192
NCB = 832
NCF = 67
ARENA = 54272
PI = float(np.pi)


class _Op:
    __slots__ = ("eng", "emit", "stream", "idx", "waits", "ms", "clock", "pos")


class Sched:
    def __init__(self, nc):
        self.nc = nc
        self.eng_ops = {e: [] for e in ("pe", "act", "dve", "pool", "sp")}
        self.stream_ops = {}
        self.know = {e: {} for e in self.eng_ops}
        self.last_w = {}
        self.readers = {}
        self.dma_cnt = {"sp": 0, "pool": 0, "act": 0}

    def _add(self, eng, emit, R, W, dma=False):
        op = _Op()
        op.eng = eng
        op.emit = emit
        op.ms = False
        if dma:
            slot = self.dma_cnt[eng] % NSLOT
            self.dma_cnt[eng] += 1
            op.stream = ("dma", eng, slot)
        else:
            op.stream = eng
        sl = self.stream_ops.setdefault(op.stream, [])
        op.idx = len(sl) + 1
        deps = []
        if dma and sl:
            deps.append(sl[-1])
        sl.append(op)
        for t in R:
            w = self.last_w.get(t)
            if w is not None:
                deps.append(w)
        for t in W:
            w = self.last_w.get(t)
            if w is not None:
                deps.append(w)
            deps.extend(self.readers.get(t, ()))
        eo = self.eng_ops[eng]
        op.pos = len(eo)
        K = self.know[eng]
        best = {}
        for d in deps:
            if d is op:
                continue
            b = best.get(d.stream)
            if b is None or b.idx < d.idx:
                best[d.stream] = d
        waits = []
        for s, d in best.items():
            if K.get(s, 0) >= d.idx:
                continue
            if s == eng:
                if eng == "pe" or eng == "sp" or (op.pos - d.pos) > SAME_WIN:
                    continue
                d.ms = True
                waits.append(d)
                K[s] = d.idx
                continue
            d.ms = True
            waits.append(d)
            for s2, i2 in d.clock.items():
                if K.get(s2, 0) < i2:
                    K[s2] = i2
        op.waits = waits
        op.clock = dict(K)
        op.clock[op.stream] = op.idx
        if dma:
            op.ms = True
        eo.append(op)
        for t in R:
            self.readers.setdefault(t, []).append(op)
        for t in W:
            self.last_w[t] = op
            self.readers[t] = []
        return op

    def mm(self, out, lhsT, rhs, start=True, stop=True, R=(), W=(), **kw):
        return self._add("pe", lambda e: e.matmul(out, lhsT, rhs, start=start, stop=stop, **kw), R, W)

    def transpose(self, out, in_, ident, R=(), W=()):
        return self._add("pe", lambda e: e.transpose(out, in_, ident), R, W)

    def act(self, out, in_, func, R=(), W=(), **kw):
        return self._add("act", lambda e: e.activation(out=out, in_=in_, func=func, **kw), R, W)

    def tt(self, out, in0, in1, op, R=(), W=(), eng="dve"):
        return self._add(eng, lambda e: e.tensor_tensor(out, in0, in1, op), R, W)

    def ts(self, out, in0, s1, s2, op0, op1=None, R=(), W=(), eng="dve"):
        if op1 is None:
            return self._add(eng, lambda e: e.tensor_scalar(out, in0, s1, None, op0), R, W)
        return self._add(eng, lambda e: e.tensor_scalar(out, in0, s1, s2, op0, op1), R, W)

    def stt(self, out, in0, scalar, in1, op0, op1, R=(), W=(), eng="dve"):
        return self._add(eng, lambda e: e.scalar_tensor_tensor(out, in0, scalar, in1, op0, op1), R, W)

    def copy(self, out, in_, R=(), W=(), eng="dve"):
        if eng == "act":
            return self._add("act", lambda e: e.copy(out, in_), R, W)
        return self._add(eng, lambda e: e.tensor_copy(out, in_), R, W)

    def recip(self, out, in_, R=(), W=()):
        return self._add("dve", lambda e: e.reciprocal(out, in_), R, W)

    def memset(self, ap, val, W=(), eng="pool"):
        return self._add(eng, lambda e: e.memset(ap, val), (), W)

    def dma(self, q, out, in_, R=(), W=()):
        return self._add(q, lambda e: e.dma_start(out=out, in_=in_), R, W, dma=True)

    def barrier(self):
        lasts = [ops[-1] for ops in self.stream_ops.values() if ops]
        for e in ("pe", "act", "dve", "pool", "sp"):
            op = _Op()
            op.eng = e
            op.emit = None
            op.ms = False
            op.stream = None
            op.idx = 0
            eo = self.eng_ops[e]
            op.pos = len(eo)
            K = self.know[e]
            waits = []
            for d in lasts:
                if d.stream == e:
                    if e != "pe" and K.get(e, 0) < d.idx:
                        d.ms = True
                        waits.append(d)
                        K[e] = d.idx
                    continue
                if K.get(d.stream, 0) >= d.idx:
                    continue
                d.ms = True
                waits.append(d)
                for s2, i2 in d.clock.items():
                    if K.get(s2, 0) < i2:
                        K[s2] = i2
            op.waits = waits
            op.clock = dict(K)
            eo.append(op)

    def finish(self, R):
        return self._add("sp", None, R, ())

    def emit(self, stack):
        nc = self.nc
        sems = {}
        for s in self.stream_ops:
            nm = "s_" + ("_".join(str(x) for x in s) if isinstance(s, tuple) else s)
            sems[s] = stack.enter_context(nc.semaphore(nm))
        val = {}
        for s, ops in self.stream_ops.items():
            c = 0
            inc = 16 if isinstance(s, tuple) else 1
            for o in ops:
                if o.ms:
                    c += inc
                val[id(o)] = c
        block = stack.enter_context(nc.Block())

        def run(eng_name, e):
            for o in self.eng_ops[eng_name]:
                for d in o.waits:
                    e.wait_ge(sems[d.stream], val[id(d)])
                if o.emit is None:
                    continue
                ins = o.emit(e)
                if o.ms:
                    ins.then_inc(sems[o.stream], 16 if isinstance(o.stream, tuple) else 1)

        @block.tensor
        def _(e):
            run("pe", e)

        @block.scalar
        def _(e):
            run("act", e)

        @block.vector
        def _(e):
            run("dve", e)

        @block.gpsimd
        def _(e):
            run("pool", e)

        @block.sync
        def _(e):
            run("sp", e)


def dil_tiles():
    tiles = []
    for cfg, d in enumerate((1, 4, 16)):
        Q0 = 1024 // d
        nq = 2048 // d
        nt = nq // 128 + 1
        for r in range(d):
            for i in range(nt):
                a = Q0 - 64 + 128 * i
                c0 = 128 if i == 0 else 0
                c1 = 128 if i == nt - 1 else 256
                tiles.append((cfg, d, r, a, c0, c1, Q0))
    return tiles


TILES = dil_tiles()
NTILE = len(TILES)


def build(mode):
    nc = bass.Bass("TRN2", target_bir_lowering=False)
    D = lambda n, sh, dt, kind="ExternalInput": nc.dram_tensor(n, sh, dt, kind=kind).ap()
    xT_d = D("xT", [1024, T], F32)
    cT_d = D("cT", [128, 8], F32)
    pos_d = D("pos", [96, T], I32)
    wmod_d = D("w_mod", [1024, 6144], F32)
    bmodT_d = D("bmodT", [128, 48], F32)
    win_d = D("w_in", [1024, 1952], F32)
    winsw_d = D("w_in_sw", [1024, 32], F32)
    vecs_d = D("vecs", [128, 32], F32)
    cstb_d = D("cstb", [128, NCB], F32)
    cstf_d = D("cstf", [128, NCF], F32)
    if mode == "A":
        okd_d = D("o_kd", [512, T], BF16, "ExternalOutput")
        ovd_d = D("o_vd", [512, T], BF16, "ExternalOutput")
        olat_d = D("o_lat", [160, T], BF16, "ExternalOutput")
    else:
        wq_d = D("w_q", [256, 768], F32)
        wqsw_d = D("w_q_sw", [256, 768], F32)
        wkv_d = D("w_kv", [128, 1024], F32)
        wout_d = D("w_out", [1024, 1024], F32)
        w1_d = D("w1", [1024, 4096], F32)
        w2_d = D("w2", [4096, 1024], F32)
        ikd_d = D("i_kd", [512, 4096], BF16)
        ivd_d = D("i_vd", [512, 4096], BF16)
        ilat_d = D("i_lat", [160, SEQ], BF16)
        mtab_d = D("mtab", [8, 128, 768], F32)
        valid_d = D("valid", [128, NTILE], F32)
        yT_d = D("yT", [1024, T], F32, "ExternalOutput")

    st = ExitStack()
    sb = lambda n, sh, dt: st.enter_context(nc.sbuf_tensor(n, sh, dt))
    xT = sb("xTs", [128, 8, T], F32)
    cosT = sb("cosT", [96, T], F32)
    sinT = sb("sinT", [96, T], F32)
    tmpf = sb("tmpf", [128, 4, 512], F32)
    tmpi = sb("tmpi", [96, 512], I32)
    posi = sb("posi", [96, T], I32) if False else None
    cstb = sb("cstbs", [128, NCB], BF16)
    cstf = sb("cstfs", [128, NCF], F32)
    vecs = sb("vecss", [128, 32], F32)
    cTs = sb("cTs", [128, 8], F32)
    cact = sb("cact", [128, 8], BF16)
    bmodT = sb("bmodTs", [128, 48], F32)
    modT = sb("modT", [128, 48], F32)
    gm = sb("gm", [128, 16], F32)
    modrow = sb("modrow", [1, 2, 512], F32)
    arena = sb("arena", [128, ARENA], BF16)
    P = [st.enter_context(nc.psum_tensor("ps%d" % i, [128, 512], F32)) for i in range(7)]
    psb = st.enter_context(nc.psum_tensor("psb", [128, 1024], BF16))
    if mode == "L":
        mtb = sb("mtb", [128, 768], F32)
        vld = sb("vld", [128, NTILE], F32)

    S = Sched(nc)

    def ar(off, n, shape=None):
        a = arena[:, off:off + n]
        return a

    ones1024 = cstb[:, 0:128]
    blk64 = cstb[:, 128:256]
    ones256 = cstb[:, 256:384]
    ones128 = cstb[:, 384:512]
    blkq96 = cstb[0:96, 512:608]
    ones32 = cstb[0:32, 608:640]
    shift64 = cstb[0:64, 704:832]
    eps_c = lambda lo, hi: cstf[lo:hi, 2:3]

    def tsl(tb):
        return slice(tb * 512, (tb + 1) * 512)

    S.dma("pool", cstb[:], cstb_d[:, :], W=["cstb"])
    S.dma("sp", cstf[:], cstf_d[:, :], W=["cstf"])
    S.dma("sp", vecs[:], vecs_d[:, :], W=["vecs"])
    S.dma("sp", cTs[:], cT_d[:, :], W=["cT"])
    S.dma("sp", bmodT[:], bmodT_d[:, :], W=["bmodT"])
    xv = xT_d.rearrange("(c p) t -> p c t", p=128)
    for tb in range(4):
        S.dma("sp", xT[:, :, tsl(tb)], xv[:, :, tsl(tb)], W=[("x", c, tb) for c in range(8)])
    if mode == "L":
        S.dma("sp", vld[:], valid_d[:, :], W=["vld"])

    S.act(cact[:], cTs[:], AF.Silu, R=["cT"], W=["cact"])
    wmv = wmod_d.rearrange("(k p) n -> p k n", p=128)
    ngrp = 4 if mode == "A" else 12
    for g in range(ngrp):
        wm = arena[:, 45056 + (g % 2) * 4096:45056 + (g % 2) * 4096 + 4096].rearrange("p (k n) -> p k n", n=512)
        S.dma("pool", wm, wmv[:, :, g * 512:(g + 1) * 512], W=[("wm", g % 2)])
        for k in range(8):
            S.mm(P[0][0:1, :], cact[:, k:k + 1], wm[:, k, :], start=(k == 0), stop=(k == 7),
                 R=["cact", ("wm", g % 2)], W=["P0"])
        S.copy(modrow[0:1, g % 2, :], P[0][0:1, :], R=["P0"], W=[("mr", g % 2)])
        for jj in range(4):
            j = g * 4 + jj
            S.mm(P[1][:, j:j + 1], modrow[0:1, g % 2, jj * 128:(jj + 1) * 128], cstf[0:1, 3:4],
                 start=True, stop=True, R=[("mr", g % 2), "cstf"], W=["P1"], skip_group_check=True)
    S.tt(modT[:, 0:ngrp * 4], P[1][:, 0:ngrp * 4], bmodT[:, 0:ngrp * 4], ALU.add, R=["P1", "bmodT"], W=["modT"])
    S.stt(gm[:, 0:8], modT[:, 8:16], 1.0, vecs[:, 0:8], ALU.add, ALU.mult, R=["modT", "vecs"], W=["gm"])
    if mode == "L":
        S.stt(gm[:, 8:16], modT[:, 32:40], 1.0, vecs[:, 8:16], ALU.add, ALU.mult, R=["modT", "vecs"], W=["gm"])

    for tb in range(4):
        S.dma("sp", tmpi[:, :], pos_d[:, tsl(tb)], W=["tmpi"])
        t0 = tmpf[0:96, 0, :]
        t1 = tmpf[0:96, 1, :]
        t2 = tmpf[0:96, 2, :]
        t3 = tmpf[0:96, 3, :]
        S.copy(t0, tmpi[:, :], R=["tmpi"], W=["t0"])
        S.ts(t0, t0, cstf[0:96, 0:1], None, ALU.mult, R=["t0", "cstf"], W=["t0"])
        for tab, shift, use_sign in ((sinT, 0.0, True), (cosT, PI / 2, False)):
            S.ts(t1, t0, shift, None, ALU.add, R=["t0"], W=["t1"])
            S.ts(t2, t1, 1.0 / (2 * PI), None, ALU.mult, R=["t1"], W=["t2"])
            S.copy(tmpi[:, :], t2, R=["t2"], W=["tmpi"])
            S.copy(t2, tmpi[:, :], R=["tmpi"], W=["t2"])
            S.stt(t3, t2, -2 * PI, t1, ALU.mult, ALU.add, R=["t2", "t1"], W=["t3"])
            S.ts(t2, t3, PI, -2 * PI, ALU.is_gt, ALU.mult, R=["t3"], W=["t2"])
            S.tt(t3, t3, t2, ALU.add, R=["t3", "t2"], W=["t3"])
            S.ts(t2, t3, -PI, 2 * PI, ALU.is_lt, ALU.mult, R=["t3"], W=["t2"])
            S.tt(t3, t3, t2, ALU.add, R=["t3", "t2"], W=["t3"])
            S.ts(t3, t3, PI, -PI, ALU.min, ALU.max, R=["t3"], W=["t3"])
            if use_sign:
                S.act(tab[:, tsl(tb)], t3, AF.Sin, R=["t3", "cstf"], W=[("rope", tb)], scale=cstf[0:96, 1:2])
            else:
                S.act(tab[:, tsl(tb)], t3, AF.Sin, R=["t3"], W=[("rope", tb)])

    QD, CQ, OT = 0, 8192, 12288
    qdT = arena[:, QD:QD + 8192].rearrange("p (c t) -> p c t", t=T)
    cqnT = arena[:, CQ:CQ + 4096].rearrange("p (c t) -> p c t", t=T)
    oT = arena[:, OT:OT + 16384].rearrange("p (c t) -> p c t", t=T)

    def rstd_from(Pn_ap, out_ap, lo, hi, R, W):
        S.act(out_ap, Pn_ap, AF.Ln, R=R + ["cstf"], W=W, bias=eps_c(lo, hi), scale=1.0)
        S.act(out_ap, out_ap, AF.Exp, R=W, W=W, scale=-0.5)

    def rmsnorm_block(tb, gcol, shcol, hout, sq, sqtok):
        S.act(sq, xT[:, :, tsl(tb)], AF.Square, R=[("x", c, tb) for c in range(8)], W=[sqtok])
        for c in range(8):
            S.mm(P[6][:, :], ones1024, sq[:, c, :], start=(c == 0), stop=(c == 7), R=[sqtok, "cstb"], W=["P6"])
        rstd_from(P[6][:, :], tmpf[:, 0, :], 0, 128, ["P6"], ["t0"])
        for c in range(8):
            sl = 1 + (c % 3)
            S.stt(tmpf[:, sl, :], xT[:, c, tsl(tb)], gm[:, gcol + c:gcol + c + 1], tmpf[:, 0, :], ALU.mult, ALU.mult,
                  R=[("x", c, tb), "t0", "gm"], W=["t%d" % sl])
            S.act(hout(c), tmpf[:, sl, :], AF.Identity, R=["t%d" % sl, "modT"], W=[("h", c, tb)],
                  bias=modT[:, shcol + c:shcol + c + 1], scale=1.0)

    WIN = 12288
    winT = arena[:, WIN:WIN + 15616].rearrange("p (k n) -> p k n", n=1952)
    HB = WIN + 15616
    SQ = HB + 8192
    STG = SQ + 4096
    wsw = arena[:, STG + 4096:STG + 4096 + 256].rearrange("p (k n) -> p k n", n=32)
    wiv = win_d.rearrange("(k p) n -> p k n", p=128)
    S.dma("pool", winT[:, :, 0:976], wiv[:, :, 0:976], W=["win"])
    S.dma("pool", winT[:, :, 976:1952], wiv[:, :, 976:1952], W=["win"])
    S.dma("pool", wsw, winsw_d.rearrange("(k p) n -> p k n", p=128), W=["wsw"])
    pbank = [0]

    def nextP():
        pbank[0] = (pbank[0] + 1) % 4
        return pbank[0], P[pbank[0]]

    stg_i = [0]

    def stage():
        stg_i[0] = (stg_i[0] + 1) % 4
        i = stg_i[0]
        return ("stg", i), arena[:, STG + i * 512:STG + (i + 1) * 512]

    def proj(tb, hbuf, col0, M, wt=None, wtok="win"):
        bi, Pp = nextP()
        for k in range(8):
            lhsT = winT[:, k, col0:col0 + M] if wt is None else wt[:, k, 0:M]
            S.mm(Pp[0:M, :], lhsT, hbuf[:, k, :], start=(k == 0), stop=(k == 7),
                 R=[wtok] + [("h", k, tb)], W=["P%d" % bi])
        return "P%d" % bi, Pp

    for tb in range(4):
        hb_off = HB + (tb % 2) * 4096
        hbuf = arena[:, hb_off:hb_off + 4096].rearrange("p (k t) -> p k t", t=512)
        sq = arena[:, SQ:SQ + 4096].rearrange("p (k t) -> p k t", t=512)
        rmsnorm_block(tb, 0, 0, lambda c: hbuf[:, c, :], sq, "sq")
        sqh = arena[:, SQ:SQ + 512]

        def headnorm(ptok, Pp, onesm, gcol, out_ap, Wt, rows=128):
            S.act(sqh[0:rows, :], Pp[0:rows, :], AF.Square, R=[ptok], W=["sqh"])
            S.mm(P[5][0:rows, :], onesm, sqh[0:rows, :], R=["sqh", "cstb"], W=["P5"])
            rstd_from(P[5][0:rows, :], tmpf[0:rows, 0, :], 0, rows, ["P5"], ["t0"])
            S.stt(out_ap, Pp[0:rows, :], vecs[0:rows, gcol:gcol + 1], tmpf[0:rows, 0, :], ALU.mult, ALU.mult,
                  R=[ptok, "t0", "vecs"], W=Wt)

        if mode == "L":
            for j in range(4):
                ptok, Pp = proj(tb, hbuf, j * 128, 128)
                headnorm(ptok, Pp, blk64, 16, qdT[:, j, tsl(tb)], [("qd", j, tb)])
            pa = proj(tb, hbuf, 1536, 128)
            pb2 = proj(tb, hbuf, 1664, 128)
            for i, (ptok, Pp) in enumerate((pa, pb2)):
                S.act(sq[:, i, :], Pp[:, :], AF.Square, R=[ptok], W=["sq"])
            for i in range(2):
                S.mm(P[5][:, :], ones256, sq[:, i, :], start=(i == 0), stop=(i == 1), R=["sq", "cstb"], W=["P5"])
            rstd_from(P[5][:, :], tmpf[:, 0, :], 0, 128, ["P5"], ["t0"])
            for i, (ptok, Pp) in enumerate((pa, pb2)):
                S.stt(cqnT[:, i, tsl(tb)], Pp[:, :], vecs[:, 18 + i:19 + i], tmpf[:, 0, :], ALU.mult, ALU.mult,
                      R=[ptok, "t0", "vecs"], W=[("cq", i, tb)])
        else:
            for j in range(4):
                ptok, Pp = proj(tb, hbuf, 512 + j * 128, 128)
                stok, stg = stage()
                headnorm(ptok, Pp, blk64, 17, stg, [stok])
                S.dma("sp", okd_d[j * 128:(j + 1) * 128, tsl(tb)], stg, R=[stok], W=[("okd", j, tb)])
            for j in range(4):
                ptok, Pp = proj(tb, hbuf, 1024 + j * 128, 128)
                stok, stg = stage()
                S.copy(stg, Pp[:, :], R=[ptok], W=[stok])
                S.dma("sp", ovd_d[j * 128:(j + 1) * 128, tsl(tb)], stg, R=[stok], W=[("ovd", j, tb)])
            ptok, Pp = proj(tb, hbuf, 1792, 128)
            stok, stg = stage()
            headnorm(ptok, Pp, ones128, 20, stg, [stok])
            S.dma("sp", olat_d[0:128, tsl(tb)], stg, R=[stok], W=[("olat", 0, tb)])
            ptok, Pp = proj(tb, hbuf, 1920, 32)
            ptok2, Pp2 = proj(tb, hbuf, 0, 32, wt=wsw, wtok="wsw")
            S.act(sqh[0:32, :], Pp[0:32, :], AF.Square, R=[ptok], W=["sqh"])
            S.mm(P[5][0:32, :], ones32, sqh[0:32, :], R=["sqh", "cstb"], W=["P5"])
            rstd_from(P[5][0:32, :], tmpf[0:32, 0, :], 0, 32, ["P5"], ["t0"])
            S.stt(tmpf[0:32, 1, :], Pp[0:32, :], vecs[0:32, 24:25], tmpf[0:32, 0, :], ALU.mult, ALU.mult,
                  R=[ptok, "t0", "vecs"], W=["t1"])
            S.stt(tmpf[0:32, 2, :], Pp2[0:32, :], vecs[0:32, 25:26], tmpf[0:32, 0, :], ALU.mult, ALU.mult,
                  R=[ptok2, "t0", "vecs"], W=["t2"])
            S.tt(tmpf[0:32, 1, :], tmpf[0:32, 1, :], cosT[0:32, tsl(tb)], ALU.mult, R=["t1", ("rope", tb)], W=["t1"])
            S.tt(tmpf[0:32, 2, :], tmpf[0:32, 2, :], sinT[0:32, tsl(tb)], ALU.mult, R=["t2", ("rope", tb)], W=["t2"])
            stok, stg = stage()
            S.tt(stg[0:32, :], tmpf[0:32, 1, :], tmpf[0:32, 2, :], ALU.add, R=["t1", "t2"], W=[stok])
            S.dma("sp", olat_d[128:160, tsl(tb)], stg[0:32, :], R=[stok], W=[("olat", 1, tb)])

    if mode == "A":
        outs = [("okd", j, tb) for j in range(4) for tb in range(4)] + [("ovd", j, tb) for j in range(4) for tb in range(4)] \
            + [("olat", i, tb) for i in range(2) for tb in range(4)]
        S.finish(outs)
        S.emit(st)
        st.close()
        return nc

    S.barrier()

    def norm_out(Pacc, ptok, ch, hh, tb):
        S.recip(tmpf[64:65, 2, :], Pacc[64:65, :], R=[ptok], W=["t2"])
        S.mm(P[6][0:64, :], cstf[64:65, 3:67], tmpf[64:65, 2, :], R=["t2", "cstf"], W=["P6"])
        S.copy(tmpf[0:64, 3, :], P[6][0:64, :], R=["P6"], W=["t3"], eng="act")
        if hh == 0:
            S.tt(oT[0:64, ch, tsl(tb)], Pacc[0:64, :], tmpf[0:64, 3, :], ALU.mult, R=[ptok, "t3"], W=[("o", ch, tb, 0)])
        else:
            ost = arena[0:64, OST:OST + 512]
            S.tt(ost, Pacc[0:64, :], tmpf[0:64, 3, :], ALU.mult, R=[ptok, "t3"], W=["ost"])
            S.mm(P[6][:, :], shift64, ost, R=["ost", "cstb"], W=["P6"])
            S.copy(oT[64:128, ch, tsl(tb)], P[6][64:128, :], R=["P6"], W=[("o", ch, tb, 1)])

    B0 = 28672
    kdT = arena[:, B0:B0 + 4096]
    vdT = arena[:, B0 + 4096:B0 + 8192]
    VT0 = B0 + 8192
    Vt = arena[:, VT0:VT0 + NTILE * 65].rearrange("p (i c) -> p i c", c=65)
    PT0 = VT0 + NTILE * 65 + 3
    OST = PT0 + 4 * 256
    S.memset(Vt[:, :, 64:65], 1.0, W=["vt1"])
    ident = cstb[:, 640:704]
    for h in range(8):
        pp, hh = h // 2, h % 2
        pb = hh * 64
        if hh == 0:
            S.dma("sp", kdT, ikd_d[pp * 128:(pp + 1) * 128, :], W=["kdT"])
            S.dma("sp", vdT, ivd_d[pp * 128:(pp + 1) * 128, :], W=["vdT"])
        S.dma("sp", mtb[:], mtab_d[h, :, :], W=["mtb"])
        for i0 in range(0, NTILE, 16):
            n = min(16, NTILE - i0)
            for s in range(n):
                cfg, d, r, a, c0, c1, Q0 = TILES[i0 + s]
                S.transpose(psb[:, s * 64:(s + 1) * 64], vdT[pb:pb + 64, bass.ds(a * d + r, 128, d)],
                            ident[pb:pb + 64, :], R=["vdT", "cstb"], W=["psb"])
            S.copy(Vt[:, i0:i0 + n, 0:64], psb[:, 0:n * 64].rearrange("p (a b) -> p a b", b=64),
                   R=["psb"], W=[("vt", i0)])
        first = [True] * 4
        last_idx = {}
        segs_all = []
        for idx, (cfg, d, r, a, c0, c1, Q0) in enumerate(TILES):
            n0 = (a - 64 + c0 - Q0) * d + r
            segs = []
            c = c0
            while c < c1:
                n = n0 + (c - c0) * d
                tb = n // 512
                cend = c
                while cend < c1 and (n0 + (cend - c0) * d) // 512 == tb:
                    cend += 1
                segs.append((tb, c, cend, n - 512 * tb))
                last_idx[tb] = (idx, len(segs) - 1)
                c = cend
            segs_all.append(segs)
        for idx, (cfg, d, r, a, c0, c1, Q0) in enumerate(TILES):
            ncol = c1 - c0
            n0 = (a - 64 + c0 - Q0) * d + r
            sb_i = idx % 2
            Ps = P[sb_i]
            S.mm(Ps[:, 0:ncol], kdT[pb:pb + 64, bass.ds(a * d + r, 128, d)],
                 qdT[pb:pb + 64, pp, bass.ds(n0, ncol, d)], R=["kdT"] + [("qd", pp, t) for t in range(4)], W=["P%d" % sb_i])
            tf = tmpf[:, sb_i, 0:ncol]
            S.act(tf, Ps[:, 0:ncol], AF.Exp, R=["P%d" % sb_i], W=["t%d" % sb_i], scale=0.125)
            pt = arena[:, PT0 + (idx % 4) * 256:PT0 + (idx % 4) * 256 + ncol]
            S.stt(pt, tf, vld[:, idx:idx + 1], mtb[:, cfg * 256 + c0:cfg * 256 + c1], ALU.mult, ALU.mult,
                  R=["t%d" % sb_i, "vld", "mtb"], W=[("pt", idx % 4)])
            for si, (tb, ca, cb, nloc) in enumerate(segs_all[idx]):
                S.mm(P[2 + tb][0:65, bass.ds(nloc, cb - ca, d)], Vt[:, idx, 0:65], pt[:, ca - c0:cb - c0],
                     start=first[tb], stop=(last_idx[tb] == (idx, si)), R=[("pt", idx % 4), ("vt", (idx // 16) * 16), "vt1"],
                     W=["P%d" % (2 + tb)], skip_group_check=True)
                first[tb] = False
        for tb in range(4):
            norm_out(P[2 + tb], "P%d" % (2 + tb), pp, hh, tb)

    S.barrier()

    QH = 0
    Qh = arena[:, QH:QH + 2048]
    WQ = 2048
    wq = arena[:, WQ:WQ + 1536].rearrange("p (k n) -> p k n", n=768)
    wqsw = arena[:, WQ + 1536:WQ + 3072].rearrange("p (k n) -> p k n", n=768)
    wkv = arena[:, WQ + 3072:WQ + 4096]
    PTC = WQ + 4096
    CK = 28672
    ckv = arena[:, CK:CK + 8192]
    Kh = arena[:, CK + 8192:CK + 16384]
    Vh = arena[:, CK + 16384:CK + 16384 + 64 * 65].rearrange("p (i c) -> p i c", c=65)
    SQC = CK + 16384 + 64 * 65
    OST = SQC + 512
    for q4 in range(4):
        S.dma("sp", ckv[:, q4 * 2048:(q4 + 1) * 2048], ilat_d[0:128, q4 * 2048:(q4 + 1) * 2048], W=[("ckv", q4)])
    S.dma("sp", Kh[64:96, :], ilat_d[128:160, :], W=["Khr"])
    S.dma("pool", wq, wq_d.rearrange("(k p) n -> p k n", p=128), W=["wq"])
    S.dma("pool", wqsw, wqsw_d.rearrange("(k p) n -> p k n", p=128), W=["wqsw"])
    S.dma("pool", wkv, wkv_d[:, :], W=["wkv"])
    S.memset(Vh[:, :, 64:65], 1.0, W=["vh1"])
    sqc = arena[:, SQC:SQC + 512]
    scale_mla = float(96.0 ** -0.5)
    for h in range(8):
        pp, hh = h // 2, h % 2
        for kb in range(16):
            ks = slice(kb * 512, (kb + 1) * 512)
            S.mm(P[4][0:64, :], wkv[:, h * 128:h * 128 + 64], ckv[:, ks], R=["wkv", ("ckv", kb // 4)], W=["P4"])
            S.copy(tmpf[0:64, 0, :], P[4][0:64, :], R=["P4"], W=["t0"])
            S.tt(sqc[0:64, :], tmpf[0:64, 0, :], tmpf[0:64, 0, :], ALU.mult, R=["t0"], W=["sqc"], eng="pool")
            S.mm(P[5][0:64, :], blk64[0:64, 0:64], sqc[0:64, :], R=["sqc", "cstb"], W=["P5"])
            rstd_from(P[5][0:64, :], tmpf[0:64, 1, :], 0, 64, ["P5"], ["t1"])
            S.stt(Kh[0:64, ks], tmpf[0:64, 0, :], vecs[0:64, 23:24], tmpf[0:64, 1, :], ALU.mult, ALU.mult,
                  R=["t0", "t1", "vecs"], W=[("Kh", kb)])
        for g in range(8):
            for s in range(8):
                kb2 = g * 8 + s
                S.mm(P[4][:, s * 64:(s + 1) * 64], ckv[:, kb2 * 128:(kb2 + 1) * 128], wkv[:, h * 128 + 64:h * 128 + 128],
                     R=["wkv", ("ckv", kb2 // 16)], W=["P4"])
            S.copy(Vh[:, g * 8:(g + 1) * 8, 0:64], P[4][:, :].rearrange("p (a b) -> p a b", b=64), R=["P4"], W=[("Vh", g)])
        for tb in range(4):
            for kk in range(2):
                S.mm(P[4][0:96, :], wq[:, kk, h * 96:(h + 1) * 96], cqnT[:, kk, tsl(tb)], start=(kk == 0), stop=(kk == 1),
                     R=["wq", ("cq", kk, tb)], W=["P4"])
            for kk in range(2):
                S.mm(P[5][0:96, :], wqsw[:, kk, h * 96:(h + 1) * 96], cqnT[:, kk, tsl(tb)], start=(kk == 0), stop=(kk == 1),
                     R=["wqsw", ("cq", kk, tb)], W=["P5"])
            S.act(sqc[0:96, :], P[4][0:96, :], AF.Square, R=["P4"], W=["sqc"])
            S.mm(P[6][0:96, :], blkq96, sqc[0:96, :], R=["sqc", "cstb"], W=["P6"])
            rstd_from(P[6][0:96, :], tmpf[0:96, 0, :], 0, 96, ["P6"], ["t0"])
            S.stt(Qh[0:64, tsl(tb)], P[4][0:64, :], vecs[0:64, 21:22], tmpf[0:64, 0, :], ALU.mult, ALU.mult,
                  R=["P4", "t0", "vecs"], W=[("Qh", tb)])
            S.stt(tmpf[64:96, 1, :], P[4][64:96, :], vecs[64:96, 21:22], tmpf[64:96, 0, :], ALU.mult, ALU.mult,
                  R=["P4", "t0", "vecs"], W=["t1"])
            S.stt(tmpf[64:96, 2, :], P[5][64:96, :], vecs[64:96, 22:23], tmpf[64:96, 0, :], ALU.mult, ALU.mult,
                  R=["P5", "t0", "vecs"], W=["t2"])
            S.tt(tmpf[64:96, 1, :], tmpf[64:96, 1, :], cosT[64:96, tsl(tb)], ALU.mult, R=["t1", ("rope", tb)], W=["t1"])
            S.tt(tmpf[64:96, 2, :], tmpf[64:96, 2, :], sinT[64:96, tsl(tb)], ALU.mult, R=["t2", ("rope", tb)], W=["t2"])
            S.tt(Qh[64:96, tsl(tb)], tmpf[64:96, 1, :], tmpf[64:96, 2, :], ALU.add, R=["t1", "t2"], W=[("Qhr", tb)])
        for qb in range(4):
            Po = P[2 + (qb % 2)]
            potok = "P%d" % (2 + (qb % 2))
            for kb2 in range(64):
                si = kb2 % 2
                S.mm(P[si][:, :], Kh[0:96, kb2 * 128:(kb2 + 1) * 128], Qh[0:96, tsl(qb)],
                     R=[("Kh", kb2 // 4), "Khr", ("Qh", qb), ("Qhr", qb)], W=["P%d" % si])
                pt = arena[:, PTC + (kb2 % 4) * 512:PTC + (kb2 % 4) * 512 + 512]
                S.act(pt, P[si][:, :], AF.Exp, R=["P%d" % si], W=[("ptc", kb2 % 4)], scale=scale_mla)
                S.mm(Po[0:65, :], Vh[:, kb2, 0:65], pt, start=(kb2 == 0), stop=(kb2 == 63),
                     R=[("ptc", kb2 % 4), ("Vh", kb2 // 8), "vh1"], W=[potok])
            norm_out(Po, potok, 4 + pp, hh, qb)

    S.barrier()

    WO = 28672
    wo = arena[:, WO:WO + 8192].rearrange("p (k n) -> p k n", n=1024)
    wov = wout_d.rearrange("(k p) n -> p k n", p=128)
    S.dma("pool", wo[:, :, 0:512], wov[:, :, 0:512], W=["wo"])
    S.dma("pool", wo[:, :, 512:1024], wov[:, :, 512:1024], W=["wo"])
    for tb in range(4):
        for dc in range(8):
            bi = dc % 4
            for ch in range(8):
                S.mm(P[bi][:, :], wo[:, ch, dc * 128:(dc + 1) * 128], oT[:, ch, tsl(tb)], start=(ch == 0), stop=(ch == 7),
                     R=["wo", ("o", ch, tb, 0), ("o", ch, tb, 1)], W=["P%d" % bi])
            S.stt(xT[:, dc, tsl(tb)], P[bi][:, :], modT[:, 16 + dc:17 + dc], xT[:, dc, tsl(tb)], ALU.mult, ALU.add,
                  R=["P%d" % bi, "modT", ("x", dc, tb)], W=[("x", dc, tb)])

    S.barrier()

    hT = arena[:, 0:16384].rearrange("p (k t) -> p k t", t=T)
    sq = arena[:, 40960:40960 + 4096].rearrange("p (k t) -> p k t", t=512)
    w1v = w1_d.rearrange("(k p) n -> p k n", p=128)
    w2v = w2_d.rearrange("(c p) n -> p c n", p=128)

    def wgrp(g):
        o1 = 16384 + (g % 2) * 8192
        W1g = arena[:, o1:o1 + 4096].rearrange("p (k n) -> p k n", n=512)
        W2g = arena[:, o1 + 4096:o1 + 8192].rearrange("p (c n) -> p c n", n=1024)
        return W1g, W2g

    def load_w(g):
        W1g, W2g = wgrp(g)
        S.dma("pool", W1g, w1v[:, :, g * 512:(g + 1) * 512], W=[("w1", g % 2)])
        S.dma("pool", W2g, w2v[:, g * 4:(g + 1) * 4, :], W=[("w2", g % 2)])

    load_w(0)
    load_w(1)
    for tb in range(4):
        rmsnorm_block(tb, 8, 24, lambda c: hT[:, c, tsl(tb)], sq, "sq")
    for g in range(8):
        W1g, W2g = wgrp(g)
        for tb in range(4):
            aT = arena[:, 32768 + (tb % 2) * 2048:32768 + (tb % 2) * 2048 + 2048].rearrange("p (c t) -> p c t", t=512)
            for c in range(4):
                bi = c % 3
                for k in range(8):
                    S.mm(P[bi][:, :], W1g[:, k, c * 128:(c + 1) * 128], hT[:, k, tsl(tb)], start=(k == 0), stop=(k == 7),
                         R=[("w1", g % 2), ("h", k, tb)], W=["P%d" % bi])
                S.act(tmpf[:, c, :], P[bi][:, :], AF.Relu, R=["P%d" % bi], W=["t%d" % c])
                S.tt(aT[:, c, :], tmpf[:, c, :], tmpf[:, c, :], ALU.mult, R=["t%d" % c], W=[("a", tb % 2, c)], eng="pool")
            for dc in range(8):
                bi = 3 + (dc % 4)
                for c in range(4):
                    S.mm(P[bi][:, :], W2g[:, c, dc * 128:(dc + 1) * 128], aT[:, c, :], start=(c == 0), stop=(c == 3),
                         R=[("w2", g % 2), ("a", tb % 2, c)], W=["P%d" % bi])
                S.stt(xT[:, dc, tsl(tb)], P[bi][:, :], modT[:, 40 + dc:41 + dc], xT[:, dc, tsl(tb)], ALU.mult, ALU.add,
                      R=["P%d" % bi, "modT", ("x", dc, tb)], W=[("x", dc, tb)])
        if g + 2 < 8:
            load_w(g + 2)

    yv = yT_d.rearrange("(c p) t -> p c t", p=128)
    for tb in range(4):
        S.dma("sp", yv[:, :, tsl(tb)], xT[:, :, tsl(tb)], R=[("x", c, tb) for c in range(8)], W=[("y", tb)])
    S.finish([("y", tb) for tb in range(4)])
    S.emit(st)
    st.close()
    return nc


_NC = {}


def _get_nc(mode):
    if mode not in _NC:
        _NC[mode] = build(mode)
    return _NC[mode]


def _consts():
    cb = np.zeros((128, NCB), np.float32)
    cb[:, 0:128] = 1.0 / 1024
    cb[0:64, 128:192] = 1.0 / 64
    cb[64:128, 192:256] = 1.0 / 64
    cb[:, 256:384] = 1.0 / 256
    cb[:, 384:512] = 1.0 / 128
    cb[0:64, 512:576] = 1.0 / 64
    cb[64:96, 576:608] = 1.0 / 32
    cb[0:32, 608:640] = 1.0 / 32
    cb[0:64, 640:704] = np.eye(64)
    cb[64:128, 640:704] = np.eye(64)
    for i in range(64):
        cb[i, 704 + 64 + i] = 1.0
    cf = np.zeros((128, NCF), np.float32)
    p = np.arange(128)
    cf[:, 0] = (10000.0 ** (-((p % 32) % 16).astype(np.float64) / 16.0)).astype(np.float32)
    cf[:, 1] = np.where((p % 32) < 16, -1.0, 1.0)
    cf[:, 2] = 1e-6
    cf[:, 3:67] = 1.0
    return cb, cf


def _mtab():
    slopes = np.array([2.0 ** (-8.0 * (h + 1) / 8) for h in range(8)], np.float64)
    k = np.arange(128)[:, None]
    c = np.arange(256)[None, :]
    dist = np.abs(c - 64 - k).astype(np.float64)
    m = np.zeros((8, 128, 3, 256), np.float64)
    for h in range(8):
        for cfg, d in enumerate((1, 4, 16)):
            m[h, :, cfg, :] = np.where(dist <= 64, np.exp(-slopes[h] * d * dist), 0.0)
    return m.reshape(8, 128, 768).astype(np.float32)


def _valid(t0):
    v = np.zeros((128, NTILE), np.float32)
    p = np.arange(128)
    for idx, (cfg, d, r, a, c0, c1, Q0) in enumerate(TILES):
        tok = (a + p) * d + r + t0 - 1024
        v[:, idx] = ((tok >= 0) & (tok < SEQ)).astype(np.float32)
    return v


def _layer_common(l, w_mod, b_mod, g_norm_mix, w_in, g_q_dil, g_k_dil, g_cq, g_ckv, g_q_nope, g_q_rope,
                  g_k_nope, g_k_rope, g_norm_mlp):
    vecs = np.zeros((128, 32), np.float32)
    vecs[:, 0:8] = g_norm_mix[l].reshape(8, 128).T
    vecs[:, 8:16] = g_norm_mlp[l].reshape(8, 128).T
    vecs[:, 16] = np.tile(g_q_dil[l], 2)
    vecs[:, 17] = np.tile(g_k_dil[l], 2)
    vecs[:, 18:20] = g_cq[l].reshape(2, 128).T
    vecs[:, 20] = g_ckv[l]
    vecs[0:64, 21] = g_q_nope[l]
    vecs[64:96, 21] = g_q_rope[l]
    vecs[0:64, 22] = g_q_nope[l]
    vecs[64:96, 22] = np.roll(g_q_rope[l], -16)
    vecs[0:64, 23] = g_k_nope[l]
    vecs[0:32, 24] = g_k_rope[l]
    vecs[0:32, 25] = np.roll(g_k_rope[l], -16)
    perm = (np.arange(32) + 16) % 32
    d = dict(
        w_mod=np.ascontiguousarray(w_mod[l]),
        bmodT=np.ascontiguousarray(b_mod[l].reshape(48, 128).T),
        w_in=np.ascontiguousarray(w_in[l]),
        w_in_sw=np.ascontiguousarray(w_in[l][:, 1920 + perm]),
        vecs=vecs,
    )
    return d


def kernel(x, c, positions, w_mod, b_mod, g_norm_mix, w_in, g_q_dil, g_k_dil, g_cq, w_q_up,
           g_ckv, w_kv_up, g_q_nope, g_q_rope, g_k_nope, g_k_rope, w_out, g_norm_mlp,
           w_mlp_in, w_mlp_out, _nlayers=2):
    A = lambda a: np.asarray(a)
    x, c, positions = A(x), A(c), A(positions)
    (w_mod, b_mod, g_norm_mix, w_in, g_q_dil, g_k_dil, g_cq, w_q_up, g_ckv, w_kv_up, g_q_nope, g_q_rope,
     g_k_nope, g_k_rope, w_out, g_norm_mlp, w_mlp_in, w_mlp_out) = [A(a).astype(np.float32, copy=False) for a in (
        w_mod, b_mod, g_norm_mix, w_in, g_q_dil, g_k_dil, g_cq, w_q_up, g_ckv, w_kv_up, g_q_nope, g_q_rope,
        g_k_nope, g_k_rope, w_out, g_norm_mlp, w_mlp_in, w_mlp_out)]
    cb, cf = _consts()
    mtab = _mtab()
    cores = list(range(8))
    xT = [np.ascontiguousarray(x[i // 4, (i % 4) * T:(i % 4 + 1) * T, :].T) for i in cores]
    base = []
    for i in cores:
        b, t0 = i // 4, (i % 4) * T
        base.append(dict(
            cT=np.ascontiguousarray(c[b].reshape(8, 128).T.astype(np.float32)),
            pos=np.ascontiguousarray(np.broadcast_to(positions[b, t0:t0 + T].astype(np.int32)[None, :], (96, T))),
            cstb=cb, cstf=cf))
    permq = np.arange(768).reshape(8, 96)
    permq = np.concatenate([permq[:, :64], permq[:, 64 + (np.arange(32) + 16) % 32]], axis=1).reshape(-1)
    for l in range(_nlayers):
        com = _layer_common(l, w_mod, b_mod, g_norm_mix, w_in, g_q_dil, g_k_dil, g_cq, g_ckv, g_q_nope, g_q_rope,
                            g_k_nope, g_k_rope, g_norm_mlp)
        in_a = [dict(xT=xT[i], **base[i], **com) for i in cores]
        ra = run_bass_kernel_spmd(_get_nc("A"), in_a, core_ids=cores).results
        lay = dict(
            w_q=np.ascontiguousarray(w_q_up[l]), w_q_sw=np.ascontiguousarray(w_q_up[l][:, permq]),
            w_kv=np.ascontiguousarray(w_kv_up[l]), w_out=np.ascontiguousarray(w_out[l]),
            w1=np.ascontiguousarray(w_mlp_in[l]), w2=np.ascontiguousarray(w_mlp_out[l]), mtab=mtab)
        in_l = []
        for i in cores:
            b, t0 = i // 4, (i % 4) * T
            grp = [ra[4 * b + j] for j in range(4)]
            lat = np.concatenate([g["o_lat"] for g in grp], axis=1)
            kd = np.concatenate([g["o_kd"] for g in grp], axis=1)
            vd = np.concatenate([g["o_vd"] for g in grp], axis=1)
            zpad = np.zeros((512, 1024), kd.dtype)
            kdp = np.concatenate([zpad, kd, zpad], axis=1)[:, t0:t0 + 4096]
            vdp = np.concatenate([zpad, vd, zpad], axis=1)[:, t0:t0 + 4096]
            in_l.append(dict(xT=xT[i], **base[i], **com, **lay, i_lat=np.ascontiguousarray(lat),
                             i_kd=np.ascontiguousarray(kdp), i_vd=np.ascontiguousarray(vdp), valid=_valid(t0)))
        rl = run_bass_kernel_spmd(_get_nc("L"), in_l, core_ids=cores).results
        xT = [np.ascontiguousarray(rl[i]["yT"]) for i in cores]
    out = np.empty((2, SEQ, 1024), np.float32)
    for i in cores:
        out[i // 4, (i % 4) * T:(i % 4 + 1) * T, :] = xT[i].T
    return out
```

```python
import numpy as np
import ml_dtypes
import concourse.bass as bass
import concourse.mybir as mybir
from concourse.bass_utils import run_bass_kernel_spmd
from contextlib import ExitStack

F32 = mybir.dt.float32
BF16 = mybir.dt.bfloat16
I32 = mybir.dt.int32
AF = mybir.ActivationFunctionType
ALU = mybir.AluOpType

NSLOT = 8
SAME_WIN = 3
T = 2048
SEQ = 8192
NCB = 832
NCF = 67
ARENA = 54272
PI = float(np.pi)


class _Op:
    __slots__ = ("eng", "emit", "stream", "idx", "waits", "ms", "clock", "pos")


class Sched:
    def __init__(self, nc):
        self.nc = nc
        self.eng_ops = {e: [] for e in ("pe", "act", "dve", "pool", "sp")}
        self.stream_ops = {}
        self.know = {e: {} for e in self.eng_ops}
        self.last_w = {}
        self.readers = {}
        self.dma_cnt = {"sp": 0, "pool": 0, "act": 0}

    def _add(self, eng, emit, R, W, dma=False, cstream=None):
        op = _Op()
        op.eng = eng
        op.emit = emit
        op.ms = False
        if dma:
            slot = self.dma_cnt[eng] % NSLOT
            self.dma_cnt[eng] += 1
            op.stream = ("dma", eng, slot)
        elif cstream is not None:
            op.stream = cstream
        else:
            op.stream = eng
        sl = self.stream_ops.setdefault(op.stream, [])
        op.idx = len(sl) + 1
        deps = []
        if dma and sl:
            deps.append(sl[-1])
        sl.append(op)
        for t in R:
            w = self.last_w.get(t)
            if w is not None:
                deps.append(w)
        for t in W:
            w = self.last_w.get(t)
            if w is not None:
                deps.append(w)
            deps.extend(self.readers.get(t, ()))
        eo = self.eng_ops[eng]
        op.pos = len(eo)
        K = self.know[eng]
        best = {}
        for d in deps:
            if d is op:
                continue
            b = best.get(d.stream)
            if b is None or b.idx < d.idx:
                best[d.stream] = d
        waits = []
        for s, d in best.items():
            if K.get(s, 0) >= d.idx:
                continue
            if s == eng:
                if eng == "pe" or eng == "sp" or (op.pos - d.pos) > SAME_WIN:
                    continue
                d.ms = True
                waits.append(d)
                K[s] = d.idx
                continue
            d.ms = True
            waits.append(d)
            for s2, i2 in d.clock.items():
                if K.get(s2, 0) < i2:
                    K[s2] = i2
        op.waits = waits
        op.clock = dict(K)
        op.clock[op.stream] = op.idx
        if dma or cstream is not None:
            op.ms = True
        eo.append(op)
        for t in R:
            self.readers.setdefault(t, []).append(op)
        for t in W:
            self.last_w[t] = op
            self.readers[t] = []
        return op

    def mm(self, out, lhsT, rhs, start=True, stop=True, R=(), W=(), **kw):
        return self._add("pe", lambda e: e.matmul(out, lhsT, rhs, start=start, stop=stop, **kw), R, W)

    def transpose(self, out, in_, ident, R=(), W=()):
        return self._add("pe", lambda e: e.transpose(out, in_, ident), R, W)

    def act(self, out, in_, func, R=(), W=(), **kw):
        return self._add("act", lambda e: e.activation(out=out, in_=in_, func=func, **kw), R, W)

    def tt(self, out, in0, in1, op, R=(), W=(), eng="dve"):
        return self._add(eng, lambda e: e.tensor_tensor(out, in0, in1, op), R, W)

    def ts(self, out, in0, s1, s2, op0, op1=None, R=(), W=(), eng="dve"):
        if op1 is None:
            return self._add(eng, lambda e: e.tensor_scalar(out, in0, s1, None, op0), R, W)
        return self._add(eng, lambda e: e.tensor_scalar(out, in0, s1, s2, op0, op1), R, W)

    def stt(self, out, in0, scalar, in1, op0, op1, R=(), W=(), eng="dve"):
        return self._add(eng, lambda e: e.scalar_tensor_tensor(out, in0, scalar, in1, op0, op1), R, W)

    def copy(self, out, in_, R=(), W=(), eng="dve"):
        if eng == "act":
            return self._add("act", lambda e: e.copy(out, in_), R, W)
        return self._add(eng, lambda e: e.tensor_copy(out, in_), R, W)

    def recip(self, out, in_, R=(), W=()):
        return self._add("dve", lambda e: e.reciprocal(out, in_), R, W)

    def memset(self, ap, val, W=(), eng="pool"):
        return self._add(eng, lambda e: e.memset(ap, val), (), W)

    def dma(self, q, out, in_, R=(), W=()):
        return self._add(q, lambda e: e.dma_start(out=out, in_=in_), R, W, dma=True)

    def barrier(self):
        lasts = [ops[-1] for ops in self.stream_ops.values() if ops]
        for e in ("pe", "act", "dve", "pool", "sp"):
            op = _Op()
            op.eng = e
            op.emit = None
            op.ms = False
            op.stream = None
            op.idx = 0
            eo = self.eng_ops[e]
            op.pos = len(eo)
            K = self.know[e]
            waits = []
            for d in lasts:
                if d.stream == e:
                    if e != "pe" and K.get(e, 0) < d.idx:
                        d.ms = True
                        waits.append(d)
                        K[e] = d.idx
                    continue
                if K.get(d.stream, 0) >= d.idx:
                    continue
                d.ms = True
                waits.append(d)
                for s2, i2 in d.clock.items():
                    if K.get(s2, 0) < i2:
                        K[s2] = i2
            op.waits = waits
            op.clock = dict(K)
            eo.append(op)

    def finish(self, R):
        return self._add("sp", None, R, ())

    def emit(self, stack):
        nc = self.nc
        sems = {}
        order = sorted(self.stream_ops, key=lambda s: 0 if (isinstance(s, tuple) and s[0] == "cc") else 1)
        for s in order:
            nm = "s_" + ("_".join(str(x) for x in s) if isinstance(s, tuple) else s)
            sems[s] = stack.enter_context(nc.semaphore(nm))
        val = {}
        for s, ops in self.stream_ops.items():
            c = 0
            inc = 16 if (isinstance(s, tuple) and s[0] == "dma") else 1
            for o in ops:
                if o.ms:
                    c += inc
                val[id(o)] = c
        block = stack.enter_context(nc.Block())

        def run(eng_name, e):
            for o in self.eng_ops[eng_name]:
                for d in o.waits:
                    e.wait_ge(sems[d.stream], val[id(d)])
                if o.emit is None:
                    continue
                ins = o.emit(e)
                if o.ms and ins is not None:
                    ins.then_inc(sems[o.stream], 16 if (isinstance(o.stream, tuple) and o.stream[0] == "dma") else 1)

        @block.tensor
        def _(e):
            run("pe", e)

        @block.scalar
        def _(e):
            run("act", e)

        @block.vector
        def _(e):
            run("dve", e)

        @block.gpsimd
        def _(e):
            run("pool", e)

        @block.sync
        def _(e):
            run("sp", e)


def dil_tiles():
    tiles = []
    for cfg, d in enumerate((1, 4, 16)):
        Q0 = 1024 // d
        nq = 2048 // d
        nt = nq // 128 + 1
        for r in range(d):
            for i in range(nt):
                a = Q0 - 64 + 128 * i
                c0 = 128 if i == 0 else 0
                c1 = 128 if i == nt - 1 else 256
                tiles.append((cfg, d, r, a, c0, c1, Q0))
    return tiles


TILES = dil_tiles()
NTILE = len(TILES)


def build(mode):
    nc = bass.Bass("TRN2", target_bir_lowering=False)
    D = lambda n, sh, dt, kind="ExternalInput": nc.dram_tensor(n, sh, dt, kind=kind).ap()
    xT_d = D("xT", [1024, T], F32)
    cT_d = D("cT", [128, 8], F32)
    pos_d = D("pos", [96, T], I32)
    NL = 2 if mode == "F" else 1
    LD = (lambda n, sh, dt: D(n, [2] + sh, dt)) if mode == "F" else (lambda n, sh, dt: D(n, sh, dt))
    wmod_d = LD("w_mod", [1024, 6144], F32)
    bmodT_d = LD("bmodT", [128, 48], F32)
    win_d = LD("w_in", [1024, 1952], F32)
    winsw_d = LD("w_in_sw", [1024, 32], F32)
    vecs_d = LD("vecs", [128, 32], F32)
    cstb_d = D("cstb", [128, NCB], F32)
    cstf_d = D("cstf", [128, NCF], F32)
    LW = (lambda ap, l: ap[l]) if mode == "F" else (lambda ap, l: ap)
    if mode == "A":
        okd_d = D("o_kd", [512, T], BF16, "ExternalOutput")
        ovd_d = D("o_vd", [512, T], BF16, "ExternalOutput")
        olat_d = D("o_lat", [160, T], BF16, "ExternalOutput")
    else:
        wq_d = LD("w_q", [256, 768], F32)
        wqsw_d = LD("w_q_sw", [256, 768], F32)
        wkv_d = LD("w_kv", [128, 1024], F32)
        wout_d = LD("w_out", [1024, 1024], F32)
        w1_d = LD("w1", [1024, 4096], F32)
        w2_d = LD("w2", [4096, 1024], F32)
        mtab_d = D("mtab", [8, 128, 768], F32)
        valid_d = D("valid", [128, NTILE], F32)
        yT_d = D("yT", [1024, T], F32, "ExternalOutput")
    if mode == "L":
        ikd_d = D("i_kd", [512, 4096], BF16)
        ivd_d = D("i_vd", [512, 4096], BF16)
        ilat_d = D("i_lat", [160, SEQ], BF16)
    if mode == "F":
        wsel_d = D("wsel", [128, 8], F32)
        okd_h = [nc.dram_tensor("b_kd%d" % i, [256, T], BF16).ap() for i in range(2)]
        ovd_h = [nc.dram_tensor("b_vd%d" % i, [256, T], BF16).ap() for i in range(2)]
        olat_d = nc.dram_tensor("b_lat", [160, T], BF16).ap()
        gkd_h = [nc.dram_tensor("g_kd%d" % i, [4 * 256, T], BF16).ap() for i in range(2)]
        gvd_h = [nc.dram_tensor("g_vd%d" % i, [4 * 256, T], BF16).ap() for i in range(2)]
        glat_d = nc.dram_tensor("g_lat", [4 * 160, T], BF16).ap()

    st = ExitStack()
    sb = lambda n, sh, dt: st.enter_context(nc.sbuf_tensor(n, sh, dt))
    xT = sb("xTs", [128, 8, T], F32)
    cosT = sb("cosT", [96, T], F32)
    sinT = sb("sinT", [96, T], F32)
    tmpf = sb("tmpf", [128, 4, 512], F32)
    tmpi = sb("tmpi", [96, 512], I32)
    posi = sb("posi", [96, T], I32) if False else None
    cstb = sb("cstbs", [128, NCB], BF16)
    cstf = sb("cstfs", [128, NCF], F32)
    vecs = sb("vecss", [128, 32], F32)
    cTs = sb("cTs", [128, 8], F32)
    cact = sb("cact", [128, 8], BF16)
    bmodT = sb("bmodTs", [128, 48], F32)
    modT = sb("modT", [128, 48], F32)
    gm = sb("gm", [128, 16], F32)
    modrow = sb("modrow", [1, 2, 512], F32)
    arena = sb("arena", [128, ARENA], BF16)
    P = [st.enter_context(nc.psum_tensor("ps%d" % i, [128, 512], F32)) for i in range(7)]
    psb = st.enter_context(nc.psum_tensor("psb", [128, 1024], BF16))
    if mode != "A":
        mtb = sb("mtb", [128, 768], F32)
        vld = sb("vld", [128, NTILE], F32)
    if mode == "F":
        wsel = sb("wsels", [128, 8], F32)

    S = Sched(nc)

    def ar(off, n, shape=None):
        a = arena[:, off:off + n]
        return a

    ones1024 = cstb[:, 0:128]
    blk64 = cstb[:, 128:256]
    ones256 = cstb[:, 256:384]
    ones128 = cstb[:, 384:512]
    blkq96 = cstb[0:96, 512:608]
    ones32 = cstb[0:32, 608:640]
    shift64 = cstb[0:64, 704:832]
    eps_c = lambda lo, hi: cstf[lo:hi, 2:3]

    def tsl(tb):
        return slice(tb * 512, (tb + 1) * 512)

    S.dma("pool", cstb[:], cstb_d[:, :], W=["cstb"])
    S.dma("sp", cstf[:], cstf_d[:, :], W=["cstf"])
    S.dma("sp", cTs[:], cT_d[:, :], W=["cT"])
    xv = xT_d.rearrange("(c p) t -> p c t", p=128)
    for tb in range(4):
        S.dma("sp", xT[:, :, tsl(tb)], xv[:, :, tsl(tb)], W=[("x", c, tb) for c in range(8)])
    if mode != "A":
        S.dma("sp", vld[:], valid_d[:, :], W=["vld"])
    if mode == "F":
        S.dma("sp", wsel[:], wsel_d[:, :], W=["wsel"])

    S.act(cact[:], cTs[:], AF.Silu, R=["cT"], W=["cact"])
    for tb in range(4):
        S.dma("sp", tmpi[:, :], pos_d[:, tsl(tb)], W=["tmpi"])
        t0 = tmpf[0:96, 0, :]
        t1 = tmpf[0:96, 1, :]
        t2 = tmpf[0:96, 2, :]
        t3 = tmpf[0:96, 3, :]
        S.copy(t0, tmpi[:, :], R=["tmpi"], W=["t0"])
        S.ts(t0, t0, cstf[0:96, 0:1], None, ALU.mult, R=["t0", "cstf"], W=["t0"])
        for tab, shift, use_sign in ((sinT, 0.0, True), (cosT, PI / 2, False)):
            S.ts(t1, t0, shift, None, ALU.add, R=["t0"], W=["t1"])
            S.ts(t2, t1, 1.0 / (2 * PI), None, ALU.mult, R=["t1"], W=["t2"])
            S.copy(tmpi[:, :], t2, R=["t2"], W=["tmpi"])
            S.copy(t2, tmpi[:, :], R=["tmpi"], W=["t2"])
            S.stt(t3, t2, -2 * PI, t1, ALU.mult, ALU.add, R=["t2", "t1"], W=["t3"])
            S.ts(t2, t3, PI, -2 * PI, ALU.is_gt, ALU.mult, R=["t3"], W=["t2"])
            S.tt(t3, t3, t2, ALU.add, R=["t3", "t2"], W=["t3"])
            S.ts(t2, t3, -PI, 2 * PI, ALU.is_lt, ALU.mult, R=["t3"], W=["t2"])
            S.tt(t3, t3, t2, ALU.add, R=["t3", "t2"], W=["t3"])
            S.ts(t3, t3, PI, -PI, ALU.min, ALU.max, R=["t3"], W=["t3"])
            if use_sign:
                S.act(tab[:, tsl(tb)], t3, AF.Sin, R=["t3", "cstf"], W=[("rope", tb)], scale=cstf[0:96, 1:2])
            else:
                S.act(tab[:, tsl(tb)], t3, AF.Sin, R=["t3"], W=[("rope", tb)])

    for l in range(NL):
        S.dma("sp", vecs[:], LW(vecs_d, l)[:, :], W=["vecs"])
        S.dma("sp", bmodT[:], LW(bmodT_d, l)[:, :], W=["bmodT"])
        wmv = LW(wmod_d, l).rearrange("(k p) n -> p k n", p=128)
        ngrp = 4 if mode == "A" else 12
        for g in range(ngrp):
            wm = arena[:, 45056 + (g % 2) * 4096:45056 + (g % 2) * 4096 + 4096].rearrange("p (k n) -> p k n", n=512)
            S.dma("pool", wm, wmv[:, :, g * 512:(g + 1) * 512], W=[("wm", g % 2)])
            for k in range(8):
                S.mm(P[0][0:1, :], cact[:, k:k + 1], wm[:, k, :], start=(k == 0), stop=(k == 7),
                     R=["cact", ("wm", g % 2)], W=["P0"])
            S.copy(modrow[0:1, g % 2, :], P[0][0:1, :], R=["P0"], W=[("mr", g % 2)])
            for jj in range(4):
                j = g * 4 + jj
                S.mm(P[1][:, j:j + 1], modrow[0:1, g % 2, jj * 128:(jj + 1) * 128], cstf[0:1, 3:4],
                     start=True, stop=True, R=[("mr", g % 2), "cstf"], W=["P1"], skip_group_check=True)
        S.tt(modT[:, 0:ngrp * 4], P[1][:, 0:ngrp * 4], bmodT[:, 0:ngrp * 4], ALU.add, R=["P1", "bmodT"], W=["modT"])
        S.stt(gm[:, 0:8], modT[:, 8:16], 1.0, vecs[:, 0:8], ALU.add, ALU.mult, R=["modT", "vecs"], W=["gm"])
        if mode != "A":
            S.stt(gm[:, 8:16], modT[:, 32:40], 1.0, vecs[:, 8:16], ALU.add, ALU.mult, R=["modT", "vecs"], W=["gm"])

        QD, CQ, OT = 0, 8192, 12288
        qdT = arena[:, QD:QD + 8192].rearrange("p (c t) -> p c t", t=T)
        cqnT = arena[:, CQ:CQ + 4096].rearrange("p (c t) -> p c t", t=T)
        oT = arena[:, OT:OT + 16384].rearrange("p (c t) -> p c t", t=T)

        def rstd_from(Pn_ap, out_ap, lo, hi, R, W):
            S.act(out_ap, Pn_ap, AF.Ln, R=R + ["cstf"], W=W, bias=eps_c(lo, hi), scale=1.0)
            S.act(out_ap, out_ap, AF.Exp, R=W, W=W, scale=-0.5)

        def rmsnorm_block(tb, gcol, shcol, hout, sq, sqtok):
            S.act(sq, xT[:, :, tsl(tb)], AF.Square, R=[("x", c, tb) for c in range(8)], W=[sqtok])
            for c in range(8):
                S.mm(P[6][:, :], ones1024, sq[:, c, :], start=(c == 0), stop=(c == 7), R=[sqtok, "cstb"], W=["P6"])
            rstd_from(P[6][:, :], tmpf[:, 0, :], 0, 128, ["P6"], ["t0"])
            for c in range(8):
                sl = 1 + (c % 3)
                S.stt(tmpf[:, sl, :], xT[:, c, tsl(tb)], gm[:, gcol + c:gcol + c + 1], tmpf[:, 0, :], ALU.mult, ALU.mult,
                      R=[("x", c, tb), "t0", "gm"], W=["t%d" % sl])
                S.act(hout(c), tmpf[:, sl, :], AF.Identity, R=["t%d" % sl, "modT"], W=[("h", c, tb)],
                      bias=modT[:, shcol + c:shcol + c + 1], scale=1.0)

        WIN = 12288
        winT = arena[:, WIN:WIN + 15616].rearrange("p (k n) -> p k n", n=1952)
        HB = WIN + 15616
        SQ = HB + 8192
        STG = SQ + 4096
        wsw = arena[:, STG + 4096:STG + 4096 + 256].rearrange("p (k n) -> p k n", n=32)
        wiv = LW(win_d, l).rearrange("(k p) n -> p k n", p=128)
        S.dma("pool", winT[:, :, 0:976], wiv[:, :, 0:976], W=["win"])
        S.dma("pool", winT[:, :, 976:1952], wiv[:, :, 976:1952], W=["win"])
        S.dma("pool", wsw, LW(winsw_d, l).rearrange("(k p) n -> p k n", p=128), W=["wsw"])
        pbank = [0]

        def nextP():
            pbank[0] = (pbank[0] + 1) % 4
            return pbank[0], P[pbank[0]]

        stg_i = [0]

        def stage():
            stg_i[0] = (stg_i[0] + 1) % 4
            i = stg_i[0]
            return ("stg", i), arena[:, STG + i * 512:STG + (i + 1) * 512]

        def proj(tb, hbuf, col0, M, wt=None, wtok="win"):
            bi, Pp = nextP()
            for k in range(8):
                lhsT = winT[:, k, col0:col0 + M] if wt is None else wt[:, k, 0:M]
                S.mm(Pp[0:M, :], lhsT, hbuf[:, k, :], start=(k == 0), stop=(k == 7),
                     R=[wtok] + [("h", k, tb)], W=["P%d" % bi])
            return "P%d" % bi, Pp

        for tb in range(4):
            hb_off = HB + (tb % 2) * 4096
            hbuf = arena[:, hb_off:hb_off + 4096].rearrange("p (k t) -> p k t", t=512)
            sq = arena[:, SQ:SQ + 4096].rearrange("p (k t) -> p k t", t=512)
            rmsnorm_block(tb, 0, 0, lambda c: hbuf[:, c, :], sq, "sq")
            sqh = arena[:, SQ:SQ + 512]

            def headnorm(ptok, Pp, onesm, gcol, out_ap, Wt, rows=128):
                S.act(sqh[0:rows, :], Pp[0:rows, :], AF.Square, R=[ptok], W=["sq"])
                S.mm(P[5][0:rows, :], onesm, sqh[0:rows, :], R=["sq", "cstb"], W=["P5"])
                rstd_from(P[5][0:rows, :], tmpf[0:rows, 0, :], 0, rows, ["P5"], ["t0"])
                S.stt(out_ap, Pp[0:rows, :], vecs[0:rows, gcol:gcol + 1], tmpf[0:rows, 0, :], ALU.mult, ALU.mult,
                      R=[ptok, "t0", "vecs"], W=Wt)

            if mode != "A":
                for j in range(4):
                    ptok, Pp = proj(tb, hbuf, j * 128, 128)
                    headnorm(ptok, Pp, blk64, 16, qdT[:, j, tsl(tb)], [("qd", j, tb)])
                pa = proj(tb, hbuf, 1536, 128)
                pb2 = proj(tb, hbuf, 1664, 128)
                for i, (ptok, Pp) in enumerate((pa, pb2)):
                    S.act(sq[:, i, :], Pp[:, :], AF.Square, R=[ptok], W=["sq"])
                for i in range(2):
                    S.mm(P[5][:, :], ones256, sq[:, i, :], start=(i == 0), stop=(i == 1), R=["sq", "cstb"], W=["P5"])
                rstd_from(P[5][:, :], tmpf[:, 0, :], 0, 128, ["P5"], ["t0"])
                for i, (ptok, Pp) in enumerate((pa, pb2)):
                    S.stt(cqnT[:, i, tsl(tb)], Pp[:, :], vecs[:, 18 + i:19 + i], tmpf[:, 0, :], ALU.mult, ALU.mult,
                          R=[ptok, "t0", "vecs"], W=[("cq", i, tb)])
            if mode != "L":
                for j in range(4):
                    ptok, Pp = proj(tb, hbuf, 512 + j * 128, 128)
                    stok, stg = stage()
                    headnorm(ptok, Pp, blk64, 17, stg, [stok])
                    S.dma("sp", (okd_d[j * 128:(j + 1) * 128, tsl(tb)] if mode == "A" else okd_h[j // 2][(j % 2) * 128:(j % 2 + 1) * 128, tsl(tb)]), stg, R=[stok], W=[("okd", j, tb)])
                for j in range(4):
                    ptok, Pp = proj(tb, hbuf, 1024 + j * 128, 128)
                    stok, stg = stage()
                    S.copy(stg, Pp[:, :], R=[ptok], W=[stok])
                    S.dma("sp", (ovd_d[j * 128:(j + 1) * 128, tsl(tb)] if mode == "A" else ovd_h[j // 2][(j % 2) * 128:(j % 2 + 1) * 128, tsl(tb)]), stg, R=[stok], W=[("ovd", j, tb)])
                ptok, Pp = proj(tb, hbuf, 1792, 128)
                stok, stg = stage()
                headnorm(ptok, Pp, ones128, 20, stg, [stok])
                S.dma("sp", olat_d[0:128, tsl(tb)], stg, R=[stok], W=[("olat", 0, tb)])
                ptok, Pp = proj(tb, hbuf, 1920, 32)
                ptok2, Pp2 = proj(tb, hbuf, 0, 32, wt=wsw, wtok="wsw")
                S.act(sqh[0:32, :], Pp[0:32, :], AF.Square, R=[ptok], W=["sq"])
                S.mm(P[5][0:32, :], ones32, sqh[0:32, :], R=["sq", "cstb"], W=["P5"])
                rstd_from(P[5][0:32, :], tmpf[0:32, 0, :], 0, 32, ["P5"], ["t0"])
                S.stt(tmpf[0:32, 1, :], Pp[0:32, :], vecs[0:32, 24:25], tmpf[0:32, 0, :], ALU.mult, ALU.mult,
                      R=[ptok, "t0", "vecs"], W=["t1"])
                S.stt(tmpf[0:32, 2, :], Pp2[0:32, :], vecs[0:32, 25:26], tmpf[0:32, 0, :], ALU.mult, ALU.mult,
                      R=[ptok2, "t0", "vecs"], W=["t2"])
                S.tt(tmpf[0:32, 1, :], tmpf[0:32, 1, :], cosT[0:32, tsl(tb)], ALU.mult, R=["t1", ("rope", tb)], W=["t1"])
                S.tt(tmpf[0:32, 2, :], tmpf[0:32, 2, :], sinT[0:32, tsl(tb)], ALU.mult, R=["t2", ("rope", tb)], W=["t2"])
                stok, stg = stage()
                S.tt(stg[0:32, :], tmpf[0:32, 1, :], tmpf[0:32, 2, :], ALU.add, R=["t1", "t2"], W=[stok])
                S.dma("sp", olat_d[128:160, tsl(tb)], stg[0:32, :], R=[stok], W=[("olat", 1, tb)])

        if mode == "A":
            outs = [("okd", j, tb) for j in range(4) for tb in range(4)] + [("ovd", j, tb) for j in range(4) for tb in range(4)] \
                + [("olat", i, tb) for i in range(2) for tb in range(4)]
            S.finish(outs)
            S.emit(st)
            st.close()
            return nc

        if mode == "F":
            for nm, src, dst, toks in (
                ("lat", olat_d, glat_d, [("olat", i, tb) for i in range(2) for tb in range(4)]),
                ("kd0", okd_h[0], gkd_h[0], [("okd", j, tb) for j in (0, 1) for tb in range(4)]),
                ("vd0", ovd_h[0], gvd_h[0], [("ovd", j, tb) for j in (0, 1) for tb in range(4)]),
                ("kd1", okd_h[1], gkd_h[1], [("okd", j, tb) for j in (2, 3) for tb in range(4)]),
                ("vd1", ovd_h[1], gvd_h[1], [("ovd", j, tb) for j in (2, 3) for tb in range(4)]),
            ):
                S._add("pool", (lambda e, src=src, dst=dst: e.collective_compute(
                    "AllGather", ALU.bypass, replica_groups=[[0, 1, 2, 3], [4, 5, 6, 7]],
                    ins=[src.opt()], outs=[dst.opt()])), toks, [("g", nm)], cstream=("cc", 0))

        S.barrier()

        def norm_out(Pacc, ptok, ch, hh, tb):
            S.recip(tmpf[64:65, 2, :], Pacc[64:65, :], R=[ptok], W=["t2"])
            S.mm(P[6][0:64, :], cstf[64:65, 3:67], tmpf[64:65, 2, :], R=["t2", "cstf"], W=["P6"])
            S.copy(tmpf[0:64, 3, :], P[6][0:64, :], R=["P6"], W=["t3"], eng="act")
            if hh == 0:
                S.tt(oT[0:64, ch, tsl(tb)], Pacc[0:64, :], tmpf[0:64, 3, :], ALU.mult, R=[ptok, "t3"], W=[("o", ch, tb, 0)])
            else:
                ost = arena[0:64, OST:OST + 512]
                S.tt(ost, Pacc[0:64, :], tmpf[0:64, 3, :], ALU.mult, R=[ptok, "t3"], W=["ost"])
                S.mm(P[6][:, :], shift64, ost, R=["ost", "cstb"], W=["P6"])
                S.copy(oT[64:128, ch, tsl(tb)], P[6][64:128, :], R=["P6"], W=[("o", ch, tb, 1)])

        B0 = 28672
        kdT = arena[:, B0:B0 + 4096]
        vdT = arena[:, B0 + 4096:B0 + 8192]
        VT0 = B0 + 8192
        Vt = arena[:, VT0:VT0 + NTILE * 65].rearrange("p (i c) -> p i c", c=65)
        PT0 = VT0 + NTILE * 65 + 3
        OST = PT0 + 4 * 256
        CAND = OST + 512
        cd_i = [0]
        S.memset(Vt[:, :, 64:65], 1.0, W=["vt1"])
        ident = cstb[:, 640:704]
        for h in range(8):
            pp, hh = h // 2, h % 2
            pb = hh * 64
            if hh == 0 and mode == "L":
                S.dma("sp", kdT, ikd_d[pp * 128:(pp + 1) * 128, :], W=["kdT"])
                S.dma("sp", vdT, ivd_d[pp * 128:(pp + 1) * 128, :], W=["vdT"])
            if hh == 0 and mode == "F":
                hf, hr = pp // 2, (pp % 2) * 128
                for dstT, own_d, g_d, nm, dtok in ((kdT, okd_h[hf], gkd_h[hf], "kd", "kdT"), (vdT, ovd_h[hf], gvd_h[hf], "vd", "vdT")):
                    S.dma("sp", dstT[:, 1024:3072], own_d[hr:hr + 128, :],
                          R=[(("okd" if nm == "kd" else "ovd"), pp, tb) for tb in range(4)], W=[dtok])
                    for side, (lo, hi, c0_, cands, wc0) in enumerate(((0, 1024, 1024, (0, 1, 2), 0), (3072, 4096, 0, (1, 2, 3), 4))):
                        for ci, r in enumerate(cands):
                            cd_i[0] = (cd_i[0] + 1) % 4
                            cand = arena[:, CAND + cd_i[0] * 1024:CAND + cd_i[0] * 1024 + 1024]
                            ctok = ("cand", cd_i[0])
                            S.dma("sp", cand, g_d[r * 256 + hr:r * 256 + hr + 128, c0_:c0_ + 1024],
                                  R=[("g", nm + str(hf))], W=[ctok])
                            if ci == 0:
                                S.ts(dstT[:, lo:hi], cand, wsel[:, wc0 + r:wc0 + r + 1], None, ALU.mult,
                                     R=[ctok, "wsel"], W=[dtok])
                            else:
                                S.stt(dstT[:, lo:hi], cand, wsel[:, wc0 + r:wc0 + r + 1], dstT[:, lo:hi], ALU.mult, ALU.add,
                                      R=[ctok, "wsel", dtok], W=[dtok])
            S.dma("sp", mtb[:], mtab_d[h, :, :], W=["mtb"])
            for i0 in range(0, NTILE, 16):
                n = min(16, NTILE - i0)
                for s in range(n):
                    cfg, d, r, a, c0, c1, Q0 = TILES[i0 + s]
                    S.transpose(psb[:, s * 64:(s + 1) * 64], vdT[pb:pb + 64, bass.ds(a * d + r, 128, d)],
                                ident[pb:pb + 64, :], R=["vdT", "cstb"], W=["psb"])
                S.copy(Vt[:, i0:i0 + n, 0:64], psb[:, 0:n * 64].rearrange("p (a b) -> p a b", b=64),
                       R=["psb"], W=[("vt", i0)])
            first = [True] * 4
            last_idx = {}
            segs_all = []
            for idx, (cfg, d, r, a, c0, c1, Q0) in enumerate(TILES):
                n0 = (a - 64 + c0 - Q0) * d + r
                segs = []
                c = c0
                while c < c1:
                    n = n0 + (c - c0) * d
                    tb = n // 512
                    cend = c
                    while cend < c1 and (n0 + (cend - c0) * d) // 512 == tb:
                        cend += 1
                    segs.append((tb, c, cend, n - 512 * tb))
                    last_idx[tb] = (idx, len(segs) - 1)
                    c = cend
                segs_all.append(segs)
            SB = (0, 1, 6)

            def dil_S(idx):
                cfg, d, r, a, c0, c1, Q0 = TILES[idx]
                ncol = c1 - c0
                n0 = (a - 64 + c0 - Q0) * d + r
                bi = SB[idx % 3]
                S.mm(P[bi][:, 0:ncol], kdT[pb:pb + 64, bass.ds(a * d + r, 128, d)],
                     qdT[pb:pb + 64, pp, bass.ds(n0, ncol, d)], R=["kdT"] + [("qd", pp, t) for t in range(4)], W=["P%d" % bi])

            def dil_rest(idx):
                cfg, d, r, a, c0, c1, Q0 = TILES[idx]
                ncol = c1 - c0
                bi = SB[idx % 3]
                ti = idx % 3
                tf = tmpf[:, ti, 0:ncol]
                S.act(tf, P[bi][:, 0:ncol], AF.Exp, R=["P%d" % bi], W=["t%d" % ti], scale=0.125)
                pt = arena[:, PT0 + (idx % 4) * 256:PT0 + (idx % 4) * 256 + ncol]
                S.stt(pt, tf, vld[:, idx:idx + 1], mtb[:, cfg * 256 + c0:cfg * 256 + c1], ALU.mult, ALU.mult,
                      R=["t%d" % ti, "vld", "mtb"], W=[("pt", idx % 4)])
                for si, (tb, ca, cb, nloc) in enumerate(segs_all[idx]):
                    S.mm(P[2 + tb][0:65, bass.ds(nloc, cb - ca, d)], Vt[:, idx, 0:65], pt[:, ca - c0:cb - c0],
                         start=first[tb], stop=(last_idx[tb] == (idx, si)), R=[("pt", idx % 4), ("vt", (idx // 16) * 16), "vt1"],
                         W=["P%d" % (2 + tb)], skip_group_check=True)
                    first[tb] = False

            LA = 2
            for idx in range(min(LA, NTILE)):
                dil_S(idx)
            for idx in range(NTILE):
                if idx + LA < NTILE:
                    dil_S(idx + LA)
                dil_rest(idx)
            for tb in range(4):
                norm_out(P[2 + tb], "P%d" % (2 + tb), pp, hh, tb)

        S.barrier()

        QH = 0
        Qh = arena[:, QH:QH + 2048]
        WQ = 2048
        wq = arena[:, WQ:WQ + 1536].rearrange("p (k n) -> p k n", n=768)
        wqsw = arena[:, WQ + 1536:WQ + 3072].rearrange("p (k n) -> p k n", n=768)
        wkv = arena[:, WQ + 3072:WQ + 4096]
        PTC = WQ + 4096
        CK = 28672
        ckv = arena[:, CK:CK + 8192]
        Kh = arena[:, CK + 8192:CK + 16384]
        Vh = arena[:, CK + 16384:CK + 16384 + 64 * 65].rearrange("p (i c) -> p i c", c=65)
        SQC = CK + 16384 + 64 * 65
        OST = SQC + 512
        if mode == "L":
            for q4 in range(4):
                S.dma("sp", ckv[:, q4 * 2048:(q4 + 1) * 2048], ilat_d[0:128, q4 * 2048:(q4 + 1) * 2048], W=[("ckv", q4)])
            S.dma("sp", Kh[64:96, :], ilat_d[128:160, :], W=["Khr"])
        else:
            for q4 in range(4):
                S.dma("sp", ckv[:, q4 * 2048:(q4 + 1) * 2048], glat_d[q4 * 160:q4 * 160 + 128, :], R=[("g", "lat")], W=[("ckv", q4)])
                S.dma("sp", Kh[64:96, q4 * 2048:(q4 + 1) * 2048], glat_d[q4 * 160 + 128:q4 * 160 + 160, :], R=[("g", "lat")], W=["Khr"])
        S.dma("pool", wq, LW(wq_d, l).rearrange("(k p) n -> p k n", p=128), W=["wq"])
        S.dma("pool", wqsw, LW(wqsw_d, l).rearrange("(k p) n -> p k n", p=128), W=["wqsw"])
        S.dma("pool", wkv, LW(wkv_d, l)[:, :], W=["wkv"])
        S.memset(Vh[:, :, 64:65], 1.0, W=["vh1"])
        sqc = arena[:, SQC:SQC + 512]
        scale_mla = float(96.0 ** -0.5)
        for h in range(8):
            pp, hh = h // 2, h % 2
            for kb in range(16):
                ks = slice(kb * 512, (kb + 1) * 512)
                S.mm(P[4][0:64, :], wkv[:, h * 128:h * 128 + 64], ckv[:, ks], R=["wkv", ("ckv", kb // 4)], W=["P4"])
                S.copy(tmpf[0:64, 0, :], P[4][0:64, :], R=["P4"], W=["t0"])
                S.tt(sqc[0:64, :], tmpf[0:64, 0, :], tmpf[0:64, 0, :], ALU.mult, R=["t0"], W=["sqc"], eng="pool")
                S.mm(P[5][0:64, :], blk64[0:64, 0:64], sqc[0:64, :], R=["sqc", "cstb"], W=["P5"])
                rstd_from(P[5][0:64, :], tmpf[0:64, 1, :], 0, 64, ["P5"], ["t1"])
                S.stt(Kh[0:64, ks], tmpf[0:64, 0, :], vecs[0:64, 23:24], tmpf[0:64, 1, :], ALU.mult, ALU.mult,
                      R=["t0", "t1", "vecs"], W=[("Kh", kb)])
            for g in range(8):
                for s in range(8):
                    kb2 = g * 8 + s
                    S.mm(P[4][:, s * 64:(s + 1) * 64], ckv[:, kb2 * 128:(kb2 + 1) * 128], wkv[:, h * 128 + 64:h * 128 + 128],
                         R=["wkv", ("ckv", kb2 // 16)], W=["P4"])
                S.copy(Vh[:, g * 8:(g + 1) * 8, 0:64], P[4][:, :].rearrange("p (a b) -> p a b", b=64), R=["P4"], W=[("Vh", g)])
            for tb in range(4):
                for kk in range(2):
                    S.mm(P[4][0:96, :], wq[:, kk, h * 96:(h + 1) * 96], cqnT[:, kk, tsl(tb)], start=(kk == 0), stop=(kk == 1),
                         R=["wq", ("cq", kk, tb)], W=["P4"])
                for kk in range(2):
                    S.mm(P[5][0:96, :], wqsw[:, kk, h * 96:(h + 1) * 96], cqnT[:, kk, tsl(tb)], start=(kk == 0), stop=(kk == 1),
                         R=["wqsw", ("cq", kk, tb)], W=["P5"])
                S.act(sqc[0:96, :], P[4][0:96, :], AF.Square, R=["P4"], W=["sqc"])
                S.mm(P[6][0:96, :], blkq96, sqc[0:96, :], R=["sqc", "cstb"], W=["P6"])
                rstd_from(P[6][0:96, :], tmpf[0:96, 0, :], 0, 96, ["P6"], ["t0"])
                S.stt(Qh[0:64, tsl(tb)], P[4][0:64, :], vecs[0:64, 21:22], tmpf[0:64, 0, :], ALU.mult, ALU.mult,
                      R=["P4", "t0", "vecs"], W=[("Qh", tb)])
                S.stt(tmpf[64:96, 1, :], P[4][64:96, :], vecs[64:96, 21:22], tmpf[64:96, 0, :], ALU.mult, ALU.mult,
                      R=["P4", "t0", "vecs"], W=["t1"])
                S.stt(tmpf[64:96, 2, :], P[5][64:96, :], vecs[64:96, 22:23], tmpf[64:96, 0, :], ALU.mult, ALU.mult,
                      R=["P5", "t0", "vecs"], W=["t2"])
                S.tt(tmpf[64:96, 1, :], tmpf[64:96, 1, :], cosT[64:96, tsl(tb)], ALU.mult, R=["t1", ("rope", tb)], W=["t1"])
                S.tt(tmpf[64:96, 2, :], tmpf[64:96, 2, :], sinT[64:96, tsl(tb)], ALU.mult, R=["t2", ("rope", tb)], W=["t2"])
                S.tt(Qh[64:96, tsl(tb)], tmpf[64:96, 1, :], tmpf[64:96, 2, :], ALU.add, R=["t1", "t2"], W=[("Qhr", tb)])
            its = [(qb, kb2) for qb in range(4) for kb2 in range(64)]

            def mla_S(n):
                qb, kb2 = its[n]
                si = n % 2
                S.mm(P[si][:, :], Kh[0:96, kb2 * 128:(kb2 + 1) * 128], Qh[0:96, tsl(qb)],
                     R=[("Kh", kb2 // 4), "Khr", ("Qh", qb), ("Qhr", qb)], W=["P%d" % si])

            mla_S(0)
            for n, (qb, kb2) in enumerate(its):
                if n + 1 < len(its):
                    mla_S(n + 1)
                si = n % 2
                Po = P[2 + (qb % 2)]
                potok = "P%d" % (2 + (qb % 2))
                pt = arena[:, PTC + (n % 4) * 512:PTC + (n % 4) * 512 + 512]
                S.act(pt, P[si][:, :], AF.Exp, R=["P%d" % si], W=[("ptc", n % 4)], scale=scale_mla)
                S.mm(Po[0:65, :], Vh[:, kb2, 0:65], pt, start=(kb2 == 0), stop=(kb2 == 63),
                     R=[("ptc", n % 4), ("Vh", kb2 // 8), "vh1"], W=[potok])
                if kb2 == 63:
                    norm_out(Po, potok, 4 + pp, hh, qb)

        S.barrier()

        WO = 28672
        wo = arena[:, WO:WO + 8192].rearrange("p (k n) -> p k n", n=1024)
        wov = LW(wout_d, l).rearrange("(k p) n -> p k n", p=128)
        S.dma("pool", wo[:, :, 0:512], wov[:, :, 0:512], W=["wo"])
        S.dma("pool", wo[:, :, 512:1024], wov[:, :, 512:1024], W=["wo"])
        for tb in range(4):
            for dc in range(8):
                bi = dc % 4
                for ch in range(8):
                    S.mm(P[bi][:, :], wo[:, ch, dc * 128:(dc + 1) * 128], oT[:, ch, tsl(tb)], start=(ch == 0), stop=(ch == 7),
                         R=["wo", ("o", ch, tb, 0), ("o", ch, tb, 1)], W=["P%d" % bi])
                S.stt(xT[:, dc, tsl(tb)], P[bi][:, :], modT[:, 16 + dc:17 + dc], xT[:, dc, tsl(tb)], ALU.mult, ALU.add,
                      R=["P%d" % bi, "modT", ("x", dc, tb)], W=[("x", dc, tb)])

        S.barrier()

        hT = arena[:, 0:16384].rearrange("p (k t) -> p k t", t=T)
        sq = arena[:, 40960:40960 + 4096].rearrange("p (k t) -> p k t", t=512)
        w1v = LW(w1_d, l).rearrange("(k p) n -> p k n", p=128)
        w2v = LW(w2_d, l).rearrange("(c p) n -> p c n", p=128)

        def wgrp(g):
            o1 = 16384 + (g % 2) * 8192
            W1g = arena[:, o1:o1 + 4096].rearrange("p (k n) -> p k n", n=512)
            W2g = arena[:, o1 + 4096:o1 + 8192].rearrange("p (c n) -> p c n", n=1024)
            return W1g, W2g

        def load_w(g):
            W1g, W2g = wgrp(g)
            S.dma("pool", W1g, w1v[:, :, g * 512:(g + 1) * 512], W=[("w1", g % 2)])
            S.dma("pool", W2g, w2v[:, g * 4:(g + 1) * 4, :], W=[("w2", g % 2)])

        load_w(0)
        load_w(1)
        for tb in range(4):
            rmsnorm_block(tb, 8, 24, lambda c: hT[:, c, tsl(tb)], sq, "sq")
        mits = [(g, tb) for g in range(8) for tb in range(4)]

        def aT_of(tb):
            return arena[:, 32768 + (tb % 2) * 2048:32768 + (tb % 2) * 2048 + 2048].rearrange("p (c t) -> p c t", t=512)

        def mlp_u(n):
            g, tb = mits[n]
            W1g, W2g = wgrp(g)
            aT = aT_of(tb)
            for c in range(4):
                bi = c % 3
                for k in range(8):
                    S.mm(P[bi][:, :], W1g[:, k, c * 128:(c + 1) * 128], hT[:, k, tsl(tb)], start=(k == 0), stop=(k == 7),
                         R=[("w1", g % 2), ("h", k, tb)], W=["P%d" % bi])
                S.act(tmpf[:, c, :], P[bi][:, :], AF.Relu, R=["P%d" % bi], W=["t%d" % c])
                S.tt(aT[:, c, :], tmpf[:, c, :], tmpf[:, c, :], ALU.mult, R=["t%d" % c], W=[("a", tb % 2, c)], eng="pool")

        def mlp_y(n):
            g, tb = mits[n]
            W1g, W2g = wgrp(g)
            aT = aT_of(tb)
            for dc in range(8):
                bi = 3 + (dc % 4)
                for c in range(4):
                    S.mm(P[bi][:, :], W2g[:, c, dc * 128:(dc + 1) * 128], aT[:, c, :], start=(c == 0), stop=(c == 3),
                         R=[("w2", g % 2), ("a", tb % 2, c)], W=["P%d" % bi])
                S.stt(xT[:, dc, tsl(tb)], P[bi][:, :], modT[:, 40 + dc:41 + dc], xT[:, dc, tsl(tb)], ALU.mult, ALU.add,
                      R=["P%d" % bi, "modT", ("x", dc, tb)], W=[("x", dc, tb)])

        mlp_u(0)
        for n, (g, tb) in enumerate(mits):
            if n + 1 < len(mits):
                mlp_u(n + 1)
            mlp_y(n)
            if tb == 3 and g + 2 < 8:
                load_w(g + 2)
        S.barrier()

    yv = yT_d.rearrange("(c p) t -> p c t", p=128)
    for tb in range(4):
        S.dma("sp", yv[:, :, tsl(tb)], xT[:, :, tsl(tb)], R=[("x", c, tb) for c in range(8)], W=[("y", tb)])
    S.finish([("y", tb) for tb in range(4)])
    S.emit(st)
    st.close()
    return nc


_NC = {}


def _get_nc(mode):
    if mode not in _NC:
        _NC[mode] = build(mode)
    return _NC[mode]


def _consts():
    cb = np.zeros((128, NCB), np.float32)
    cb[:, 0:128] = 1.0 / 1024
    cb[0:64, 128:192] = 1.0 / 64
    cb[64:128, 192:256] = 1.0 / 64
    cb[:, 256:384] = 1.0 / 256
    cb[:, 384:512] = 1.0 / 128
    cb[0:64, 512:576] = 1.0 / 64
    cb[64:96, 576:608] = 1.0 / 32
    cb[0:32, 608:640] = 1.0 / 32
    cb[0:64, 640:704] = np.eye(64)
    cb[64:128, 640:704] = np.eye(64)
    for i in range(64):
        cb[i, 704 + 64 + i] = 1.0
    cf = np.zeros((128, NCF), np.float32)
    p = np.arange(128)
    cf[:, 0] = (10000.0 ** (-((p % 32) % 16).astype(np.float64) / 16.0)).astype(np.float32)
    cf[:, 1] = np.where((p % 32) < 16, -1.0, 1.0)
    cf[:, 2] = 1e-6
    cf[:, 3:67] = 1.0
    return cb, cf


def _mtab():
    slopes = np.array([2.0 ** (-8.0 * (h + 1) / 8) for h in range(8)], np.float64)
    k = np.arange(128)[:, None]
    c = np.arange(256)[None, :]
    dist = np.abs(c - 64 - k).astype(np.float64)
    m = np.zeros((8, 128, 3, 256), np.float64)
    for h in range(8):
        for cfg, d in enumerate((1, 4, 16)):
            m[h, :, cfg, :] = np.where(dist <= 64, np.exp(-slopes[h] * d * dist), 0.0)
    return m.reshape(8, 128, 768).astype(np.float32)


def _valid(t0):
    v = np.zeros((128, NTILE), np.float32)
    p = np.arange(128)
    for idx, (cfg, d, r, a, c0, c1, Q0) in enumerate(TILES):
        tok = (a + p) * d + r + t0 - 1024
        v[:, idx] = ((tok >= 0) & (tok < SEQ)).astype(np.float32)
    return v


def _layer_common(l, w_mod, b_mod, g_norm_mix, w_in, g_q_dil, g_k_dil, g_cq, g_ckv, g_q_nope, g_q_rope,
                  g_k_nope, g_k_rope, g_norm_mlp):
    vecs = np.zeros((128, 32), np.float32)
    vecs[:, 0:8] = g_norm_mix[l].reshape(8, 128).T
    vecs[:, 8:16] = g_norm_mlp[l].reshape(8, 128).T
    vecs[:, 16] = np.tile(g_q_dil[l], 2)
    vecs[:, 17] = np.tile(g_k_dil[l], 2)
    vecs[:, 18:20] = g_cq[l].reshape(2, 128).T
    vecs[:, 20] = g_ckv[l]
    vecs[0:64, 21] = g_q_nope[l]
    vecs[64:96, 21] = g_q_rope[l]
    vecs[0:64, 22] = g_q_nope[l]
    vecs[64:96, 22] = np.roll(g_q_rope[l], -16)
    vecs[0:64, 23] = g_k_nope[l]
    vecs[0:32, 24] = g_k_rope[l]
    vecs[0:32, 25] = np.roll(g_k_rope[l], -16)
    perm = (np.arange(32) + 16) % 32
    d = dict(
        w_mod=np.ascontiguousarray(w_mod[l]),
        bmodT=np.ascontiguousarray(b_mod[l].reshape(48, 128).T),
        w_in=np.ascontiguousarray(w_in[l]),
        w_in_sw=np.ascontiguousarray(w_in[l][:, 1920 + perm]),
        vecs=vecs,
    )
    return d


def kernel(x, c, positions, w_mod, b_mod, g_norm_mix, w_in, g_q_dil, g_k_dil, g_cq, w_q_up,
           g_ckv, w_kv_up, g_q_nope, g_q_rope, g_k_nope, g_k_rope, w_out, g_norm_mlp,
           w_mlp_in, w_mlp_out):
    A = lambda a: np.asarray(a)
    x, c, positions = A(x), A(c), A(positions)
    (w_mod, b_mod, g_norm_mix, w_in, g_q_dil, g_k_dil, g_cq, w_q_up, g_ckv, w_kv_up, g_q_nope, g_q_rope,
     g_k_nope, g_k_rope, w_out, g_norm_mlp, w_mlp_in, w_mlp_out) = [A(a).astype(np.float32, copy=False) for a in (
        w_mod, b_mod, g_norm_mix, w_in, g_q_dil, g_k_dil, g_cq, w_q_up, g_ckv, w_kv_up, g_q_nope, g_q_rope,
        g_k_nope, g_k_rope, w_out, g_norm_mlp, w_mlp_in, w_mlp_out)]
    cb, cf = _consts()
    mtab = _mtab()
    cores = list(range(8))
    coms = [_layer_common(l, w_mod, b_mod, g_norm_mix, w_in, g_q_dil, g_k_dil, g_cq, g_ckv, g_q_nope, g_q_rope,
                          g_k_nope, g_k_rope, g_norm_mlp) for l in range(2)]
    permq = np.arange(768).reshape(8, 96)
    permq = np.concatenate([permq[:, :64], permq[:, 64 + (np.arange(32) + 16) % 32]], axis=1).reshape(-1)
    shared = dict(
        w_mod=np.ascontiguousarray(w_mod), w_in=np.ascontiguousarray(w_in),
        bmodT=np.stack([cm["bmodT"] for cm in coms]), w_in_sw=np.stack([cm["w_in_sw"] for cm in coms]),
        vecs=np.stack([cm["vecs"] for cm in coms]),
        w_q=np.ascontiguousarray(w_q_up), w_q_sw=np.ascontiguousarray(w_q_up[:, :, permq]),
        w_kv=np.ascontiguousarray(w_kv_up), w_out=np.ascontiguousarray(w_out),
        w1=np.ascontiguousarray(w_mlp_in), w2=np.ascontiguousarray(w_mlp_out),
        mtab=mtab, cstb=cb, cstf=cf)
    ins = []
    for i in cores:
        b, g = i // 4, i % 4
        t0 = g * T
        wsel = np.zeros((128, 8), np.float32)
        if g - 1 >= 0:
            wsel[:, g - 1] = 1.0
        if g + 1 <= 3:
            wsel[:, 4 + g + 1] = 1.0
        ins.append(dict(
            xT=np.ascontiguousarray(x[b, t0:t0 + T, :].T),
            cT=np.ascontiguousarray(c[b].reshape(8, 128).T.astype(np.float32)),
            pos=np.ascontiguousarray(np.broadcast_to(positions[b, t0:t0 + T].astype(np.int32)[None, :], (96, T))),
            valid=_valid(t0), wsel=wsel, **shared))
    res = run_bass_kernel_spmd(_get_nc("F"), ins, core_ids=cores).results
    out = np.empty((2, SEQ, 1024), np.float32)
    for i in cores:
        out[i // 4, (i % 4) * T:(i % 4 + 1) * T, :] = res[i]["yT"].T
    return out


def kernel_unfused(x, c, positions, w_mod, b_mod, g_norm_mix, w_in, g_q_dil, g_k_dil, g_cq, w_q_up,
           g_ckv, w_kv_up, g_q_nope, g_q_rope, g_k_nope, g_k_rope, w_out, g_norm_mlp,
           w_mlp_in, w_mlp_out, _nlayers=2):
    A = lambda a: np.asarray(a)
    x, c, positions = A(x), A(c), A(positions)
    (w_mod, b_mod, g_norm_mix, w_in, g_q_dil, g_k_dil, g_cq, w_q_up, g_ckv, w_kv_up, g_q_nope, g_q_rope,
     g_k_nope, g_k_rope, w_out, g_norm_mlp, w_mlp_in, w_mlp_out) = [A(a).astype(np.float32, copy=False) for a in (
        w_mod, b_mod, g_norm_mix, w_in, g_q_dil, g_k_dil, g_cq, w_q_up, g_ckv, w_kv_up, g_q_nope, g_q_rope,
        g_k_nope, g_k_rope, w_out, g_norm_mlp, w_mlp_in, w_mlp_out)]
    cb, cf = _consts()
    mtab = _mtab()
    cores = list(range(8))
    xT = [np.ascontiguousarray(x[i // 4, (i % 4) * T:(i % 4 + 1) * T, :].T) for i in cores]
    base = []
    for i in cores:
        b, t0 = i // 4, (i % 4) * T
        base.append(dict(
            cT=np.ascontiguousarray(c[b].reshape(8, 128).T.astype(np.float32)),
            pos=np.ascontiguousarray(np.broadcast_to(positions[b, t0:t0 + T].astype(np.int32)[None, :], (96, T))),
            cstb=cb, cstf=cf))
    permq = np.arange(768).reshape(8, 96)
    permq = np.concatenate([permq[:, :64], permq[:, 64 + (np.arange(32) + 16) % 32]], axis=1).reshape(-1)
    for l in range(_nlayers):
        com = _layer_common(l, w_mod, b_mod, g_norm_mix, w_in, g_q_dil, g_k_dil, g_cq, g_ckv, g_q_nope, g_q_rope,
                            g_k_nope, g_k_rope, g_norm_mlp)
        in_a = [dict(xT=xT[i], **base[i], **com) for i in cores]
        ra = run_bass_kernel_spmd(_get_nc("A"), in_a, core_ids=cores).results
        lay = dict(
            w_q=np.ascontiguousarray(w_q_up[l]), w_q_sw=np.ascontiguousarray(w_q_up[l][:, permq]),
            w_kv=np.ascontiguousarray(w_kv_up[l]), w_out=np.ascontiguousarray(w_out[l]),
            w1=np.ascontiguousarray(w_mlp_in[l]), w2=np.ascontiguousarray(w_mlp_out[l]), mtab=mtab)
        in_l = []
        for i in cores:
            b, t0 = i // 4, (i % 4) * T
            grp = [ra[4 * b + j] for j in range(4)]
            lat = np.concatenate([g["o_lat"] for g in grp], axis=1)
            kd = np.concatenate([g["o_kd"] for g in grp], axis=1)
            vd = np.concatenate([g["o_vd"] for g in grp], axis=1)
            zpad = np.zeros((512, 1024), kd.dtype)
            kdp = np.concatenate([zpad, kd, zpad], axis=1)[:, t0:t0 + 4096]
            vdp = np.concatenate([zpad, vd, zpad], axis=1)[:, t0:t0 + 4096]
            in_l.append(dict(xT=xT[i], **base[i], **com, **lay, i_lat=np.ascontiguousarray(lat),
                             i_kd=np.ascontiguousarray(kdp), i_vd=np.ascontiguousarray(vdp), valid=_valid(t0)))
        rl = run_bass_kernel_spmd(_get_nc("L"), in_l, core_ids=cores).results
        xT = [np.ascontiguousarray(rl[i]["yT"]) for i in cores]
    out = np.empty((2, SEQ, 1024), np.float32)
    for i in cores:
        out[i // 4, (i % 4) * T:(i % 4 + 1) * T, :] = xT[i].T
    return out
```

```python
import numpy as np
import ml_dtypes
import concourse.bass as bass
import concourse.mybir as mybir
from concourse.bass_utils import run_bass_kernel_spmd
from contextlib import ExitStack

F32 = mybir.dt.float32
BF16 = mybir.dt.bfloat16
I32 = mybir.dt.int32
AF = mybir.ActivationFunctionType
ALU = mybir.AluOpType

NSLOT = 8
SAME_WIN = 3
T = 2048
SEQ = 8192
NCB = 832
NCF = 67
ARENA = 54272
PI = float(np.pi)


class _Op:
    __slots__ = ("eng", "emit", "stream", "idx", "waits", "ms", "clock", "pos")


class Sched:
    def __init__(self, nc):
        self.nc = nc
        self.eng_ops = {e: [] for e in ("pe", "act", "dve", "pool", "sp")}
        self.stream_ops = {}
        self.know = {e: {} for e in self.eng_ops}
        self.last_w = {}
        self.readers = {}
        self.dma_cnt = {"sp": 0, "pool": 0, "act": 0}

    def _add(self, eng, emit, R, W, dma=False, cstream=None):
        op = _Op()
        op.eng = eng
        op.emit = emit
        op.ms = False
        if dma:
            slot = self.dma_cnt[eng] % NSLOT
            self.dma_cnt[eng] += 1
            op.stream = ("dma", eng, slot)
        elif cstream is not None:
            op.stream = cstream
        else:
            op.stream = eng
        sl = self.stream_ops.setdefault(op.stream, [])
        op.idx = len(sl) + 1
        deps = []
        if dma and sl:
            deps.append(sl[-1])
        sl.append(op)
        for t in R:
            w = self.last_w.get(t)
            if w is not None:
                deps.append(w)
        for t in W:
            w = self.last_w.get(t)
            if w is not None:
                deps.append(w)
            deps.extend(self.readers.get(t, ()))
        eo = self.eng_ops[eng]
        op.pos = len(eo)
        K = self.know[eng]
        best = {}
        for d in deps:
            if d is op:
                continue
            b = best.get(d.stream)
            if b is None or b.idx < d.idx:
                best[d.stream] = d
        waits = []
        for s, d in best.items():
            if K.get(s, 0) >= d.idx:
                continue
            if s == eng:
                if eng == "pe" or eng == "sp" or (op.pos - d.pos) > SAME_WIN:
                    continue
                d.ms = True
                waits.append(d)
                K[s] = d.idx
                continue
            d.ms = True
            waits.append(d)
            for s2, i2 in d.clock.items():
                if K.get(s2, 0) < i2:
                    K[s2] = i2
        op.waits = waits
        op.clock = dict(K)
        op.clock[op.stream] = op.idx
        if dma or cstream is not None:
            op.ms = True
        eo.append(op)
        for t in R:
            self.readers.setdefault(t, []).append(op)
        for t in W:
            self.last_w[t] = op
            self.readers[t] = []
        return op

    def mm(self, out, lhsT, rhs, start=True, stop=True, R=(), W=(), **kw):
        return self._add("pe", lambda e: e.matmul(out, lhsT, rhs, start=start, stop=stop, **kw), R, W)

    def transpose(self, out, in_, ident, R=(), W=()):
        return self._add("pe", lambda e: e.transpose(out, in_, ident), R, W)

    def act(self, out, in_, func, R=(), W=(), **kw):
        return self._add("act", lambda e: e.activation(out=out, in_=in_, func=func, **kw), R, W)

    def tt(self, out, in0, in1, op, R=(), W=(), eng="dve"):
        return self._add(eng, lambda e: e.tensor_tensor(out, in0, in1, op), R, W)

    def ts(self, out, in0, s1, s2, op0, op1=None, R=(), W=(), eng="dve"):
        if op1 is None:
            return self._add(eng, lambda e: e.tensor_scalar(out, in0, s1, None, op0), R, W)
        return self._add(eng, lambda e: e.tensor_scalar(out, in0, s1, s2, op0, op1), R, W)

    def stt(self, out, in0, scalar, in1, op0, op1, R=(), W=(), eng="dve"):
        return self._add(eng, lambda e: e.scalar_tensor_tensor(out, in0, scalar, in1, op0, op1), R, W)

    def copy(self, out, in_, R=(), W=(), eng="dve"):
        if eng == "act":
            return self._add("act", lambda e: e.copy(out, in_), R, W)
        return self._add(eng, lambda e: e.tensor_copy(out, in_), R, W)

    def recip(self, out, in_, R=(), W=()):
        return self._add("dve", lambda e: e.reciprocal(out, in_), R, W)

    def memset(self, ap, val, W=(), eng="pool"):
        return self._add(eng, lambda e: e.memset(ap, val), (), W)

    def dma(self, q, out, in_, R=(), W=()):
        return self._add(q, lambda e: e.dma_start(out=out, in_=in_), R, W, dma=True)

    def barrier(self):
        lasts = [ops[-1] for ops in self.stream_ops.values() if ops]
        for e in ("pe", "act", "dve", "pool", "sp"):
            op = _Op()
            op.eng = e
            op.emit = None
            op.ms = False
            op.stream = None
            op.idx = 0
            eo = self.eng_ops[e]
            op.pos = len(eo)
            K = self.know[e]
            waits = []
            for d in lasts:
                if d.stream == e:
                    if e != "pe" and K.get(e, 0) < d.idx:
                        d.ms = True
                        waits.append(d)
                        K[e] = d.idx
                    continue
                if K.get(d.stream, 0) >= d.idx:
                    continue
                d.ms = True
                waits.append(d)
                for s2, i2 in d.clock.items():
                    if K.get(s2, 0) < i2:
                        K[s2] = i2
            op.waits = waits
            op.clock = dict(K)
            eo.append(op)

    def finish(self, R):
        return self._add("sp", None, R, ())

    def emit(self, stack):
        nc = self.nc
        sems = {}
        order = sorted(self.stream_ops, key=lambda s: 0 if (isinstance(s, tuple) and s[0] == "cc") else 1)
        for s in order:
            nm = "s_" + ("_".join(str(x) for x in s) if isinstance(s, tuple) else s)
            sems[s] = stack.enter_context(nc.semaphore(nm))
        val = {}
        for s, ops in self.stream_ops.items():
            c = 0
            inc = 16 if (isinstance(s, tuple) and s[0] == "dma") else 1
            for o in ops:
                if o.ms:
                    c += inc
                val[id(o)] = c
        block = stack.enter_context(nc.Block())

        def run(eng_name, e):
            for o in self.eng_ops[eng_name]:
                for d in o.waits:
                    e.wait_ge(sems[d.stream], val[id(d)])
                if o.emit is None:
                    continue
                ins = o.emit(e)
                if o.ms and ins is not None:
                    ins.then_inc(sems[o.stream], 16 if (isinstance(o.stream, tuple) and o.stream[0] == "dma") else 1)

        @block.tensor
        def _(e):
            run("pe", e)

        @block.scalar
        def _(e):
            run("act", e)

        @block.vector
        def _(e):
            run("dve", e)

        @block.gpsimd
        def _(e):
            run("pool", e)

        @block.sync
        def _(e):
            run("sp", e)


def dil_tiles():
    tiles = []
    for cfg, d in enumerate((1, 4, 16)):
        Q0 = 1024 // d
        nq = 2048 // d
        nt = nq // 128 + 1
        for r in range(d):
            for i in range(nt):
                a = Q0 - 64 + 128 * i
                c0 = 128 if i == 0 else 0
                c1 = 128 if i == nt - 1 else 256
                tiles.append((cfg, d, r, a, c0, c1, Q0))
    return tiles


TILES = dil_tiles()
NTILE = len(TILES)


def build(mode):
    nc = bass.Bass("TRN2", target_bir_lowering=False)
    D = lambda n, sh, dt, kind="ExternalInput": nc.dram_tensor(n, sh, dt, kind=kind).ap()
    xT_d = D("xT", [1024, T], F32)
    cT_d = D("cT", [128, 8], F32)
    pos_d = D("pos", [96, T], I32)
    NL = 2 if mode == "F" else 1
    LD = (lambda n, sh, dt: D(n, [2] + sh, dt)) if mode == "F" else (lambda n, sh, dt: D(n, sh, dt))
    wmod_d = LD("w_mod", [1024, 6144], F32)
    bmodT_d = LD("bmodT", [128, 48], F32)
    win_d = LD("w_in", [1024, 1952], F32)
    winsw_d = LD("w_in_sw", [1024, 32], F32)
    vecs_d = LD("vecs", [128, 32], F32)
    cstb_d = D("cstb", [128, NCB], F32)
    cstf_d = D("cstf", [128, NCF], F32)
    LW = (lambda ap, l: ap[l]) if mode == "F" else (lambda ap, l: ap)
    if mode == "A":
        okd_d = D("o_kd", [512, T], BF16, "ExternalOutput")
        ovd_d = D("o_vd", [512, T], BF16, "ExternalOutput")
        olat_d = D("o_lat", [160, T], BF16, "ExternalOutput")
    else:
        wq_d = LD("w_q", [256, 768], F32)
        wqsw_d = LD("w_q_sw", [256, 768], F32)
        wkv_d = LD("w_kv", [128, 1024], F32)
        wout_d = LD("w_out", [1024, 1024], F32)
        w1_d = LD("w1", [1024, 4096], F32)
        w2_d = LD("w2", [4096, 1024], F32)
        mtab_d = D("mtab", [8, 128, 768], F32)
        valid_d = D("valid", [128, NTILE], F32)
        yT_d = D("yT", [1024, T], F32, "ExternalOutput")
    if mode == "L":
        ikd_d = D("i_kd", [512, 4096], BF16)
        ivd_d = D("i_vd", [512, 4096], BF16)
        ilat_d = D("i_lat", [160, SEQ], BF16)
    if mode == "F":
        wsel_d = D("wsel", [128, 8], F32)
        okd_h = [nc.dram_tensor("b_kd%d" % i, [256, T], BF16).ap() for i in range(2)]
        ovd_h = [nc.dram_tensor("b_vd%d" % i, [256, T], BF16).ap() for i in range(2)]
        olat_d = nc.dram_tensor("b_lat", [160, T], BF16).ap()
        gkd_h = [nc.dram_tensor("g_kd%d" % i, [4 * 256, T], BF16).ap() for i in range(2)]
        gvd_h = [nc.dram_tensor("g_vd%d" % i, [4 * 256, T], BF16).ap() for i in range(2)]
        glat_d = nc.dram_tensor("g_lat", [4 * 160, T], BF16).ap()

    st = ExitStack()
    sb = lambda n, sh, dt: st.enter_context(nc.sbuf_tensor(n, sh, dt))
    xT = sb("xTs", [128, 8, T], F32)
    cosT = sb("cosT", [96, T], F32)
    sinT = sb("sinT", [96, T], F32)
    tmpf = sb("tmpf", [128, 4, 512], F32)
    tmpi = sb("tmpi", [96, 512], I32)
    posi = sb("posi", [96, T], I32) if False else None
    cstb = sb("cstbs", [128, NCB], BF16)
    cstf = sb("cstfs", [128, NCF], F32)
    vecs = sb("vecss", [128, 32], F32)
    cTs = sb("cTs", [128, 8], F32)
    cact = sb("cact", [128, 8], BF16)
    bmodT = sb("bmodTs", [128, 48], F32)
    modT = sb("modT", [128, 48], F32)
    gm = sb("gm", [128, 16], F32)
    modrow = sb("modrow", [1, 2, 512], F32)
    arena = sb("arena", [128, ARENA], BF16)
    P = [st.enter_context(nc.psum_tensor("ps%d" % i, [128, 512], F32)) for i in range(7)]
    psb = st.enter_context(nc.psum_tensor("psb", [128, 1024], BF16))
    if mode != "A":
        mtb = sb("mtb", [128, 768], F32)
        vld = sb("vld", [128, NTILE], F32)
    if mode == "F":
        wsel = sb("wsels", [128, 8], F32)

    S = Sched(nc)

    def ar(off, n, shape=None):
        a = arena[:, off:off + n]
        return a

    ones1024 = cstb[:, 0:128]
    blk64 = cstb[:, 128:256]
    ones256 = cstb[:, 256:384]
    ones128 = cstb[:, 384:512]
    blkq96 = cstb[0:96, 512:608]
    ones32 = cstb[0:32, 608:640]
    shift64 = cstb[0:64, 704:832]
    eps_c = lambda lo, hi: cstf[lo:hi, 2:3]

    def tsl(tb):
        return slice(tb * 512, (tb + 1) * 512)

    S.dma("pool", cstb[:], cstb_d[:, :], W=["cstb"])
    S.dma("sp", cstf[:], cstf_d[:, :], W=["cstf"])
    S.dma("sp", cTs[:], cT_d[:, :], W=["cT"])
    xv = xT_d.rearrange("(c p) t -> p c t", p=128)
    for tb in range(4):
        S.dma("sp", xT[:, :, tsl(tb)], xv[:, :, tsl(tb)], W=[("x", c, tb) for c in range(8)])
    if mode != "A":
        S.dma("sp", vld[:], valid_d[:, :], W=["vld"])
    if mode == "F":
        S.dma("sp", wsel[:], wsel_d[:, :], W=["wsel"])

    S.act(cact[:], cTs[:], AF.Silu, R=["cT"], W=["cact"])
    for tb in range(4):
        S.dma("sp", tmpi[:, :], pos_d[:, tsl(tb)], W=["tmpi"])
        t0 = tmpf[0:96, 0, :]
        t1 = tmpf[0:96, 1, :]
        t2 = tmpf[0:96, 2, :]
        t3 = tmpf[0:96, 3, :]
        S.copy(t0, tmpi[:, :], R=["tmpi"], W=["t0"])
        S.ts(t0, t0, cstf[0:96, 0:1], None, ALU.mult, R=["t0", "cstf"], W=["t0"])
        for tab, shift, use_sign in ((sinT, 0.0, True), (cosT, PI / 2, False)):
            S.ts(t1, t0, shift, None, ALU.add, R=["t0"], W=["t1"])
            S.ts(t2, t1, 1.0 / (2 * PI), None, ALU.mult, R=["t1"], W=["t2"])
            S.copy(tmpi[:, :], t2, R=["t2"], W=["tmpi"])
            S.copy(t2, tmpi[:, :], R=["tmpi"], W=["t2"])
            S.stt(t3, t2, -2 * PI, t1, ALU.mult, ALU.add, R=["t2", "t1"], W=["t3"])
            S.ts(t2, t3, PI, -2 * PI, ALU.is_gt, ALU.mult, R=["t3"], W=["t2"])
            S.tt(t3, t3, t2, ALU.add, R=["t3", "t2"], W=["t3"])
            S.ts(t2, t3, -PI, 2 * PI, ALU.is_lt, ALU.mult, R=["t3"], W=["t2"])
            S.tt(t3, t3, t2, ALU.add, R=["t3", "t2"], W=["t3"])
            S.ts(t3, t3, PI, -PI, ALU.min, ALU.max, R=["t3"], W=["t3"])
            if use_sign:
                S.act(tab[:, tsl(tb)], t3, AF.Sin, R=["t3", "cstf"], W=[("rope", tb)], scale=cstf[0:96, 1:2])
            else:
                S.act(tab[:, tsl(tb)], t3, AF.Sin, R=["t3"], W=[("rope", tb)])

    for l in range(NL):
        S.dma("sp", vecs[:], LW(vecs_d, l)[:, :], W=["vecs"])
        S.dma("sp", bmodT[:], LW(bmodT_d, l)[:, :], W=["bmodT"])
        wmv = LW(wmod_d, l).rearrange("(k p) n -> p k n", p=128)
        ngrp = 4 if mode == "A" else 12
        for g in range(ngrp):
            wm = arena[:, 45056 + (g % 2) * 4096:45056 + (g % 2) * 4096 + 4096].rearrange("p (k n) -> p k n", n=512)
            S.dma("pool", wm, wmv[:, :, g * 512:(g + 1) * 512], W=[("wm", g % 2)])
            for k in range(8):
                S.mm(P[0][0:1, :], cact[:, k:k + 1], wm[:, k, :], start=(k == 0), stop=(k == 7),
                     R=["cact", ("wm", g % 2)], W=["P0"])
            S.copy(modrow[0:1, g % 2, :], P[0][0:1, :], R=["P0"], W=[("mr", g % 2)])
            for jj in range(4):
                j = g * 4 + jj
                S.mm(P[1][:, j:j + 1], modrow[0:1, g % 2, jj * 128:(jj + 1) * 128], cstf[0:1, 3:4],
                     start=True, stop=True, R=[("mr", g % 2), "cstf"], W=["P1"], skip_group_check=True)
        S.tt(modT[:, 0:ngrp * 4], P[1][:, 0:ngrp * 4], bmodT[:, 0:ngrp * 4], ALU.add, R=["P1", "bmodT"], W=["modT"])
        S.stt(gm[:, 0:8], modT[:, 8:16], 1.0, vecs[:, 0:8], ALU.add, ALU.mult, R=["modT", "vecs"], W=["gm"])
        if mode != "A":
            S.stt(gm[:, 8:16], modT[:, 32:40], 1.0, vecs[:, 8:16], ALU.add, ALU.mult, R=["modT", "vecs"], W=["gm"])

        QD, CQ, OT = 0, 8192, 12288
        qdT = arena[:, QD:QD + 8192].rearrange("p (c t) -> p c t", t=T)
        cqnT = arena[:, CQ:CQ + 4096].rearrange("p (c t) -> p c t", t=T)
        oT = arena[:, OT:OT + 16384].rearrange("p (c t) -> p c t", t=T)

        def rstd_from(Pn_ap, out_ap, lo, hi, R, W):
            S.act(out_ap, Pn_ap, AF.Ln, R=R + ["cstf"], W=W, bias=eps_c(lo, hi), scale=1.0)
            S.act(out_ap, out_ap, AF.Exp, R=W, W=W, scale=-0.5)

        def rmsnorm_block(tb, gcol, shcol, hout, sq, sqtok):
            S.act(sq, xT[:, :, tsl(tb)], AF.Square, R=[("x", c, tb) for c in range(8)], W=[sqtok])
            for c in range(8):
                S.mm(P[6][:, :], ones1024, sq[:, c, :], start=(c == 0), stop=(c == 7), R=[sqtok, "cstb"], W=["P6"])
            rstd_from(P[6][:, :], tmpf[:, 0, :], 0, 128, ["P6"], ["t0"])
            for c in range(8):
                sl = 1 + (c % 3)
                S.stt(tmpf[:, sl, :], xT[:, c, tsl(tb)], gm[:, gcol + c:gcol + c + 1], tmpf[:, 0, :], ALU.mult, ALU.mult,
                      R=[("x", c, tb), "t0", "gm"], W=["t%d" % sl])
                S.act(hout(c), tmpf[:, sl, :], AF.Identity, R=["t%d" % sl, "modT"], W=[("h", c, tb)],
                      bias=modT[:, shcol + c:shcol + c + 1], scale=1.0)

        WIN = 12288
        winT = arena[:, WIN:WIN + 15616].rearrange("p (k n) -> p k n", n=1952)
        HB = WIN + 15616
        SQ = HB + 8192
        STG = SQ + 4096
        wsw = arena[:, STG + 4096:STG + 4096 + 256].rearrange("p (k n) -> p k n", n=32)
        wiv = LW(win_d, l).rearrange("(k p) n -> p k n", p=128)
        S.dma("pool", winT[:, :, 0:976], wiv[:, :, 0:976], W=["win"])
        S.dma("pool", winT[:, :, 976:1952], wiv[:, :, 976:1952], W=["win"])
        S.dma("pool", wsw, LW(winsw_d, l).rearrange("(k p) n -> p k n", p=128), W=["wsw"])
        pbank = [0]

        def nextP():
            pbank[0] = (pbank[0] + 1) % 4
            return pbank[0], P[pbank[0]]

        stg_i = [0]

        def stage():
            stg_i[0] = (stg_i[0] + 1) % 4
            i = stg_i[0]
            return ("stg", i), arena[:, STG + i * 512:STG + (i + 1) * 512]

        def proj(tb, hbuf, col0, M, wt=None, wtok="win"):
            bi, Pp = nextP()
            for k in range(8):
                lhsT = winT[:, k, col0:col0 + M] if wt is None else wt[:, k, 0:M]
                S.mm(Pp[0:M, :], lhsT, hbuf[:, k, :], start=(k == 0), stop=(k == 7),
                     R=[wtok] + [("h", k, tb)], W=["P%d" % bi])
            return "P%d" % bi, Pp

        for tb in range(4):
            hb_off = HB + (tb % 2) * 4096
            hbuf = arena[:, hb_off:hb_off + 4096].rearrange("p (k t) -> p k t", t=512)
            sq = arena[:, SQ:SQ + 4096].rearrange("p (k t) -> p k t", t=512)
            rmsnorm_block(tb, 0, 0, lambda c: hbuf[:, c, :], sq, "sq")
            sqh = arena[:, SQ:SQ + 512]

            def headnorm(ptok, Pp, onesm, gcol, out_ap, Wt, rows=128):
                S.act(sqh[0:rows, :], Pp[0:rows, :], AF.Square, R=[ptok], W=["sq"])
                S.mm(P[5][0:rows, :], onesm, sqh[0:rows, :], R=["sq", "cstb"], W=["P5"])
                rstd_from(P[5][0:rows, :], tmpf[0:rows, 0, :], 0, rows, ["P5"], ["t0"])
                S.stt(out_ap, Pp[0:rows, :], vecs[0:rows, gcol:gcol + 1], tmpf[0:rows, 0, :], ALU.mult, ALU.mult,
                      R=[ptok, "t0", "vecs"], W=Wt)

            if mode != "A":
                for j in range(4):
                    ptok, Pp = proj(tb, hbuf, j * 128, 128)
                    headnorm(ptok, Pp, blk64, 16, qdT[:, j, tsl(tb)], [("qd", j, tb)])
                pa = proj(tb, hbuf, 1536, 128)
                pb2 = proj(tb, hbuf, 1664, 128)
                for i, (ptok, Pp) in enumerate((pa, pb2)):
                    S.act(sq[:, i, :], Pp[:, :], AF.Square, R=[ptok], W=["sq"])
                for i in range(2):
                    S.mm(P[5][:, :], ones256, sq[:, i, :], start=(i == 0), stop=(i == 1), R=["sq", "cstb"], W=["P5"])
                rstd_from(P[5][:, :], tmpf[:, 0, :], 0, 128, ["P5"], ["t0"])
                for i, (ptok, Pp) in enumerate((pa, pb2)):
                    S.stt(cqnT[:, i, tsl(tb)], Pp[:, :], vecs[:, 18 + i:19 + i], tmpf[:, 0, :], ALU.mult, ALU.mult,
                          R=[ptok, "t0", "vecs"], W=[("cq", i, tb)])
            if mode != "L":
                for j in range(4):
                    ptok, Pp = proj(tb, hbuf, 512 + j * 128, 128)
                    stok, stg = stage()
                    headnorm(ptok, Pp, blk64, 17, stg, [stok])
                    S.dma("sp", (okd_d[j * 128:(j + 1) * 128, tsl(tb)] if mode == "A" else okd_h[j // 2][(j % 2) * 128:(j % 2 + 1) * 128, tsl(tb)]), stg, R=[stok], W=[("okd", j, tb)])
                for j in range(4):
                    ptok, Pp = proj(tb, hbuf, 1024 + j * 128, 128)
                    stok, stg = stage()
                    S.copy(stg, Pp[:, :], R=[ptok], W=[stok])
                    S.dma("sp", (ovd_d[j * 128:(j + 1) * 128, tsl(tb)] if mode == "A" else ovd_h[j // 2][(j % 2) * 128:(j % 2 + 1) * 128, tsl(tb)]), stg, R=[stok], W=[("ovd", j, tb)])
                ptok, Pp = proj(tb, hbuf, 1792, 128)
                stok, stg = stage()
                headnorm(ptok, Pp, ones128, 20, stg, [stok])
                S.dma("sp", olat_d[0:128, tsl(tb)], stg, R=[stok], W=[("olat", 0, tb)])
                ptok, Pp = proj(tb, hbuf, 1920, 32)
                ptok2, Pp2 = proj(tb, hbuf, 0, 32, wt=wsw, wtok="wsw")
                S.act(sqh[0:32, :], Pp[0:32, :], AF.Square, R=[ptok], W=["sq"])
                S.mm(P[5][0:32, :], ones32, sqh[0:32, :], R=["sq", "cstb"], W=["P5"])
                rstd_from(P[5][0:32, :], tmpf[0:32, 0, :], 0, 32, ["P5"], ["t0"])
                S.stt(tmpf[0:32, 1, :], Pp[0:32, :], vecs[0:32, 24:25], tmpf[0:32, 0, :], ALU.mult, ALU.mult,
                      R=[ptok, "t0", "vecs"], W=["t1"])
                S.stt(tmpf[0:32, 2, :], Pp2[0:32, :], vecs[0:32, 25:26], tmpf[0:32, 0, :], ALU.mult, ALU.mult,
                      R=[ptok2, "t0", "vecs"], W=["t2"])
                S.tt(tmpf[0:32, 1, :], tmpf[0:32, 1, :], cosT[0:32, tsl(tb)], ALU.mult, R=["t1", ("rope", tb)], W=["t1"])
                S.tt(tmpf[0:32, 2, :], tmpf[0:32, 2, :], sinT[0:32, tsl(tb)], ALU.mult, R=["t2", ("rope", tb)], W=["t2"])
                stok, stg = stage()
                S.tt(stg[0:32, :], tmpf[0:32, 1, :], tmpf[0:32, 2, :], ALU.add, R=["t1", "t2"], W=[stok])
                S.dma("sp", olat_d[128:160, tsl(tb)], stg[0:32, :], R=[stok], W=[("olat", 1, tb)])

        if mode == "A":
            outs = [("okd", j, tb) for j in range(4) for tb in range(4)] + [("ovd", j, tb) for j in range(4) for tb in range(4)] \
                + [("olat", i, tb) for i in range(2) for tb in range(4)]
            S.finish(outs)
            S.emit(st)
            st.close()
            return nc

        if mode == "F":
            for nm, src, dst, toks in (
                ("lat", olat_d, glat_d, [("olat", i, tb) for i in range(2) for tb in range(4)]),
                ("kd0", okd_h[0], gkd_h[0], [("okd", j, tb) for j in (0, 1) for tb in range(4)]),
                ("vd0", ovd_h[0], gvd_h[0], [("ovd", j, tb) for j in (0, 1) for tb in range(4)]),
                ("kd1", okd_h[1], gkd_h[1], [("okd", j, tb) for j in (2, 3) for tb in range(4)]),
                ("vd1", ovd_h[1], gvd_h[1], [("ovd", j, tb) for j in (2, 3) for tb in range(4)]),
            ):
                S._add("pool", (lambda e, src=src, dst=dst: e.collective_compute(
                    "AllGather", ALU.bypass, replica_groups=[[0, 1, 2, 3], [4, 5, 6, 7]],
                    ins=[src.opt()], outs=[dst.opt()])), toks, [("g", nm)], cstream=("cc", 0))

        S.barrier()

        def norm_out(Pacc, ptok, ch, hh, tb):
            S.recip(tmpf[64:65, 2, :], Pacc[64:65, :], R=[ptok], W=["t2"])
            S.mm(P[6][0:64, :], cstf[64:65, 3:67], tmpf[64:65, 2, :], R=["t2", "cstf"], W=["P6"])
            S.copy(tmpf[0:64, 3, :], P[6][0:64, :], R=["P6"], W=["t3"], eng="act")
            if hh == 0:
                S.tt(oT[0:64, ch, tsl(tb)], Pacc[0:64, :], tmpf[0:64, 3, :], ALU.mult, R=[ptok, "t3"], W=[("o", ch, tb, 0)])
            else:
                ost = arena[0:64, OST:OST + 512]
                S.tt(ost, Pacc[0:64, :], tmpf[0:64, 3, :], ALU.mult, R=[ptok, "t3"], W=["ost"])
                S.mm(P[6][:, :], shift64, ost, R=["ost", "cstb"], W=["P6"])
                S.copy(oT[64:128, ch, tsl(tb)], P[6][64:128, :], R=["P6"], W=[("o", ch, tb, 1)])

        B0 = 28672
        kdT = arena[:, B0:B0 + 4096]
        vdT = arena[:, B0 + 4096:B0 + 8192]
        VT0 = B0 + 8192
        Vt = arena[:, VT0:VT0 + NTILE * 65].rearrange("p (i c) -> p i c", c=65)
        PT0 = VT0 + NTILE * 65 + 3
        OST = PT0 + 4 * 256
        CAND = OST + 512
        cd_i = [0]
        S.memset(Vt[:, :, 64:65], 1.0, W=["vt1"])
        ident = cstb[:, 640:704]
        for h in range(8):
            pp, hh = h // 2, h % 2
            pb = hh * 64
            if hh == 0 and mode == "L":
                S.dma("sp", kdT, ikd_d[pp * 128:(pp + 1) * 128, :], W=["kdT"])
                S.dma("sp", vdT, ivd_d[pp * 128:(pp + 1) * 128, :], W=["vdT"])
            if hh == 0 and mode == "F":
                hf, hr = pp // 2, (pp % 2) * 128
                for dstT, own_d, g_d, nm, dtok in ((kdT, okd_h[hf], gkd_h[hf], "kd", "kdT"), (vdT, ovd_h[hf], gvd_h[hf], "vd", "vdT")):
                    S.dma("sp", dstT[:, 1024:3072], own_d[hr:hr + 128, :],
                          R=[(("okd" if nm == "kd" else "ovd"), pp, tb) for tb in range(4)], W=[dtok])
                    for side, (lo, hi, c0_, cands, wc0) in enumerate(((0, 1024, 1024, (0, 1, 2), 0), (3072, 4096, 0, (1, 2, 3), 4))):
                        for ci, r in enumerate(cands):
                            cd_i[0] = (cd_i[0] + 1) % 4
                            cand = arena[:, CAND + cd_i[0] * 1024:CAND + cd_i[0] * 1024 + 1024]
                            ctok = ("cand", cd_i[0])
                            S.dma("sp", cand, g_d[r * 256 + hr:r * 256 + hr + 128, c0_:c0_ + 1024],
                                  R=[("g", nm + str(hf))], W=[ctok])
                            if ci == 0:
                                S.ts(dstT[:, lo:hi], cand, wsel[:, wc0 + r:wc0 + r + 1], None, ALU.mult,
                                     R=[ctok, "wsel"], W=[dtok])
                            else:
                                S.stt(dstT[:, lo:hi], cand, wsel[:, wc0 + r:wc0 + r + 1], dstT[:, lo:hi], ALU.mult, ALU.add,
                                      R=[ctok, "wsel", dtok], W=[dtok])
            S.dma("sp", mtb[:], mtab_d[h, :, :], W=["mtb"])
            for i0 in range(0, NTILE, 16):
                n = min(16, NTILE - i0)
                for s in range(n):
                    cfg, d, r, a, c0, c1, Q0 = TILES[i0 + s]
                    S.transpose(psb[:, s * 64:(s + 1) * 64], vdT[pb:pb + 64, bass.ds(a * d + r, 128, d)],
                                ident[pb:pb + 64, :], R=["vdT", "cstb"], W=["psb"])
                S.copy(Vt[:, i0:i0 + n, 0:64], psb[:, 0:n * 64].rearrange("p (a b) -> p a b", b=64),
                       R=["psb"], W=[("vt", i0)])
            first = [True] * 4
            last_idx = {}
            segs_all = []
            for idx, (cfg, d, r, a, c0, c1, Q0) in enumerate(TILES):
                n0 = (a - 64 + c0 - Q0) * d + r
                segs = []
                c = c0
                while c < c1:
                    n = n0 + (c - c0) * d
                    tb = n // 512
                    cend = c
                    while cend < c1 and (n0 + (cend - c0) * d) // 512 == tb:
                        cend += 1
                    segs.append((tb, c, cend, n - 512 * tb))
                    last_idx[tb] = (idx, len(segs) - 1)
                    c = cend
                segs_all.append(segs)
            SB = (0, 1, 6)

            def dil_S(idx):
                cfg, d, r, a, c0, c1, Q0 = TILES[idx]
                ncol = c1 - c0
                n0 = (a - 64 + c0 - Q0) * d + r
                bi = SB[idx % 3]
                S.mm(P[bi][:, 0:ncol], kdT[pb:pb + 64, bass.ds(a * d + r, 128, d)],
                     qdT[pb:pb + 64, pp, bass.ds(n0, ncol, d)], R=["kdT"] + [("qd", pp, t) for t in range(4)], W=["P%d" % bi])

            def dil_rest(idx):
                cfg, d, r, a, c0, c1, Q0 = TILES[idx]
                ncol = c1 - c0
                bi = SB[idx % 3]
                ti = idx % 3
                tf = tmpf[:, ti, 0:ncol]
                S.act(tf, P[bi][:, 0:ncol], AF.Exp, R=["P%d" % bi], W=["t%d" % ti], scale=0.125)
                pt = arena[:, PT0 + (idx % 4) * 256:PT0 + (idx % 4) * 256 + ncol]
                S.stt(pt, tf, vld[:, idx:idx + 1], mtb[:, cfg * 256 + c0:cfg * 256 + c1], ALU.mult, ALU.mult,
                      R=["t%d" % ti, "vld", "mtb"], W=[("pt", idx % 4)])
                for si, (tb, ca, cb, nloc) in enumerate(segs_all[idx]):
                    S.mm(P[2 + tb][0:65, bass.ds(nloc, cb - ca, d)], Vt[:, idx, 0:65], pt[:, ca - c0:cb - c0],
                         start=first[tb], stop=(last_idx[tb] == (idx, si)), R=[("pt", idx % 4), ("vt", (idx // 16) * 16), "vt1"],
                         W=["P%d" % (2 + tb)], skip_group_check=True)
                    first[tb] = False

            LA = 2
            for idx in range(min(LA, NTILE)):
                dil_S(idx)
            for idx in range(NTILE):
                if idx + LA < NTILE:
                    dil_S(idx + LA)
                dil_rest(idx)
            for tb in range(4):
                norm_out(P[2 + tb], "P%d" % (2 + tb), pp, hh, tb)

        S.barrier()

        QH = 0
        Qh = arena[:, QH:QH + 2048]
        WQ = 2048
        wq = arena[:, WQ:WQ + 1536].rearrange("p (k n) -> p k n", n=768)
        wqsw = arena[:, WQ + 1536:WQ + 3072].rearrange("p (k n) -> p k n", n=768)
        wkv = arena[:, WQ + 3072:WQ + 4096]
        PTC = WQ + 4096
        CK = 28672
        ckv = arena[:, CK:CK + 8192]
        Kh = arena[:, CK + 8192:CK + 16384]
        Vh = arena[:, CK + 16384:CK + 16384 + 64 * 65].rearrange("p (i c) -> p i c", c=65)
        SQC = CK + 16384 + 64 * 65
        OST = SQC + 512
        if mode == "L":
            for q4 in range(4):
                S.dma("sp", ckv[:, q4 * 2048:(q4 + 1) * 2048], ilat_d[0:128, q4 * 2048:(q4 + 1) * 2048], W=[("ckv", q4)])
            S.dma("sp", Kh[64:96, :], ilat_d[128:160, :], W=["Khr"])
        else:
            for q4 in range(4):
                S.dma("sp", ckv[:, q4 * 2048:(q4 + 1) * 2048], glat_d[q4 * 160:q4 * 160 + 128, :], R=[("g", "lat")], W=[("ckv", q4)])
                S.dma("sp", Kh[64:96, q4 * 2048:(q4 + 1) * 2048], glat_d[q4 * 160 + 128:q4 * 160 + 160, :], R=[("g", "lat")], W=["Khr"])
        S.dma("pool", wq, LW(wq_d, l).rearrange("(k p) n -> p k n", p=128), W=["wq"])
        S.dma("pool", wqsw, LW(wqsw_d, l).rearrange("(k p) n -> p k n", p=128), W=["wqsw"])
        S.dma("pool", wkv, LW(wkv_d, l)[:, :], W=["wkv"])
        S.memset(Vh[:, :, 64:65], 1.0, W=["vh1"])
        sqc = arena[:, SQC:SQC + 512]
        scale_mla = float(96.0 ** -0.5)
        for h in range(8):
            pp, hh = h // 2, h % 2
            KNB, NBB = (4, 0), (5, 1)
            sqcs = (sqc, arena[:, OST + 512:OST + 1024])

            def kg_mm(kb):
                b = kb % 2
                S.mm(P[KNB[b]][0:64, :], wkv[:, h * 128:h * 128 + 64], ckv[:, kb * 512:(kb + 1) * 512],
                     R=["wkv", ("ckv", kb // 4)], W=["P%d" % KNB[b]])

            def kg_rest(kb):
                b = kb % 2
                ks = slice(kb * 512, (kb + 1) * 512)
                kf, rs = tmpf[0:64, 2 * b, :], tmpf[0:64, 2 * b + 1, :]
                tkf, trs = "t%d" % (2 * b), "t%d" % (2 * b + 1)
                S.copy(kf, P[KNB[b]][0:64, :], R=["P%d" % KNB[b]], W=[tkf])
                S.tt(sqcs[b][0:64, :], kf, kf, ALU.mult, R=[tkf], W=[("sqc", b)], eng="pool")
                S.mm(P[NBB[b]][0:64, :], blk64[0:64, 0:64], sqcs[b][0:64, :], R=[("sqc", b), "cstb"], W=["P%d" % NBB[b]])
                rstd_from(P[NBB[b]][0:64, :], rs, 0, 64, ["P%d" % NBB[b]], [trs])
                S.stt(Kh[0:64, ks], kf, vecs[0:64, 23:24], rs, ALU.mult, ALU.mult, R=[tkf, trs, "vecs"], W=[("Kh", kb)])

            kg_mm(0)
            for kb in range(16):
                if kb + 1 < 16:
                    kg_mm(kb + 1)
                kg_rest(kb)
            for g in range(8):
                vb = 4 + (g % 2)
                for s in range(8):
                    kb2 = g * 8 + s
                    S.mm(P[vb][:, s * 64:(s + 1) * 64], ckv[:, kb2 * 128:(kb2 + 1) * 128], wkv[:, h * 128 + 64:h * 128 + 128],
                         R=["wkv", ("ckv", kb2 // 16)], W=["P%d" % vb])
                S.copy(Vh[:, g * 8:(g + 1) * 8, 0:64], P[vb][:, :].rearrange("p (a b) -> p a b", b=64), R=["P%d" % vb], W=[("Vh", g)])
            for tb in range(4):
                for kk in range(2):
                    S.mm(P[4][0:96, :], wq[:, kk, h * 96:(h + 1) * 96], cqnT[:, kk, tsl(tb)], start=(kk == 0), stop=(kk == 1),
                         R=["wq", ("cq", kk, tb)], W=["P4"])
                for kk in range(2):
                    S.mm(P[5][0:96, :], wqsw[:, kk, h * 96:(h + 1) * 96], cqnT[:, kk, tsl(tb)], start=(kk == 0), stop=(kk == 1),
                         R=["wqsw", ("cq", kk, tb)], W=["P5"])
                S.act(sqc[0:96, :], P[4][0:96, :], AF.Square, R=["P4"], W=[("sqc", 0)])
                S.mm(P[6][0:96, :], blkq96, sqc[0:96, :], R=[("sqc", 0), "cstb"], W=["P6"])
                rstd_from(P[6][0:96, :], tmpf[0:96, 0, :], 0, 96, ["P6"], ["t0"])
                S.stt(Qh[0:64, tsl(tb)], P[4][0:64, :], vecs[0:64, 21:22], tmpf[0:64, 0, :], ALU.mult, ALU.mult,
                      R=["P4", "t0", "vecs"], W=[("Qh", tb)])
                S.stt(tmpf[64:96, 1, :], P[4][64:96, :], vecs[64:96, 21:22], tmpf[64:96, 0, :], ALU.mult, ALU.mult,
                      R=["P4", "t0", "vecs"], W=["t1"])
                S.stt(tmpf[64:96, 2, :], P[5][64:96, :], vecs[64:96, 22:23], tmpf[64:96, 0, :], ALU.mult, ALU.mult,
                      R=["P5", "t0", "vecs"], W=["t2"])
                S.tt(tmpf[64:96, 1, :], tmpf[64:96, 1, :], cosT[64:96, tsl(tb)], ALU.mult, R=["t1", ("rope", tb)], W=["t1"])
                S.tt(tmpf[64:96, 2, :], tmpf[64:96, 2, :], sinT[64:96, tsl(tb)], ALU.mult, R=["t2", ("rope", tb)], W=["t2"])
                S.tt(Qh[64:96, tsl(tb)], tmpf[64:96, 1, :], tmpf[64:96, 2, :], ALU.add, R=["t1", "t2"], W=[("Qhr", tb)])
            its = [(qb, kb2) for qb in range(4) for kb2 in range(64)]

            def mla_S(n):
                qb, kb2 = its[n]
                si = n % 2
                S.mm(P[si][:, :], Kh[0:96, kb2 * 128:(kb2 + 1) * 128], Qh[0:96, tsl(qb)],
                     R=[("Kh", kb2 // 4), "Khr", ("Qh", qb), ("Qhr", qb)], W=["P%d" % si])

            mla_S(0)
            for n, (qb, kb2) in enumerate(its):
                if n + 1 < len(its):
                    mla_S(n + 1)
                si = n % 2
                Po = P[2 + (qb % 2)]
                potok = "P%d" % (2 + (qb % 2))
                pt = arena[:, PTC + (n % 4) * 512:PTC + (n % 4) * 512 + 512]
                S.act(pt, P[si][:, :], AF.Exp, R=["P%d" % si], W=[("ptc", n % 4)], scale=scale_mla)
                S.mm(Po[0:65, :], Vh[:, kb2, 0:65], pt, start=(kb2 == 0), stop=(kb2 == 63),
                     R=[("ptc", n % 4), ("Vh", kb2 // 8), "vh1"], W=[potok])
                if kb2 == 63:
                    norm_out(Po, potok, 4 + pp, hh, qb)

        S.barrier()

        WO = 28672
        wo = arena[:, WO:WO + 8192].rearrange("p (k n) -> p k n", n=1024)
        wov = LW(wout_d, l).rearrange("(k p) n -> p k n", p=128)
        S.dma("pool", wo[:, :, 0:512], wov[:, :, 0:512], W=["wo"])
        S.dma("pool", wo[:, :, 512:1024], wov[:, :, 512:1024], W=["wo"])
        for tb in range(4):
            for dc in range(8):
                bi = dc % 4
                for ch in range(8):
                    S.mm(P[bi][:, :], wo[:, ch, dc * 128:(dc + 1) * 128], oT[:, ch, tsl(tb)], start=(ch == 0), stop=(ch == 7),
                         R=["wo", ("o", ch, tb, 0), ("o", ch, tb, 1)], W=["P%d" % bi])
                S.stt(xT[:, dc, tsl(tb)], P[bi][:, :], modT[:, 16 + dc:17 + dc], xT[:, dc, tsl(tb)], ALU.mult, ALU.add,
                      R=["P%d" % bi, "modT", ("x", dc, tb)], W=[("x", dc, tb)])

        S.barrier()

        hT = arena[:, 0:16384].rearrange("p (k t) -> p k t", t=T)
        sq = arena[:, 40960:40960 + 4096].rearrange("p (k t) -> p k t", t=512)
        w1v = LW(w1_d, l).rearrange("(k p) n -> p k n", p=128)
        w2v = LW(w2_d, l).rearrange("(c p) n -> p c n", p=128)

        def wgrp(g):
            o1 = 16384 + (g % 2) * 8192
            W1g = arena[:, o1:o1 + 4096].rearrange("p (k n) -> p k n", n=512)
            W2g = arena[:, o1 + 4096:o1 + 8192].rearrange("p (c n) -> p c n", n=1024)
            return W1g, W2g

        def load_w(g):
            W1g, W2g = wgrp(g)
            S.dma("pool", W1g, w1v[:, :, g * 512:(g + 1) * 512], W=[("w1", g % 2)])
            S.dma("pool", W2g, w2v[:, g * 4:(g + 1) * 4, :], W=[("w2", g % 2)])

        load_w(0)
        load_w(1)
        for tb in range(4):
            rmsnorm_block(tb, 8, 24, lambda c: hT[:, c, tsl(tb)], sq, "sq")
        mits = [(g, tb) for g in range(8) for tb in range(4)]

        def aT_of(tb):
            return arena[:, 32768 + (tb % 2) * 2048:32768 + (tb % 2) * 2048 + 2048].rearrange("p (c t) -> p c t", t=512)

        def mlp_u(n):
            g, tb = mits[n]
            W1g, W2g = wgrp(g)
            aT = aT_of(tb)
            for c in range(4):
                bi = c % 3
                for k in range(8):
                    S.mm(P[bi][:, :], W1g[:, k, c * 128:(c + 1) * 128], hT[:, k, tsl(tb)], start=(k == 0), stop=(k == 7),
                         R=[("w1", g % 2), ("h", k, tb)], W=["P%d" % bi])
                S.act(tmpf[:, c, :], P[bi][:, :], AF.Relu, R=["P%d" % bi], W=["t%d" % c])
                S.tt(aT[:, c, :], tmpf[:, c, :], tmpf[:, c, :], ALU.mult, R=["t%d" % c], W=[("a", tb % 2, c)], eng="pool")

        def mlp_y(n):
            g, tb = mits[n]
            W1g, W2g = wgrp(g)
            aT = aT_of(tb)
            for dc in range(8):
                bi = 3 + (dc % 4)
                for c in range(4):
                    S.mm(P[bi][:, :], W2g[:, c, dc * 128:(dc + 1) * 128], aT[:, c, :], start=(c == 0), stop=(c == 3),
                         R=[("w2", g % 2), ("a", tb % 2, c)], W=["P%d" % bi])
                S.stt(xT[:, dc, tsl(tb)], P[bi][:, :], modT[:, 40 + dc:41 + dc], xT[:, dc, tsl(tb)], ALU.mult, ALU.add,
                      R=["P%d" % bi, "modT", ("x", dc, tb)], W=[("x", dc, tb)])

        mlp_u(0)
        for n, (g, tb) in enumerate(mits):
            if n + 1 < len(mits):
                mlp_u(n + 1)
            mlp_y(n)
            if tb == 3 and g + 2 < 8:
                load_w(g + 2)
        S.barrier()

    yv = yT_d.rearrange("(c p) t -> p c t", p=128)
    for tb in range(4):
        S.dma("sp", yv[:, :, tsl(tb)], xT[:, :, tsl(tb)], R=[("x", c, tb) for c in range(8)], W=[("y", tb)])
    S.finish([("y", tb) for tb in range(4)])
    S.emit(st)
    st.close()
    return nc


_NC = {}


def _get_nc(mode):
    if mode not in _NC:
        _NC[mode] = build(mode)
    return _NC[mode]


def _consts():
    cb = np.zeros((128, NCB), np.float32)
    cb[:, 0:128] = 1.0 / 1024
    cb[0:64, 128:192] = 1.0 / 64
    cb[64:128, 192:256] = 1.0 / 64
    cb[:, 256:384] = 1.0 / 256
    cb[:, 384:512] = 1.0 / 128
    cb[0:64, 512:576] = 1.0 / 64
    cb[64:96, 576:608] = 1.0 / 32
    cb[0:32, 608:640] = 1.0 / 32
    cb[0:64, 640:704] = np.eye(64)
    cb[64:128, 640:704] = np.eye(64)
    for i in range(64):
        cb[i, 704 + 64 + i] = 1.0
    cf = np.zeros((128, NCF), np.float32)
    p = np.arange(128)
    cf[:, 0] = (10000.0 ** (-((p % 32) % 16).astype(np.float64) / 16.0)).astype(np.float32)
    cf[:, 1] = np.where((p % 32) < 16, -1.0, 1.0)
    cf[:, 2] = 1e-6
    cf[:, 3:67] = 1.0
    return cb, cf


def _mtab():
    slopes = np.array([2.0 ** (-8.0 * (h + 1) / 8) for h in range(8)], np.float64)
    k = np.arange(128)[:, None]
    c = np.arange(256)[None, :]
    dist = np.abs(c - 64 - k).astype(np.float64)
    m = np.zeros((8, 128, 3, 256), np.float64)
    for h in range(8):
        for cfg, d in enumerate((1, 4, 16)):
            m[h, :, cfg, :] = np.where(dist <= 64, np.exp(-slopes[h] * d * dist), 0.0)
    return m.reshape(8, 128, 768).astype(np.float32)


def _valid(t0):
    v = np.zeros((128, NTILE), np.float32)
    p = np.arange(128)
    for idx, (cfg, d, r, a, c0, c1, Q0) in enumerate(TILES):
        tok = (a + p) * d + r + t0 - 1024
        v[:, idx] = ((tok >= 0) & (tok < SEQ)).astype(np.float32)
    return v


def _layer_common(l, w_mod, b_mod, g_norm_mix, w_in, g_q_dil, g_k_dil, g_cq, g_ckv, g_q_nope, g_q_rope,
                  g_k_nope, g_k_rope, g_norm_mlp):
    vecs = np.zeros((128, 32), np.float32)
    vecs[:, 0:8] = g_norm_mix[l].reshape(8, 128).T
    vecs[:, 8:16] = g_norm_mlp[l].reshape(8, 128).T
    vecs[:, 16] = np.tile(g_q_dil[l], 2)
    vecs[:, 17] = np.tile(g_k_dil[l], 2)
    vecs[:, 18:20] = g_cq[l].reshape(2, 128).T
    vecs[:, 20] = g_ckv[l]
    vecs[0:64, 21] = g_q_nope[l]
    vecs[64:96, 21] = g_q_rope[l]
    vecs[0:64, 22] = g_q_nope[l]
    vecs[64:96, 22] = np.roll(g_q_rope[l], -16)
    vecs[0:64, 23] = g_k_nope[l]
    vecs[0:32, 24] = g_k_rope[l]
    vecs[0:32, 25] = np.roll(g_k_rope[l], -16)
    perm = (np.arange(32) + 16) % 32
    d = dict(
        w_mod=np.ascontiguousarray(w_mod[l]),
        bmodT=np.ascontiguousarray(b_mod[l].reshape(48, 128).T),
        w_in=np.ascontiguousarray(w_in[l]),
        w_in_sw=np.ascontiguousarray(w_in[l][:, 1920 + perm]),
        vecs=vecs,
    )
    return d


def kernel(x, c, positions, w_mod, b_mod, g_norm_mix, w_in, g_q_dil, g_k_dil, g_cq, w_q_up,
           g_ckv, w_kv_up, g_q_nope, g_q_rope, g_k_nope, g_k_rope, w_out, g_norm_mlp,
           w_mlp_in, w_mlp_out):
    A = lambda a: np.asarray(a)
    x, c, positions = A(x), A(c), A(positions)
    (w_mod, b_mod, g_norm_mix, w_in, g_q_dil, g_k_dil, g_cq, w_q_up, g_ckv, w_kv_up, g_q_nope, g_q_rope,
     g_k_nope, g_k_rope, w_out, g_norm_mlp, w_mlp_in, w_mlp_out) = [A(a).astype(np.float32, copy=False) for a in (
        w_mod, b_mod, g_norm_mix, w_in, g_q_dil, g_k_dil, g_cq, w_q_up, g_ckv, w_kv_up, g_q_nope, g_q_rope,
        g_k_nope, g_k_rope, w_out, g_norm_mlp, w_mlp_in, w_mlp_out)]
    cb, cf = _consts()
    mtab = _mtab()
    cores = list(range(8))
    coms = [_layer_common(l, w_mod, b_mod, g_norm_mix, w_in, g_q_dil, g_k_dil, g_cq, g_ckv, g_q_nope, g_q_rope,
                          g_k_nope, g_k_rope, g_norm_mlp) for l in range(2)]
    permq = np.arange(768).reshape(8, 96)
    permq = np.concatenate([permq[:, :64], permq[:, 64 + (np.arange(32) + 16) % 32]], axis=1).reshape(-1)
    shared = dict(
        w_mod=np.ascontiguousarray(w_mod), w_in=np.ascontiguousarray(w_in),
        bmodT=np.stack([cm["bmodT"] for cm in coms]), w_in_sw=np.stack([cm["w_in_sw"] for cm in coms]),
        vecs=np.stack([cm["vecs"] for cm in coms]),
        w_q=np.ascontiguousarray(w_q_up), w_q_sw=np.ascontiguousarray(w_q_up[:, :, permq]),
        w_kv=np.ascontiguousarray(w_kv_up), w_out=np.ascontiguousarray(w_out),
        w1=np.ascontiguousarray(w_mlp_in), w2=np.ascontiguousarray(w_mlp_out),
        mtab=mtab, cstb=cb, cstf=cf)
    ins = []
    for i in cores:
        b, g = i // 4, i % 4
        t0 = g * T
        wsel = np.zeros((128, 8), np.float32)
        if g - 1 >= 0:
            wsel[:, g - 1] = 1.0
        if g + 1 <= 3:
            wsel[:, 4 + g + 1] = 1.0
        ins.append(dict(
            xT=np.ascontiguousarray(x[b, t0:t0 + T, :].T),
            cT=np.ascontiguousarray(c[b].reshape(8, 128).T.astype(np.float32)),
            pos=np.ascontiguousarray(np.broadcast_to(positions[b, t0:t0 + T].astype(np.int32)[None, :], (96, T))),
            valid=_valid(t0), wsel=wsel, **shared))
    res = run_bass_kernel_spmd(_get_nc("F"), ins, core_ids=cores).results
    out = np.empty((2, SEQ, 1024), np.float32)
    for i in cores:
        out[i // 4, (i % 4) * T:(i % 4 + 1) * T, :] = res[i]["yT"].T
    return out


def kernel_unfused(x, c, positions, w_mod, b_mod, g_norm_mix, w_in, g_q_dil, g_k_dil, g_cq, w_q_up,
           g_ckv, w_kv_up, g_q_nope, g_q_rope, g_k_nope, g_k_rope, w_out, g_norm_mlp,
           w_mlp_in, w_mlp_out, _nlayers=2):
    A = lambda a: np.asarray(a)
    x, c, positions = A(x), A(c), A(positions)
    (w_mod, b_mod, g_norm_mix, w_in, g_q_dil, g_k_dil, g_cq, w_q_up, g_ckv, w_kv_up, g_q_nope, g_q_rope,
     g_k_nope, g_k_rope, w_out, g_norm_mlp, w_mlp_in, w_mlp_out) = [A(a).astype(np.float32, copy=False) for a in (
        w_mod, b_mod, g_norm_mix, w_in, g_q_dil, g_k_dil, g_cq, w_q_up, g_ckv, w_kv_up, g_q_nope, g_q_rope,
        g_k_nope, g_k_rope, w_out, g_norm_mlp, w_mlp_in, w_mlp_out)]
    cb, cf = _consts()
    mtab = _mtab()
    cores = list(range(8))
    xT = [np.ascontiguousarray(x[i // 4, (i % 4) * T:(i % 4 + 1) * T, :].T) for i in cores]
    base = []
    for i in cores:
        b, t0 = i // 4, (i % 4) * T
        base.append(dict(
            cT=np.ascontiguousarray(c[b].reshape(8, 128).T.astype(np.float32)),
            pos=np.ascontiguousarray(np.broadcast_to(positions[b, t0:t0 + T].astype(np.int32)[None, :], (96, T))),
            cstb=cb, cstf=cf))
    permq = np.arange(768).reshape(8, 96)
    permq = np.concatenate([permq[:, :64], permq[:, 64 + (np.arange(32) + 16) % 32]], axis=1).reshape(-1)
    for l in range(_nlayers):
        com = _layer_common(l, w_mod, b_mod, g_norm_mix, w_in, g_q_dil, g_k_dil, g_cq, g_ckv, g_q_nope, g_q_rope,
                            g_k_nope, g_k_rope, g_norm_mlp)
        in_a = [dict(xT=xT[i], **base[i], **com) for i in cores]
        ra = run_bass_kernel_spmd(_get_nc("A"), in_a, core_ids=cores).results
        lay = dict(
            w_q=np.ascontiguousarray(w_q_up[l]), w_q_sw=np.ascontiguousarray(w_q_up[l][:, permq]),
            w_kv=np.ascontiguousarray(w_kv_up[l]), w_out=np.ascontiguousarray(w_out[l]),
            w1=np.ascontiguousarray(w_mlp_in[l]), w2=np.ascontiguousarray(w_mlp_out[l]), mtab=mtab)
        in_l = []
        for i in cores:
            b, t0 = i // 4, (i % 4) * T
            grp = [ra[4 * b + j] for j in range(4)]
            lat = np.concatenate([g["o_lat"] for g in grp], axis=1)
            kd = np.concatenate([g["o_kd"] for g in grp], axis=1)
            vd = np.concatenate([g["o_vd"] for g in grp], axis=1)
            zpad = np.zeros((512, 1024), kd.dtype)
            kdp = np.concatenate([zpad, kd, zpad], axis=1)[:, t0:t0 + 4096]
            vdp = np.concatenate([zpad, vd, zpad], axis=1)[:, t0:t0 + 4096]
            in_l.append(dict(xT=xT[i], **base[i], **com, **lay, i_lat=np.ascontiguousarray(lat),
                             i_kd=np.ascontiguousarray(kdp), i_vd=np.ascontiguousarray(vdp), valid=_valid(t0)))
        rl = run_bass_kernel_spmd(_get_nc("L"), in_l, core_ids=cores).results
        xT = [np.ascontiguousarray(rl[i]["yT"]) for i in cores]
    out = np.empty((2, SEQ, 1024), np.float32)
    for i in cores:
        out[i // 4, (i % 4) * T:(i % 4 + 1) * T, :] = xT[i].T
    return out
```

```python
import numpy as np
import ml_dtypes
import concourse.bass as bass
import concourse.mybir as mybir
from concourse.bass_utils import run_bass_kernel_spmd
from contextlib import ExitStack

F32 = mybir.dt.float32
BF16 = mybir.dt.bfloat16
I32 = mybir.dt.int32
AF = mybir.ActivationFunctionType
ALU = mybir.AluOpType

NSLOT = 8
SAME_WIN = 3
T = 2048
SEQ = 8192
NCB = 832
NCF = 67
ARENA = 54272
PI = float(np.pi)


class _Op:
    __slots__ = ("eng", "emit", "stream", "idx", "waits", "ms", "clock", "pos")


class Sched:
    def __init__(self, nc):
        self.nc = nc
        self.eng_ops = {e: [] for e in ("pe", "act", "dve", "pool", "sp")}
        self.stream_ops = {}
        self.know = {e: {} for e in self.eng_ops}
        self.last_w = {}
        self.readers = {}
        self.dma_cnt = {"sp": 0, "pool": 0, "act": 0}

    def _add(self, eng, emit, R, W, dma=False, cstream=None):
        op = _Op()
        op.eng = eng
        op.emit = emit
        op.ms = False
        if dma:
            slot = self.dma_cnt[eng] % NSLOT
            self.dma_cnt[eng] += 1
            op.stream = ("dma", eng, slot)
        elif cstream is not None:
            op.stream = cstream
        else:
            op.stream = eng
        sl = self.stream_ops.setdefault(op.stream, [])
        op.idx = len(sl) + 1
        deps = []
        if dma and sl:
            deps.append(sl[-1])
        sl.append(op)
        for t in R:
            w = self.last_w.get(t)
            if w is not None:
                deps.append(w)
        for t in W:
            w = self.last_w.get(t)
            if w is not None:
                deps.append(w)
            deps.extend(self.readers.get(t, ()))
        eo = self.eng_ops[eng]
        op.pos = len(eo)
        K = self.know[eng]
        best = {}
        for d in deps:
            if d is op:
                continue
            b = best.get(d.stream)
            if b is None or b.idx < d.idx:
                best[d.stream] = d
        waits = []
        for s, d in best.items():
            if K.get(s, 0) >= d.idx:
                continue
            if s == eng:
                if eng == "pe" or eng == "sp" or (op.pos - d.pos) > SAME_WIN:
                    continue
                d.ms = True
                waits.append(d)
                K[s] = d.idx
                continue
            d.ms = True
            waits.append(d)
            for s2, i2 in d.clock.items():
                if K.get(s2, 0) < i2:
                    K[s2] = i2
        op.waits = waits
        op.clock = dict(K)
        op.clock[op.stream] = op.idx
        if dma or cstream is not None:
            op.ms = True
        eo.append(op)
        for t in R:
            self.readers.setdefault(t, []).append(op)
        for t in W:
            self.last_w[t] = op
            self.readers[t] = []
        return op

    def mm(self, out, lhsT, rhs, start=True, stop=True, R=(), W=(), **kw):
        return self._add("pe", lambda e: e.matmul(out, lhsT, rhs, start=start, stop=stop, **kw), R, W)

    def transpose(self, out, in_, ident, R=(), W=()):
        return self._add("pe", lambda e: e.transpose(out, in_, ident), R, W)

    def act(self, out, in_, func, R=(), W=(), **kw):
        return self._add("act", lambda e: e.activation(out=out, in_=in_, func=func, **kw), R, W)

    def tt(self, out, in0, in1, op, R=(), W=(), eng="dve"):
        return self._add(eng, lambda e: e.tensor_tensor(out, in0, in1, op), R, W)

    def ts(self, out, in0, s1, s2, op0, op1=None, R=(), W=(), eng="dve"):
        if op1 is None:
            return self._add(eng, lambda e: e.tensor_scalar(out, in0, s1, None, op0), R, W)
        return self._add(eng, lambda e: e.tensor_scalar(out, in0, s1, s2, op0, op1), R, W)

    def stt(self, out, in0, scalar, in1, op0, op1, R=(), W=(), eng="dve"):
        return self._add(eng, lambda e: e.scalar_tensor_tensor(out, in0, scalar, in1, op0, op1), R, W)

    def copy(self, out, in_, R=(), W=(), eng="dve"):
        if eng == "act":
            return self._add("act", lambda e: e.copy(out, in_), R, W)
        return self._add(eng, lambda e: e.tensor_copy(out, in_), R, W)

    def recip(self, out, in_, R=(), W=()):
        return self._add("dve", lambda e: e.reciprocal(out, in_), R, W)

    def memset(self, ap, val, W=(), eng="pool"):
        return self._add(eng, lambda e: e.memset(ap, val), (), W)

    def dma(self, q, out, in_, R=(), W=()):
        return self._add(q, lambda e: e.dma_start(out=out, in_=in_), R, W, dma=True)

    def barrier(self):
        lasts = [ops[-1] for ops in self.stream_ops.values() if ops]
        for e in ("pe", "act", "dve", "pool", "sp"):
            op = _Op()
            op.eng = e
            op.emit = None
            op.ms = False
            op.stream = None
            op.idx = 0
            eo = self.eng_ops[e]
            op.pos = len(eo)
            K = self.know[e]
            waits = []
            for d in lasts:
                if d.stream == e:
                    if e != "pe" and K.get(e, 0) < d.idx:
                        d.ms = True
                        waits.append(d)
                        K[e] = d.idx
                    continue
                if K.get(d.stream, 0) >= d.idx:
                    continue
                d.ms = True
                waits.append(d)
                for s2, i2 in d.clock.items():
                    if K.get(s2, 0) < i2:
                        K[s2] = i2
            op.waits = waits
            op.clock = dict(K)
            eo.append(op)

    def finish(self, R):
        return self._add("sp", None, R, ())

    def emit(self, stack):
        nc = self.nc
        sems = {}
        order = sorted(self.stream_ops, key=lambda s: 0 if (isinstance(s, tuple) and s[0] == "cc") else 1)
        for s in order:
            nm = "s_" + ("_".join(str(x) for x in s) if isinstance(s, tuple) else s)
            sems[s] = stack.enter_context(nc.semaphore(nm))
        val = {}
        for s, ops in self.stream_ops.items():
            c = 0
            inc = 16 if (isinstance(s, tuple) and s[0] == "dma") else 1
            for o in ops:
                if o.ms:
                    c += inc
                val[id(o)] = c
        block = stack.enter_context(nc.Block())

        def run(eng_name, e):
            for o in self.eng_ops[eng_name]:
                for d in o.waits:
                    e.wait_ge(sems[d.stream], val[id(d)])
                if o.emit is None:
                    continue
                ins = o.emit(e)
                if o.ms and ins is not None:
                    ins.then_inc(sems[o.stream], 16 if (isinstance(o.stream, tuple) and o.stream[0] == "dma") else 1)

        @block.tensor
        def _(e):
            run("pe", e)

        @block.scalar
        def _(e):
            run("act", e)

        @block.vector
        def _(e):
            run("dve", e)

        @block.gpsimd
        def _(e):
            run("pool", e)

        @block.sync
        def _(e):
            run("sp", e)


def dil_tiles():
    tiles = []
    for cfg, d in enumerate((1, 4, 16)):
        Q0 = 1024 // d
        nq = 2048 // d
        nt = nq // 128 + 1
        for r in range(d):
            for i in range(nt):
                a = Q0 - 64 + 128 * i
                c0 = 128 if i == 0 else 0
                c1 = 128 if i == nt - 1 else 256
                tiles.append((cfg, d, r, a, c0, c1, Q0))
    return tiles


TILES = dil_tiles()
NTILE = len(TILES)


def build(mode):
    nc = bass.Bass("TRN2", target_bir_lowering=False)
    D = lambda n, sh, dt, kind="ExternalInput": nc.dram_tensor(n, sh, dt, kind=kind).ap()
    xT_d = D("xT", [1024, T], F32)
    cT_d = D("cT", [128, 8], F32)
    pos_d = D("pos", [96, T], I32)
    NL = 2 if mode == "F" else 1
    LD = (lambda n, sh, dt: D(n, [2] + sh, dt)) if mode == "F" else (lambda n, sh, dt: D(n, sh, dt))
    wmod_d = LD("w_mod", [1024, 6144], F32)
    bmodT_d = LD("bmodT", [128, 48], F32)
    win_d = LD("w_in", [1024, 1952], F32)
    winsw_d = LD("w_in_sw", [1024, 32], F32)
    vecs_d = LD("vecs", [128, 32], F32)
    cstb_d = D("cstb", [128, NCB], F32)
    cstf_d = D("cstf", [128, NCF], F32)
    LW = (lambda ap, l: ap[l]) if mode == "F" else (lambda ap, l: ap)
    if mode == "A":
        okd_d = D("o_kd", [512, T], BF16, "ExternalOutput")
        ovd_d = D("o_vd", [512, T], BF16, "ExternalOutput")
        olat_d = D("o_lat", [160, T], BF16, "ExternalOutput")
    else:
        wq_d = LD("w_q", [256, 768], F32)
        wqsw_d = LD("w_q_sw", [256, 768], F32)
        wkv_d = LD("w_kv", [128, 1024], F32)
        wout_d = LD("w_out", [1024, 1024], F32)
        w1_d = LD("w1", [1024, 4096], F32)
        w2_d = LD("w2", [4096, 1024], F32)
        mtab_d = D("mtab", [8, 128, 768], F32)
        valid_d = D("valid", [128, NTILE], F32)
        yT_d = D("yT", [1024, T], F32, "ExternalOutput")
    if mode == "L":
        ikd_d = D("i_kd", [512, 4096], BF16)
        ivd_d = D("i_vd", [512, 4096], BF16)
        ilat_d = D("i_lat", [160, SEQ], BF16)
    if mode == "F":
        wsel_d = D("wsel", [128, 8], F32)
        okd_h = [nc.dram_tensor("b_kd%d" % i, [256, T], BF16).ap() for i in range(2)]
        ovd_h = [nc.dram_tensor("b_vd%d" % i, [256, T], BF16).ap() for i in range(2)]
        olat_d = nc.dram_tensor("b_lat", [160, T], BF16).ap()
        gkd_h = [nc.dram_tensor("g_kd%d" % i, [4 * 256, T], BF16).ap() for i in range(2)]
        gvd_h = [nc.dram_tensor("g_vd%d" % i, [4 * 256, T], BF16).ap() for i in range(2)]
        glat_d = nc.dram_tensor("g_lat", [4 * 160, T], BF16).ap()

    st = ExitStack()
    sb = lambda n, sh, dt: st.enter_context(nc.sbuf_tensor(n, sh, dt))
    xT = sb("xTs", [128, 8, T], F32)
    cosT = sb("cosT", [96, T], F32)
    sinT = sb("sinT", [96, T], F32)
    tmpf = sb("tmpf", [128, 4, 512], F32)
    tmpi = sb("tmpi", [96, 512], I32)
    posi = sb("posi", [96, T], I32) if False else None
    cstb = sb("cstbs", [128, NCB], BF16)
    cstf = sb("cstfs", [128, NCF], F32)
    vecs = sb("vecss", [128, 32], F32)
    cTs = sb("cTs", [128, 8], F32)
    cact = sb("cact", [128, 8], BF16)
    bmodT = sb("bmodTs", [128, 48], F32)
    modT = sb("modT", [128, 48], F32)
    gm = sb("gm", [128, 16], F32)
    modrow = sb("modrow", [1, 2, 512], F32)
    arena = sb("arena", [128, ARENA], BF16)
    P = [st.enter_context(nc.psum_tensor("ps%d" % i, [128, 512], F32)) for i in range(7)]
    psb = st.enter_context(nc.psum_tensor("psb", [128, 1024], BF16))
    if mode != "A":
        mtb = sb("mtb", [128, 768], F32)
        vld = sb("vld", [128, NTILE], F32)
    if mode == "F":
        wsel = sb("wsels", [128, 8], F32)

    S = Sched(nc)

    def ar(off, n, shape=None):
        a = arena[:, off:off + n]
        return a

    ones1024 = cstb[:, 0:128]
    blk64 = cstb[:, 128:256]
    ones256 = cstb[:, 256:384]
    ones128 = cstb[:, 384:512]
    blkq96 = cstb[0:96, 512:608]
    ones32 = cstb[0:32, 608:640]
    shift64 = cstb[0:64, 704:832]
    eps_c = lambda lo, hi: cstf[lo:hi, 2:3]

    def tsl(tb):
        return slice(tb * 512, (tb + 1) * 512)

    S.dma("pool", cstb[:], cstb_d[:, :], W=["cstb"])
    S.dma("sp", cstf[:], cstf_d[:, :], W=["cstf"])
    S.dma("sp", cTs[:], cT_d[:, :], W=["cT"])
    xv = xT_d.rearrange("(c p) t -> p c t", p=128)
    for tb in range(4):
        S.dma("sp", xT[:, :, tsl(tb)], xv[:, :, tsl(tb)], W=[("x", c, tb) for c in range(8)])
    if mode != "A":
        S.dma("sp", vld[:], valid_d[:, :], W=["vld"])
    if mode == "F":
        S.dma("sp", wsel[:], wsel_d[:, :], W=["wsel"])

    S.act(cact[:], cTs[:], AF.Silu, R=["cT"], W=["cact"])
    for tb in range(4):
        S.dma("sp", tmpi[:, :], pos_d[:, tsl(tb)], W=["tmpi"])
        t0 = tmpf[0:96, 0, :]
        t1 = tmpf[0:96, 1, :]
        t2 = tmpf[0:96, 2, :]
        t3 = tmpf[0:96, 3, :]
        S.copy(t0, tmpi[:, :], R=["tmpi"], W=["t0"])
        S.ts(t0, t0, cstf[0:96, 0:1], None, ALU.mult, R=["t0", "cstf"], W=["t0"])
        for tab, shift, use_sign in ((sinT, 0.0, True), (cosT, PI / 2, False)):
            S.ts(t1, t0, shift, None, ALU.add, R=["t0"], W=["t1"])
            S.ts(t2, t1, 1.0 / (2 * PI), None, ALU.mult, R=["t1"], W=["t2"])
            S.copy(tmpi[:, :], t2, R=["t2"], W=["tmpi"])
            S.copy(t2, tmpi[:, :], R=["tmpi"], W=["t2"])
            S.stt(t3, t2, -2 * PI, t1, ALU.mult, ALU.add, R=["t2", "t1"], W=["t3"])
            S.ts(t2, t3, PI, -2 * PI, ALU.is_gt, ALU.mult, R=["t3"], W=["t2"])
            S.tt(t3, t3, t2, ALU.add, R=["t3", "t2"], W=["t3"])
            S.ts(t2, t3, -PI, 2 * PI, ALU.is_lt, ALU.mult, R=["t3"], W=["t2"])
            S.tt(t3, t3, t2, ALU.add, R=["t3", "t2"], W=["t3"])
            S.ts(t3, t3, PI, -PI, ALU.min, ALU.max, R=["t3"], W=["t3"])
            if use_sign:
                S.act(tab[:, tsl(tb)], t3, AF.Sin, R=["t3", "cstf"], W=[("rope", tb)], scale=cstf[0:96, 1:2])
            else:
                S.act(tab[:, tsl(tb)], t3, AF.Sin, R=["t3"], W=[("rope", tb)])

    for l in range(NL):
        S.dma("sp", vecs[:], LW(vecs_d, l)[:, :], W=["vecs"])
        S.dma("sp", bmodT[:], LW(bmodT_d, l)[:, :], W=["bmodT"])
        wmv = LW(wmod_d, l).rearrange("(k p) n -> p k n", p=128)
        ngrp = 4 if mode == "A" else 12
        for g in range(ngrp):
            wm = arena[:, 45056 + (g % 2) * 4096:45056 + (g % 2) * 4096 + 4096].rearrange("p (k n) -> p k n", n=512)
            S.dma("pool", wm, wmv[:, :, g * 512:(g + 1) * 512], W=[("wm", g % 2)])
            for k in range(8):
                S.mm(P[0][0:1, :], cact[:, k:k + 1], wm[:, k, :], start=(k == 0), stop=(k == 7),
                     R=["cact", ("wm", g % 2)], W=["P0"])
            S.copy(modrow[0:1, g % 2, :], P[0][0:1, :], R=["P0"], W=[("mr", g % 2)])
            for jj in range(4):
                j = g * 4 + jj
                S.mm(P[1][:, j:j + 1], modrow[0:1, g % 2, jj * 128:(jj + 1) * 128], cstf[0:1, 3:4],
                     start=True, stop=True, R=[("mr", g % 2), "cstf"], W=["P1"], skip_group_check=True)
        S.tt(modT[:, 0:ngrp * 4], P[1][:, 0:ngrp * 4], bmodT[:, 0:ngrp * 4], ALU.add, R=["P1", "bmodT"], W=["modT"])
        S.stt(gm[:, 0:8], modT[:, 8:16], 1.0, vecs[:, 0:8], ALU.add, ALU.mult, R=["modT", "vecs"], W=["gm"])
        if mode != "A":
            S.stt(gm[:, 8:16], modT[:, 32:40], 1.0, vecs[:, 8:16], ALU.add, ALU.mult, R=["modT", "vecs"], W=["gm"])

        QD, CQ, OT = 0, 8192, 12288
        qdT = arena[:, QD:QD + 8192].rearrange("p (c t) -> p c t", t=T)
        cqnT = arena[:, CQ:CQ + 4096].rearrange("p (c t) -> p c t", t=T)
        oT = arena[:, OT:OT + 16384].rearrange("p (c t) -> p c t", t=T)

        def rstd_from(Pn_ap, out_ap, lo, hi, R, W):
            S.act(out_ap, Pn_ap, AF.Ln, R=R + ["cstf"], W=W, bias=eps_c(lo, hi), scale=1.0)
            S.act(out_ap, out_ap, AF.Exp, R=W, W=W, scale=-0.5)

        def rmsnorm_block(tb, gcol, shcol, hout, sq, sqtok):
            S.act(sq, xT[:, :, tsl(tb)], AF.Square, R=[("x", c, tb) for c in range(8)], W=[sqtok])
            for c in range(8):
                S.mm(P[6][:, :], ones1024, sq[:, c, :], start=(c == 0), stop=(c == 7), R=[sqtok, "cstb"], W=["P6"])
            rstd_from(P[6][:, :], tmpf[:, 0, :], 0, 128, ["P6"], ["t0"])
            for c in range(8):
                sl = 1 + (c % 3)
                S.stt(tmpf[:, sl, :], xT[:, c, tsl(tb)], gm[:, gcol + c:gcol + c + 1], tmpf[:, 0, :], ALU.mult, ALU.mult,
                      R=[("x", c, tb), "t0", "gm"], W=["t%d" % sl])
                S.act(hout(c), tmpf[:, sl, :], AF.Identity, R=["t%d" % sl, "modT"], W=[("h", c, tb)],
                      bias=modT[:, shcol + c:shcol + c + 1], scale=1.0)

        WIN = 12288
        winT = arena[:, WIN:WIN + 15616].rearrange("p (k n) -> p k n", n=1952)
        HB = WIN + 15616
        SQ = HB + 8192
        STG = SQ + 4096
        wsw = arena[:, STG + 4096:STG + 4096 + 256].rearrange("p (k n) -> p k n", n=32)
        wiv = LW(win_d, l).rearrange("(k p) n -> p k n", p=128)
        S.dma("pool", winT[:, :, 0:976], wiv[:, :, 0:976], W=["win"])
        S.dma("pool", winT[:, :, 976:1952], wiv[:, :, 976:1952], W=["win"])
        S.dma("pool", wsw, LW(winsw_d, l).rearrange("(k p) n -> p k n", p=128), W=["wsw"])
        pbank = [0]

        def nextP():
            pbank[0] = (pbank[0] + 1) % 4
            return pbank[0], P[pbank[0]]

        stg_i = [0]

        def stage():
            stg_i[0] = (stg_i[0] + 1) % 4
            i = stg_i[0]
            return ("stg", i), arena[:, STG + i * 512:STG + (i + 1) * 512]

        def proj(tb, hbuf, col0, M, wt=None, wtok="win"):
            bi, Pp = nextP()
            for k in range(8):
                lhsT = winT[:, k, col0:col0 + M] if wt is None else wt[:, k, 0:M]
                S.mm(Pp[0:M, :], lhsT, hbuf[:, k, :], start=(k == 0), stop=(k == 7),
                     R=[wtok] + [("h", k, tb)], W=["P%d" % bi])
            return "P%d" % bi, Pp

        for tb in range(4):
            hb_off = HB + (tb % 2) * 4096
            hbuf = arena[:, hb_off:hb_off + 4096].rearrange("p (k t) -> p k t", t=512)
            sq = arena[:, SQ:SQ + 4096].rearrange("p (k t) -> p k t", t=512)
            rmsnorm_block(tb, 0, 0, lambda c: hbuf[:, c, :], sq, "sq")
            sqh = arena[:, SQ:SQ + 512]

            def headnorm(ptok, Pp, onesm, gcol, out_ap, Wt, rows=128):
                S.act(sqh[0:rows, :], Pp[0:rows, :], AF.Square, R=[ptok], W=["sq"])
                S.mm(P[5][0:rows, :], onesm, sqh[0:rows, :], R=["sq", "cstb"], W=["P5"])
                rstd_from(P[5][0:rows, :], tmpf[0:rows, 0, :], 0, rows, ["P5"], ["t0"])
                S.stt(out_ap, Pp[0:rows, :], vecs[0:rows, gcol:gcol + 1], tmpf[0:rows, 0, :], ALU.mult, ALU.mult,
                      R=[ptok, "t0", "vecs"], W=Wt)

            if mode != "A":
                for j in range(4):
                    ptok, Pp = proj(tb, hbuf, j * 128, 128)
                    headnorm(ptok, Pp, blk64, 16, qdT[:, j, tsl(tb)], [("qd", j, tb)])
                pa = proj(tb, hbuf, 1536, 128)
                pb2 = proj(tb, hbuf, 1664, 128)
                for i, (ptok, Pp) in enumerate((pa, pb2)):
                    S.act(sq[:, i, :], Pp[:, :], AF.Square, R=[ptok], W=["sq"])
                for i in range(2):
                    S.mm(P[5][:, :], ones256, sq[:, i, :], start=(i == 0), stop=(i == 1), R=["sq", "cstb"], W=["P5"])
                rstd_from(P[5][:, :], tmpf[:, 0, :], 0, 128, ["P5"], ["t0"])
                for i, (ptok, Pp) in enumerate((pa, pb2)):
                    S.stt(cqnT[:, i, tsl(tb)], Pp[:, :], vecs[:, 18 + i:19 + i], tmpf[:, 0, :], ALU.mult, ALU.mult,
                          R=[ptok, "t0", "vecs"], W=[("cq", i, tb)])
            if mode != "L":
                for j in range(4):
                    ptok, Pp = proj(tb, hbuf, 512 + j * 128, 128)
                    stok, stg = stage()
                    headnorm(ptok, Pp, blk64, 17, stg, [stok])
                    S.dma("sp", (okd_d[j * 128:(j + 1) * 128, tsl(tb)] if mode == "A" else okd_h[j // 2][(j % 2) * 128:(j % 2 + 1) * 128, tsl(tb)]), stg, R=[stok], W=[("okd", j, tb)])
                for j in range(4):
                    ptok, Pp = proj(tb, hbuf, 1024 + j * 128, 128)
                    stok, stg = stage()
                    S.copy(stg, Pp[:, :], R=[ptok], W=[stok])
                    S.dma("sp", (ovd_d[j * 128:(j + 1) * 128, tsl(tb)] if mode == "A" else ovd_h[j // 2][(j % 2) * 128:(j % 2 + 1) * 128, tsl(tb)]), stg, R=[stok], W=[("ovd", j, tb)])
                ptok, Pp = proj(tb, hbuf, 1792, 128)
                stok, stg = stage()
                headnorm(ptok, Pp, ones128, 20, stg, [stok])
                S.dma("sp", olat_d[0:128, tsl(tb)], stg, R=[stok], W=[("olat", 0, tb)])
                ptok, Pp = proj(tb, hbuf, 1920, 32)
                ptok2, Pp2 = proj(tb, hbuf, 0, 32, wt=wsw, wtok="wsw")
                S.act(sqh[0:32, :], Pp[0:32, :], AF.Square, R=[ptok], W=["sq"])
                S.mm(P[5][0:32, :], ones32, sqh[0:32, :], R=["sq", "cstb"], W=["P5"])
                rstd_from(P[5][0:32, :], tmpf[0:32, 0, :], 0, 32, ["P5"], ["t0"])
                S.stt(tmpf[0:32, 1, :], Pp[0:32, :], vecs[0:32, 24:25], tmpf[0:32, 0, :], ALU.mult, ALU.mult,
                      R=[ptok, "t0", "vecs"], W=["t1"])
                S.stt(tmpf[0:32, 2, :], Pp2[0:32, :], vecs[0:32, 25:26], tmpf[0:32, 0, :], ALU.mult, ALU.mult,
                      R=[ptok2, "t0", "vecs"], W=["t2"])
                S.tt(tmpf[0:32, 1, :], tmpf[0:32, 1, :], cosT[0:32, tsl(tb)], ALU.mult, R=["t1", ("rope", tb)], W=["t1"])
                S.tt(tmpf[0:32, 2, :], tmpf[0:32, 2, :], sinT[0:32, tsl(tb)], ALU.mult, R=["t2", ("rope", tb)], W=["t2"])
                stok, stg = stage()
                S.tt(stg[0:32, :], tmpf[0:32, 1, :], tmpf[0:32, 2, :], ALU.add, R=["t1", "t2"], W=[stok])
                S.dma("sp", olat_d[128:160, tsl(tb)], stg[0:32, :], R=[stok], W=[("olat", 1, tb)])

        if mode == "A":
            outs = [("okd", j, tb) for j in range(4) for tb in range(4)] + [("ovd", j, tb) for j in range(4) for tb in range(4)] \
                + [("olat", i, tb) for i in range(2) for tb in range(4)]
            S.finish(outs)
            S.emit(st)
            st.close()
            return nc

        if mode == "F":
            for nm, src, dst, toks in (
                ("lat", olat_d, glat_d, [("olat", i, tb) for i in range(2) for tb in range(4)]),
                ("kd0", okd_h[0], gkd_h[0], [("okd", j, tb) for j in (0, 1) for tb in range(4)]),
                ("vd0", ovd_h[0], gvd_h[0], [("ovd", j, tb) for j in (0, 1) for tb in range(4)]),
                ("kd1", okd_h[1], gkd_h[1], [("okd", j, tb) for j in (2, 3) for tb in range(4)]),
                ("vd1", ovd_h[1], gvd_h[1], [("ovd", j, tb) for j in (2, 3) for tb in range(4)]),
            ):
                S._add("pool", (lambda e, src=src, dst=dst: e.collective_compute(
                    "AllGather", ALU.bypass, replica_groups=[[0, 1, 2, 3], [4, 5, 6, 7]],
                    ins=[src.opt()], outs=[dst.opt()])), toks, [("g", nm)], cstream=("cc", 0))

        S.barrier()

        def norm_out(Pacc, ptok, ch, hh, tb):
            S.recip(tmpf[64:65, 2, :], Pacc[64:65, :], R=[ptok], W=["t2"])
            S.mm(P[6][0:64, :], cstf[64:65, 3:67], tmpf[64:65, 2, :], R=["t2", "cstf"], W=["P6"])
            S.copy(tmpf[0:64, 3, :], P[6][0:64, :], R=["P6"], W=["t3"], eng="act")
            if hh == 0:
                S.tt(oT[0:64, ch, tsl(tb)], Pacc[0:64, :], tmpf[0:64, 3, :], ALU.mult, R=[ptok, "t3"], W=[("o", ch, tb, 0)])
            else:
                ost = arena[0:64, OST:OST + 512]
                S.tt(ost, Pacc[0:64, :], tmpf[0:64, 3, :], ALU.mult, R=[ptok, "t3"], W=["ost"])
                S.mm(P[6][:, :], shift64, ost, R=["ost", "cstb"], W=["P6"])
                S.copy(oT[64:128, ch, tsl(tb)], P[6][64:128, :], R=["P6"], W=[("o", ch, tb, 1)])

        B0 = 28672
        kdT = arena[:, B0:B0 + 4096]
        vdT = arena[:, B0 + 4096:B0 + 8192]
        VT0 = B0 + 8192
        Vt = arena[:, VT0:VT0 + NTILE * 65].rearrange("p (i c) -> p i c", c=65)
        PT0 = VT0 + NTILE * 65 + 3
        OST = PT0 + 4 * 256
        CAND = OST + 512
        cd_i = [0]
        S.memset(Vt[:, :, 64:65], 1.0, W=["vt1"])
        ident = cstb[:, 640:704]
        for h in range(8):
            pp, hh = h // 2, h % 2
            pb = hh * 64
            if hh == 0 and mode == "L":
                S.dma("sp", kdT, ikd_d[pp * 128:(pp + 1) * 128, :], W=["kdT"])
                S.dma("sp", vdT, ivd_d[pp * 128:(pp + 1) * 128, :], W=["vdT"])
            if hh == 0 and mode == "F":
                hf, hr = pp // 2, (pp % 2) * 128
                for dstT, own_d, g_d, nm, dtok in ((kdT, okd_h[hf], gkd_h[hf], "kd", "kdT"), (vdT, ovd_h[hf], gvd_h[hf], "vd", "vdT")):
                    S.dma("sp", dstT[:, 1024:3072], own_d[hr:hr + 128, :],
                          R=[(("okd" if nm == "kd" else "ovd"), pp, tb) for tb in range(4)], W=[dtok])
                    for side, (lo, hi, c0_, cands, wc0) in enumerate(((0, 1024, 1024, (0, 1, 2), 0), (3072, 4096, 0, (1, 2, 3), 4))):
                        for ci, r in enumerate(cands):
                            cd_i[0] = (cd_i[0] + 1) % 4
                            cand = arena[:, CAND + cd_i[0] * 1024:CAND + cd_i[0] * 1024 + 1024]
                            ctok = ("cand", cd_i[0])
                            S.dma("sp", cand, g_d[r * 256 + hr:r * 256 + hr + 128, c0_:c0_ + 1024],
                                  R=[("g", nm + str(hf))], W=[ctok])
                            if ci == 0:
                                S.ts(dstT[:, lo:hi], cand, wsel[:, wc0 + r:wc0 + r + 1], None, ALU.mult,
                                     R=[ctok, "wsel"], W=[dtok])
                            else:
                                S.stt(dstT[:, lo:hi], cand, wsel[:, wc0 + r:wc0 + r + 1], dstT[:, lo:hi], ALU.mult, ALU.add,
                                      R=[ctok, "wsel", dtok], W=[dtok])
            S.dma("sp", mtb[:], mtab_d[h, :, :], W=["mtb"])
            for i0 in range(0, NTILE, 16):
                n = min(16, NTILE - i0)
                for s in range(n):
                    cfg, d, r, a, c0, c1, Q0 = TILES[i0 + s]
                    S.transpose(psb[:, s * 64:(s + 1) * 64], vdT[pb:pb + 64, bass.ds(a * d + r, 128, d)],
                                ident[pb:pb + 64, :], R=["vdT", "cstb"], W=["psb"])
                S.copy(Vt[:, i0:i0 + n, 0:64], psb[:, 0:n * 64].rearrange("p (a b) -> p a b", b=64),
                       R=["psb"], W=[("vt", i0)])
            first = [True] * 4
            last_idx = {}
            segs_all = []
            for idx, (cfg, d, r, a, c0, c1, Q0) in enumerate(TILES):
                n0 = (a - 64 + c0 - Q0) * d + r
                segs = []
                c = c0
                while c < c1:
                    n = n0 + (c - c0) * d
                    tb = n // 512
                    cend = c
                    while cend < c1 and (n0 + (cend - c0) * d) // 512 == tb:
                        cend += 1
                    segs.append((tb, c, cend, n - 512 * tb))
                    last_idx[tb] = (idx, len(segs) - 1)
                    c = cend
                segs_all.append(segs)
            SB = (0, 1, 6)

            def dil_S(idx):
                cfg, d, r, a, c0, c1, Q0 = TILES[idx]
                ncol = c1 - c0
                n0 = (a - 64 + c0 - Q0) * d + r
                bi = SB[idx % 3]
                S.mm(P[bi][:, 0:ncol], kdT[pb:pb + 64, bass.ds(a * d + r, 128, d)],
                     qdT[pb:pb + 64, pp, bass.ds(n0, ncol, d)], R=["kdT"] + [("qd", pp, t) for t in range(4)], W=["P%d" % bi])

            def dil_rest(idx):
                cfg, d, r, a, c0, c1, Q0 = TILES[idx]
                ncol = c1 - c0
                bi = SB[idx % 3]
                ti = idx % 3
                tf = tmpf[:, ti, 0:ncol]
                S.act(tf, P[bi][:, 0:ncol], AF.Exp, R=["P%d" % bi], W=["t%d" % ti], scale=0.125)
                pt = arena[:, PT0 + (idx % 4) * 256:PT0 + (idx % 4) * 256 + ncol]
                S.stt(pt, tf, vld[:, idx:idx + 1], mtb[:, cfg * 256 + c0:cfg * 256 + c1], ALU.mult, ALU.mult,
                      R=["t%d" % ti, "vld", "mtb"], W=[("pt", idx % 4)])
                for si, (tb, ca, cb, nloc) in enumerate(segs_all[idx]):
                    S.mm(P[2 + tb][0:65, bass.ds(nloc, cb - ca, d)], Vt[:, idx, 0:65], pt[:, ca - c0:cb - c0],
                         start=first[tb], stop=(last_idx[tb] == (idx, si)), R=[("pt", idx % 4), ("vt", (idx // 16) * 16), "vt1"],
                         W=["P%d" % (2 + tb)], skip_group_check=True)
                    first[tb] = False

            LA = 2
            for idx in range(min(LA, NTILE)):
                dil_S(idx)
            for idx in range(NTILE):
                if idx + LA < NTILE:
                    dil_S(idx + LA)
                dil_rest(idx)
            for tb in range(4):
                norm_out(P[2 + tb], "P%d" % (2 + tb), pp, hh, tb)

        S.barrier()

        QH = 0
        Qh = arena[:, QH:QH + 2048]
        WQ = 2048
        wq = arena[:, WQ:WQ + 1536].rearrange("p (k n) -> p k n", n=768)
        wqsw = arena[:, WQ + 1536:WQ + 3072].rearrange("p (k n) -> p k n", n=768)
        wkv = arena[:, WQ + 3072:WQ + 4096]
        PTC = WQ + 4096
        CK = 28672
        ckv = arena[:, CK:CK + 8192]
        Kh = arena[:, CK + 8192:CK + 16384]
        Vh = arena[:, CK + 16384:CK + 16384 + 64 * 65].rearrange("p (i c) -> p i c", c=65)
        SQC = CK + 16384 + 64 * 65
        OST = SQC + 512
        if mode == "L":
            for q4 in range(4):
                S.dma("sp", ckv[:, q4 * 2048:(q4 + 1) * 2048], ilat_d[0:128, q4 * 2048:(q4 + 1) * 2048], W=[("ckv", q4)])
            S.dma("sp", Kh[64:96, :], ilat_d[128:160, :], W=["Khr"])
        else:
            for q4 in range(4):
                S.dma("sp", ckv[:, q4 * 2048:(q4 + 1) * 2048], glat_d[q4 * 160:q4 * 160 + 128, :], R=[("g", "lat")], W=[("ckv", q4)])
                S.dma("sp", Kh[64:96, q4 * 2048:(q4 + 1) * 2048], glat_d[q4 * 160 + 128:q4 * 160 + 160, :], R=[("g", "lat")], W=["Khr"])
        S.dma("pool", wq, LW(wq_d, l).rearrange("(k p) n -> p k n", p=128), W=["wq"])
        S.dma("pool", wqsw, LW(wqsw_d, l).rearrange("(k p) n -> p k n", p=128), W=["wqsw"])
        S.dma("pool", wkv, LW(wkv_d, l)[:, :], W=["wkv"])
        S.memset(Vh[:, :, 64:65], 1.0, W=["vh1"])
        sqc = arena[:, SQC:SQC + 512]
        scale_mla = float(96.0 ** -0.5)
        for h in range(8):
            pp, hh = h // 2, h % 2
            KNB, NBB = (4, 0), (5, 1)
            sqcs = (sqc, arena[:, OST + 512:OST + 1024])

            def kg_mm(kb):
                b = kb % 2
                S.mm(P[KNB[b]][0:64, :], wkv[:, h * 128:h * 128 + 64], ckv[:, kb * 512:(kb + 1) * 512],
                     R=["wkv", ("ckv", kb // 4)], W=["P%d" % KNB[b]])

            def kg_a(kb):
                b = kb % 2
                kf, tkf = tmpf[0:64, 2 * b, :], "t%d" % (2 * b)
                S.copy(kf, P[KNB[b]][0:64, :], R=["P%d" % KNB[b]], W=[tkf])
                S.tt(sqcs[b][0:64, :], kf, kf, ALU.mult, R=[tkf], W=[("sqc", b)], eng="pool")
                S.mm(P[NBB[b]][0:64, :], blk64[0:64, 0:64], sqcs[b][0:64, :], R=[("sqc", b), "cstb"], W=["P%d" % NBB[b]])

            def kg_b(kb):
                b = kb % 2
                ks = slice(kb * 512, (kb + 1) * 512)
                kf, rs = tmpf[0:64, 2 * b, :], tmpf[0:64, 2 * b + 1, :]
                tkf, trs = "t%d" % (2 * b), "t%d" % (2 * b + 1)
                rstd_from(P[NBB[b]][0:64, :], rs, 0, 64, ["P%d" % NBB[b]], [trs])
                S.stt(Kh[0:64, ks], kf, vecs[0:64, 23:24], rs, ALU.mult, ALU.mult, R=[tkf, trs, "vecs"], W=[("Kh", kb)])

            kg_mm(0)
            kg_a(0)
            for kb in range(16):
                if kb + 1 < 16:
                    kg_mm(kb + 1)
                    kg_a(kb + 1)
                kg_b(kb)
            for g in range(8):
                vb = 4 + (g % 2)
                for s in range(8):
                    kb2 = g * 8 + s
                    S.mm(P[vb][:, s * 64:(s + 1) * 64], ckv[:, kb2 * 128:(kb2 + 1) * 128], wkv[:, h * 128 + 64:h * 128 + 128],
                         R=["wkv", ("ckv", kb2 // 16)], W=["P%d" % vb])
                S.copy(Vh[:, g * 8:(g + 1) * 8, 0:64], P[vb][:, :].rearrange("p (a b) -> p a b", b=64), R=["P%d" % vb], W=[("Vh", g)])
            for tb in range(4):
                for kk in range(2):
                    S.mm(P[4][0:96, :], wq[:, kk, h * 96:(h + 1) * 96], cqnT[:, kk, tsl(tb)], start=(kk == 0), stop=(kk == 1),
                         R=["wq", ("cq", kk, tb)], W=["P4"])
                for kk in range(2):
                    S.mm(P[5][0:96, :], wqsw[:, kk, h * 96:(h + 1) * 96], cqnT[:, kk, tsl(tb)], start=(kk == 0), stop=(kk == 1),
                         R=["wqsw", ("cq", kk, tb)], W=["P5"])
                S.act(sqc[0:96, :], P[4][0:96, :], AF.Square, R=["P4"], W=[("sqc", 0)])
                S.mm(P[6][0:96, :], blkq96, sqc[0:96, :], R=[("sqc", 0), "cstb"], W=["P6"])
                rstd_from(P[6][0:96, :], tmpf[0:96, 0, :], 0, 96, ["P6"], ["t0"])
                S.stt(Qh[0:64, tsl(tb)], P[4][0:64, :], vecs[0:64, 21:22], tmpf[0:64, 0, :], ALU.mult, ALU.mult,
                      R=["P4", "t0", "vecs"], W=[("Qh", tb)])
                S.stt(tmpf[64:96, 1, :], P[4][64:96, :], vecs[64:96, 21:22], tmpf[64:96, 0, :], ALU.mult, ALU.mult,
                      R=["P4", "t0", "vecs"], W=["t1"])
                S.stt(tmpf[64:96, 2, :], P[5][64:96, :], vecs[64:96, 22:23], tmpf[64:96, 0, :], ALU.mult, ALU.mult,
                      R=["P5", "t0", "vecs"], W=["t2"])
                S.tt(tmpf[64:96, 1, :], tmpf[64:96, 1, :], cosT[64:96, tsl(tb)], ALU.mult, R=["t1", ("rope", tb)], W=["t1"])
                S.tt(tmpf[64:96, 2, :], tmpf[64:96, 2, :], sinT[64:96, tsl(tb)], ALU.mult, R=["t2", ("rope", tb)], W=["t2"])
                S.tt(Qh[64:96, tsl(tb)], tmpf[64:96, 1, :], tmpf[64:96, 2, :], ALU.add, R=["t1", "t2"], W=[("Qhr", tb)])
            its = [(qb, kb2) for qb in range(4) for kb2 in range(64)]

            def mla_S(n):
                qb, kb2 = its[n]
                si = n % 2
                S.mm(P[si][:, :], Kh[0:96, kb2 * 128:(kb2 + 1) * 128], Qh[0:96, tsl(qb)],
                     R=[("Kh", kb2 // 4), "Khr", ("Qh", qb), ("Qhr", qb)], W=["P%d" % si])

            mla_S(0)
            for n, (qb, kb2) in enumerate(its):
                if n + 1 < len(its):
                    mla_S(n + 1)
                si = n % 2
                Po = P[2 + (qb % 2)]
                potok = "P%d" % (2 + (qb % 2))
                pt = arena[:, PTC + (n % 4) * 512:PTC + (n % 4) * 512 + 512]
                S.act(pt, P[si][:, :], AF.Exp, R=["P%d" % si], W=[("ptc", n % 4)], scale=scale_mla)
                S.mm(Po[0:65, :], Vh[:, kb2, 0:65], pt, start=(kb2 == 0), stop=(kb2 == 63),
                     R=[("ptc", n % 4), ("Vh", kb2 // 8), "vh1"], W=[potok])
                if kb2 == 63:
                    norm_out(Po, potok, 4 + pp, hh, qb)

        S.barrier()

        WO = 28672
        wo = arena[:, WO:WO + 8192].rearrange("p (k n) -> p k n", n=1024)
        wov = LW(wout_d, l).rearrange("(k p) n -> p k n", p=128)
        S.dma("pool", wo[:, :, 0:512], wov[:, :, 0:512], W=["wo"])
        S.dma("pool", wo[:, :, 512:1024], wov[:, :, 512:1024], W=["wo"])
        for tb in range(4):
            for dc in range(8):
                bi = dc % 4
                for ch in range(8):
                    S.mm(P[bi][:, :], wo[:, ch, dc * 128:(dc + 1) * 128], oT[:, ch, tsl(tb)], start=(ch == 0), stop=(ch == 7),
                         R=["wo", ("o", ch, tb, 0), ("o", ch, tb, 1)], W=["P%d" % bi])
                S.stt(xT[:, dc, tsl(tb)], P[bi][:, :], modT[:, 16 + dc:17 + dc], xT[:, dc, tsl(tb)], ALU.mult, ALU.add,
                      R=["P%d" % bi, "modT", ("x", dc, tb)], W=[("x", dc, tb)])

        S.barrier()

        hT = arena[:, 0:16384].rearrange("p (k t) -> p k t", t=T)
        sq = arena[:, 40960:40960 + 4096].rearrange("p (k t) -> p k t", t=512)
        w1v = LW(w1_d, l).rearrange("(k p) n -> p k n", p=128)
        w2v = LW(w2_d, l).rearrange("(c p) n -> p c n", p=128)

        def wgrp(g):
            o1 = 16384 + (g % 2) * 8192
            W1g = arena[:, o1:o1 + 4096].rearrange("p (k n) -> p k n", n=512)
            W2g = arena[:, o1 + 4096:o1 + 8192].rearrange("p (c n) -> p c n", n=1024)
            return W1g, W2g

        def load_w(g):
            W1g, W2g = wgrp(g)
            S.dma("pool", W1g, w1v[:, :, g * 512:(g + 1) * 512], W=[("w1", g % 2)])
            S.dma("pool", W2g, w2v[:, g * 4:(g + 1) * 4, :], W=[("w2", g % 2)])

        load_w(0)
        load_w(1)
        for tb in range(4):
            rmsnorm_block(tb, 8, 24, lambda c: hT[:, c, tsl(tb)], sq, "sq")
        mits = [(g, tb) for g in range(8) for tb in range(4)]

        def aT_of(tb):
            return arena[:, 32768 + (tb % 2) * 2048:32768 + (tb % 2) * 2048 + 2048].rearrange("p (c t) -> p c t", t=512)

        def mlp_u(n):
            g, tb = mits[n]
            W1g, W2g = wgrp(g)
            aT = aT_of(tb)
            for c in range(4):
                bi = c % 3
                for k in range(8):
                    S.mm(P[bi][:, :], W1g[:, k, c * 128:(c + 1) * 128], hT[:, k, tsl(tb)], start=(k == 0), stop=(k == 7),
                         R=[("w1", g % 2), ("h", k, tb)], W=["P%d" % bi])
                S.act(tmpf[:, c, :], P[bi][:, :], AF.Relu, R=["P%d" % bi], W=["t%d" % c])
                S.tt(aT[:, c, :], tmpf[:, c, :], tmpf[:, c, :], ALU.mult, R=["t%d" % c], W=[("a", tb % 2, c)], eng="pool")

        def mlp_y(n):
            g, tb = mits[n]
            W1g, W2g = wgrp(g)
            aT = aT_of(tb)
            for dc in range(8):
                bi = 3 + (dc % 4)
                for c in range(4):
                    S.mm(P[bi][:, :], W2g[:, c, dc * 128:(dc + 1) * 128], aT[:, c, :], start=(c == 0), stop=(c == 3),
                         R=[("w2", g % 2), ("a", tb % 2, c)], W=["P%d" % bi])
                S.stt(xT[:, dc, tsl(tb)], P[bi][:, :], modT[:, 40 + dc:41 + dc], xT[:, dc, tsl(tb)], ALU.mult, ALU.add,
                      R=["P%d" % bi, "modT", ("x", dc, tb)], W=[("x", dc, tb)])

        mlp_u(0)
        for n, (g, tb) in enumerate(mits):
            if n + 1 < len(mits):
                mlp_u(n + 1)
            mlp_y(n)
            if tb == 3 and g + 2 < 8:
                load_w(g + 2)
        S.barrier()

    yv = yT_d.rearrange("(c p) t -> p c t", p=128)
    for tb in range(4):
        S.dma("sp", yv[:, :, tsl(tb)], xT[:, :, tsl(tb)], R=[("x", c, tb) for c in range(8)], W=[("y", tb)])
    S.finish([("y", tb) for tb in range(4)])
    S.emit(st)
    st.close()
    return nc


_NC = {}


def _get_nc(mode):
    if mode not in _NC:
        _NC[mode] = build(mode)
    return _NC[mode]


def _consts():
    cb = np.zeros((128, NCB), np.float32)
    cb[:, 0:128] = 1.0 / 1024
    cb[0:64, 128:192] = 1.0 / 64
    cb[64:128, 192:256] = 1.0 / 64
    cb[:, 256:384] = 1.0 / 256
    cb[:, 384:512] = 1.0 / 128
    cb[0:64, 512:576] = 1.0 / 64
    cb[64:96, 576:608] = 1.0 / 32
    cb[0:32, 608:640] = 1.0 / 32
    cb[0:64, 640:704] = np.eye(64)
    cb[64:128, 640:704] = np.eye(64)
    for i in range(64):
        cb[i, 704 + 64 + i] = 1.0
    cf = np.zeros((128, NCF), np.float32)
    p = np.arange(128)
    cf[:, 0] = (10000.0 ** (-((p % 32) % 16).astype(np.float64) / 16.0)).astype(np.float32)
    cf[:, 1] = np.where((p % 32) < 16, -1.0, 1.0)
    cf[:, 2] = 1e-6
    cf[:, 3:67] = 1.0
    return cb, cf


def _mtab():
    slopes = np.array([2.0 ** (-8.0 * (h + 1) / 8) for h in range(8)], np.float64)
    k = np.arange(128)[:, None]
    c = np.arange(256)[None, :]
    dist = np.abs(c - 64 - k).astype(np.float64)
    m = np.zeros((8, 128, 3, 256), np.float64)
    for h in range(8):
        for cfg, d in enumerate((1, 4, 16)):
            m[h, :, cfg, :] = np.where(dist <= 64, np.exp(-slopes[h] * d * dist), 0.0)
    return m.reshape(8, 128, 768).astype(np.float32)


def _valid(t0):
    v = np.zeros((128, NTILE), np.float32)
    p = np.arange(128)
    for idx, (cfg, d, r, a, c0, c1, Q0) in enumerate(TILES):
        tok = (a + p) * d + r + t0 - 1024
        v[:, idx] = ((tok >= 0) & (tok < SEQ)).astype(np.float32)
    return v


def _layer_common(l, w_mod, b_mod, g_norm_mix, w_in, g_q_dil, g_k_dil, g_cq, g_ckv, g_q_nope, g_q_rope,
                  g_k_nope, g_k_rope, g_norm_mlp):
    vecs = np.zeros((128, 32), np.float32)
    vecs[:, 0:8] = g_norm_mix[l].reshape(8, 128).T
    vecs[:, 8:16] = g_norm_mlp[l].reshape(8, 128).T
    vecs[:, 16] = np.tile(g_q_dil[l], 2)
    vecs[:, 17] = np.tile(g_k_dil[l], 2)
    vecs[:, 18:20] = g_cq[l].reshape(2, 128).T
    vecs[:, 20] = g_ckv[l]
    vecs[0:64, 21] = g_q_nope[l]
    vecs[64:96, 21] = g_q_rope[l]
    vecs[0:64, 22] = g_q_nope[l]
    vecs[64:96, 22] = np.roll(g_q_rope[l], -16)
    vecs[0:64, 23] = g_k_nope[l]
    vecs[0:32, 24] = g_k_rope[l]
    vecs[0:32, 25] = np.roll(g_k_rope[l], -16)
    perm = (np.arange(32) + 16) % 32
    d = dict(
        w_mod=np.ascontiguousarray(w_mod[l]),
        bmodT=np.ascontiguousarray(b_mod[l].reshape(48, 128).T),
        w_in=np.ascontiguousarray(w_in[l]),
        w_in_sw=np.ascontiguousarray(w_in[l][:, 1920 + perm]),
        vecs=vecs,
    )
    return d


def kernel(x, c, positions, w_mod, b_mod, g_norm_mix, w_in, g_q_dil, g_k_dil, g_cq, w_q_up,
           g_ckv, w_kv_up, g_q_nope, g_q_rope, g_k_nope, g_k_rope, w_out, g_norm_mlp,
           w_mlp_in, w_mlp_out):
    A = lambda a: np.asarray(a)
    x, c, positions = A(x), A(c), A(positions)
    (w_mod, b_mod, g_norm_mix, w_in, g_q_dil, g_k_dil, g_cq, w_q_up, g_ckv, w_kv_up, g_q_nope, g_q_rope,
     g_k_nope, g_k_rope, w_out, g_norm_mlp, w_mlp_in, w_mlp_out) = [A(a).astype(np.float32, copy=False) for a in (
        w_mod, b_mod, g_norm_mix, w_in, g_q_dil, g_k_dil, g_cq, w_q_up, g_ckv, w_kv_up, g_q_nope, g_q_rope,
        g_k_nope, g_k_rope, w_out, g_norm_mlp, w_mlp_in, w_mlp_out)]
    cb, cf = _consts()
    mtab = _mtab()
    cores = list(range(8))
    coms = [_layer_common(l, w_mod, b_mod, g_norm_mix, w_in, g_q_dil, g_k_dil, g_cq, g_ckv, g_q_nope, g_q_rope,
                          g_k_nope, g_k_rope, g_norm_mlp) for l in range(2)]
    permq = np.arange(768).reshape(8, 96)
    permq = np.concatenate([permq[:, :64], permq[:, 64 + (np.arange(32) + 16) % 32]], axis=1).reshape(-1)
    shared = dict(
        w_mod=np.ascontiguousarray(w_mod), w_in=np.ascontiguousarray(w_in),
        bmodT=np.stack([cm["bmodT"] for cm in coms]), w_in_sw=np.stack([cm["w_in_sw"] for cm in coms]),
        vecs=np.stack([cm["vecs"] for cm in coms]),
        w_q=np.ascontiguousarray(w_q_up), w_q_sw=np.ascontiguousarray(w_q_up[:, :, permq]),
        w_kv=np.ascontiguousarray(w_kv_up), w_out=np.ascontiguousarray(w_out),
        w1=np.ascontiguousarray(w_mlp_in), w2=np.ascontiguousarray(w_mlp_out),
        mtab=mtab, cstb=cb, cstf=cf)
    ins = []
    for i in cores:
        b, g = i // 4, i % 4
        t0 = g * T
        wsel = np.zeros((128, 8), np.float32)
        if g - 1 >= 0:
            wsel[:, g - 1] = 1.0
        if g + 1 <= 3:
            wsel[:, 4 + g + 1] = 1.0
        ins.append(dict(
            xT=np.ascontiguousarray(x[b, t0:t0 + T, :].T),
            cT=np.ascontiguousarray(c[b].reshape(8, 128).T.astype(np.float32)),
            pos=np.ascontiguousarray(np.broadcast_to(positions[b, t0:t0 + T].astype(np.int32)[None, :], (96, T))),
            valid=_valid(t0), wsel=wsel, **shared))
    res = run_bass_kernel_spmd(_get_nc("F"), ins, core_ids=cores).results
    out = np.empty((2, SEQ, 1024), np.float32)
    for i in cores:
        out[i // 4, (i % 4) * T:(i % 4 + 1) * T, :] = res[i]["yT"].T
    return out


def kernel_unfused(x, c, positions, w_mod, b_mod, g_norm_mix, w_in, g_q_dil, g_k_dil, g_cq, w_q_up,
           g_ckv, w_kv_up, g_q_nope, g_q_rope, g_k_nope, g_k_rope, w_out, g_norm_mlp,
           w_mlp_in, w_mlp_out, _nlayers=2):
    A = lambda a: np.asarray(a)
    x, c, positions = A(x), A(c), A(positions)
    (w_mod, b_mod, g_norm_mix, w_in, g_q_dil, g_k_dil, g_cq, w_q_up, g_ckv, w_kv_up, g_q_nope, g_q_rope,
     g_k_nope, g_k_rope, w_out, g_norm_mlp, w_mlp_in, w_mlp_out) = [A(a).astype(np.float32, copy=False) for a in (
        w_mod, b_mod, g_norm_mix, w_in, g_q_dil, g_k_dil, g_cq, w_q_up, g_ckv, w_kv_up, g_q_nope, g_q_rope,
        g_k_nope, g_k_rope, w_out, g_norm_mlp, w_mlp_in, w_mlp_out)]
    cb, cf = _consts()
    mtab = _mtab()
    cores = list(range(8))
    xT = [np.ascontiguousarray(x[i // 4, (i % 4) * T:(i % 4 + 1) * T, :].T) for i in cores]
    base = []
    for i in cores:
        b, t0 = i // 4, (i % 4) * T
        base.append(dict(
            cT=np.ascontiguousarray(c[b].reshape(8, 128).T.astype(np.float32)),
            pos=np.ascontiguousarray(np.broadcast_to(positions[b, t0:t0 + T].astype(np.int32)[None, :], (96, T))),
            cstb=cb, cstf=cf))
    permq = np.arange(768).reshape(8, 96)
    permq = np.concatenate([permq[:, :64], permq[:, 64 + (np.arange(32) + 16) % 32]], axis=1).reshape(-1)
    for l in range(_nlayers):
        com = _layer_common(l, w_mod, b_mod, g_norm_mix, w_in, g_q_dil, g_k_dil, g_cq, g_ckv, g_q_nope, g_q_rope,
                            g_k_nope, g_k_rope, g_norm_mlp)
        in_a = [dict(xT=xT[i], **base[i], **com) for i in cores]
        ra = run_bass_kernel_spmd(_get_nc("A"), in_a, core_ids=cores).results
        lay = dict(
            w_q=np.ascontiguousarray(w_q_up[l]), w_q_sw=np.ascontiguousarray(w_q_up[l][:, permq]),
            w_kv=np.ascontiguousarray(w_kv_up[l]), w_out=np.ascontiguousarray(w_out[l]),
            w1=np.ascontiguousarray(w_mlp_in[l]), w2=np.ascontiguousarray(w_mlp_out[l]), mtab=mtab)
        in_l = []
        for i in cores:
            b, t0 = i // 4, (i % 4) * T
            grp = [ra[4 * b + j] for j in range(4)]
            lat = np.concatenate([g["o_lat"] for g in grp], axis=1)
            kd = np.concatenate([g["o_kd"] for g in grp], axis=1)
            vd = np.concatenate([g["o_vd"] for g in grp], axis=1)
            zpad = np.zeros((512, 1024), kd.dtype)
            kdp = np.concatenate([zpad, kd, zpad], axis=1)[:, t0:t0 + 4096]
            vdp = np.concatenate([zpad, vd, zpad], axis=1)[:, t0:t0 + 4096]
            in_l.append(dict(xT=xT[i], **base[i], **com, **lay, i_lat=np.ascontiguousarray(lat),
                             i_kd=np.ascontiguousarray(kdp), i_vd=np.ascontiguousarray(vdp), valid=_valid(t0)))
        rl = run_bass_kernel_spmd(_get_nc("L"), in_l, core_ids=cores).results
        xT = [np.ascontiguousarray(rl[i]["yT"]) for i in cores]
    out = np.empty((2, SEQ, 1024), np.float32)
    for i in cores:
        out[i // 4, (i % 4) * T:(i % 4 + 1) * T, :] = xT[i].T
    return out
```
